# Optimizing a Trainium2 kernel written in Bass

```python
import math
import jax, jax.numpy as jnp
from jax import lax
import numpy as np

D_MODEL = 2048
BATCH = 16
SEQ = 2048
DEPTH = 4

D_MIX = D_MODEL
ATT_WIDTH = D_MIX // 2
SSM_WIDTH = D_MIX - ATT_WIDTH
ATT_HEAD_DIM = 128
ATT_HEADS = ATT_WIDTH // ATT_HEAD_DIM
ATT_BLOCK = 128
SSM_HEAD_DIM = 64
SSM_HEADS = SSM_WIDTH // SSM_HEAD_DIM
SSM_GROUPS = 2
SSM_HEADS_PER_GROUP = SSM_HEADS // SSM_GROUPS
SSM_STATE = 128
CONV_WIDTH = 4
CONV_DIM = SSM_WIDTH + 2 * SSM_GROUPS * SSM_STATE
SSD_CHUNK = 128
D_FF = -(-8 * D_MODEL // (3 * 256)) * 256
IN_DIM = 3 * ATT_WIDTH + SSM_WIDTH + CONV_DIM + SSM_HEADS
EPS = 1e-6

kernel_name = 'hymba_stickbreaking_ssd_swiglu'


def rmsnorm(x, g):
    xf = x.astype(jnp.float32)
    y = xf * lax.rsqrt(jnp.mean(xf * xf, axis=-1, keepdims=True) + EPS)
    return (y * g.astype(jnp.float32)).astype(x.dtype)


def stick_breaking_attention(q, k, v):
    bsz, s, h, dh = q.shape
    nb = s // ATT_BLOCK
    scale = dh ** -0.5
    qf = q.astype(jnp.float32).transpose(0, 2, 1, 3)
    kf = k.astype(jnp.float32).transpose(0, 2, 1, 3)
    vf = v.astype(jnp.float32).transpose(0, 2, 1, 3)
    q_blocks = qf.reshape(bsz, h, nb, ATT_BLOCK, dh).transpose(2, 0, 1, 3, 4)
    key_pos = jnp.arange(s)

    def one_block(args):
        qb, blk = args
        z = jnp.einsum('bhqd,bhkd->bhqk', qb, kf) * scale
        query_pos = blk * ATT_BLOCK + jnp.arange(ATT_BLOCK)
        mask = key_pos[None, :] < query_pos[:, None]
        log_beta = jax.nn.log_sigmoid(z)
        log_remain = jnp.where(mask, log_beta - z, 0.0)
        later = lax.cumsum(log_remain, axis=3, reverse=True) - log_remain
        weights = jnp.where(mask, jnp.exp(log_beta + later), 0.0)
        return jnp.einsum('bhqk,bhkd->bhqd', weights, vf)

    out = lax.map(one_block, (q_blocks, jnp.arange(nb)))
    out = out.transpose(1, 0, 3, 2, 4).reshape(bsz, s, h, dh)
    return out.astype(q.dtype)


def causal_depthwise_conv(u, w, bias):
    s = u.shape[1]
    up = jnp.pad(u, ((0, 0), (CONV_WIDTH - 1, 0), (0, 0)))
    out = bias
    for i in range(CONV_WIDTH):
        out = out + up[:, i:i + s] * w[i]
    return out


def ssd_chunked(x, dt, a, b_in, c_in, d_skip):
    bsz, s, h, p = x.shape
    nc = s // SSD_CHUNK
    g, hg, n = SSM_GROUPS, SSM_HEADS_PER_GROUP, SSM_STATE
    xf = x.astype(jnp.float32)
    dtf = dt.astype(jnp.float32)
    xdt = (xf * dtf[..., None]).reshape(bsz, nc, SSD_CHUNK, g, hg, p)
    bc = b_in.astype(jnp.float32).reshape(bsz, nc, SSD_CHUNK, g, n)
    cc = c_in.astype(jnp.float32).reshape(bsz, nc, SSD_CHUNK, g, n)
    da = (dtf * a.astype(jnp.float32)).reshape(bsz, nc, SSD_CHUNK, g, hg)
    a_cum = jnp.cumsum(da, axis=2)
    pos = jnp.arange(SSD_CHUNK)
    causal = (pos[:, None] >= pos[None, :])[:, :, None, None]
    seg = a_cum[:, :, :, None] - a_cum[:, :, None, :]
    decay = jnp.exp(jnp.where(causal, seg, -jnp.inf))
    cb = jnp.einsum('bclgn,bcsgn->bclsg', cc, bc)
    y_diag = jnp.einsum('bclsg,bclsgi,bcsgip->bclgip', cb, decay, xdt)
    decay_to_end = jnp.exp(a_cum[:, :, -1:] - a_cum)
    states = jnp.einsum('bcsgn,bcsgi,bcsgip->bcgipn', bc, decay_to_end, xdt)
    chunk_decay = jnp.exp(a_cum[:, :, -1])

    def step(h_prev, inp):
        st, dec = inp
        return h_prev * dec[..., None, None] + st, h_prev

    h0 = jnp.zeros((bsz, g, hg, p, n), jnp.float32)
    _, h_in = lax.scan(step, h0, (states.transpose(1, 0, 2, 3, 4, 5), chunk_decay.transpose(1, 0, 2, 3)))
    h_in = h_in.transpose(1, 0, 2, 3, 4, 5)
    y_off = jnp.einsum('bclgn,bcgipn,bclgi->bclgip', cc, h_in, jnp.exp(a_cum))
    y = (y_diag + y_off).reshape(bsz, s, h, p) + xf * d_skip.astype(jnp.float32)[:, None]
    return y


def hybrid_layer(x, norm_mix, w_in, q_gain, k_gain, conv_w, conv_b, dt_bias, a_log, d_skip,
                 attn_out_gain, ssm_out_gain, w_out, norm_ffn, w_gate, w_up, w_down):
    bsz, s, _ = x.shape
    h = rmsnorm(x, norm_mix)
    proj = h @ w_in
    splits = [ATT_WIDTH, 2 * ATT_WIDTH, 3 * ATT_WIDTH, 3 * ATT_WIDTH + SSM_WIDTH,
              3 * ATT_WIDTH + SSM_WIDTH + CONV_DIM]
    q, k, v, z, xbc, dt = jnp.split(proj, splits, axis=-1)

    q = rmsnorm(q.reshape(bsz, s, ATT_HEADS, ATT_HEAD_DIM), q_gain)
    k = rmsnorm(k.reshape(bsz, s, ATT_HEADS, ATT_HEAD_DIM), k_gain)
    v = v.reshape(bsz, s, ATT_HEADS, ATT_HEAD_DIM)
    o_att = stick_breaking_attention(q, k, v).reshape(bsz, s, ATT_WIDTH)
    o_att = rmsnorm(o_att, attn_out_gain)

    xbc = jax.nn.silu(causal_depthwise_conv(xbc, conv_w, conv_b))
    xs, bm, cm = jnp.split(xbc, [SSM_WIDTH, SSM_WIDTH + SSM_GROUPS * SSM_STATE], axis=-1)
    dt = jax.nn.softplus(dt.astype(jnp.float32) + dt_bias.astype(jnp.float32))
    a = -jnp.exp(a_log.astype(jnp.float32))
    y = ssd_chunked(xs.reshape(bsz, s, SSM_HEADS, SSM_HEAD_DIM), dt, a,
                    bm.reshape(bsz, s, SSM_GROUPS, SSM_STATE),
                    cm.reshape(bsz, s, SSM_GROUPS, SSM_STATE), d_skip)
    y = y.reshape(bsz, s, SSM_WIDTH) * jax.nn.silu(z.astype(jnp.float32))
    yg = y.reshape(bsz, s, SSM_GROUPS, SSM_WIDTH // SSM_GROUPS)
    yg = yg * lax.rsqrt(jnp.mean(yg * yg, axis=-1, keepdims=True) + EPS)
    o_ssm = (yg.reshape(bsz, s, SSM_WIDTH) * ssm_out_gain.astype(jnp.float32)).astype(x.dtype)

    x = x + jnp.concatenate([o_att, o_ssm], axis=-1) @ w_out

    h = rmsnorm(x, norm_ffn)
    x = x + (jax.nn.silu(h @ w_gate) * (h @ w_up)) @ w_down
    return x


def setup_inputs(seed: int = 0) -> dict:
    key = jax.random.key(seed)
    ks = jax.random.split(key, 18)
    f32 = jnp.float32

    def normal(k, shape, scale):
        return jax.random.normal(k, shape, f32) * scale

    x = normal(ks[0], (BATCH, SEQ, D_MODEL), 1.0)
    norm_mix = 1.0 + normal(ks[1], (DEPTH, D_MODEL), 0.02)
    w_in = normal(ks[2], (DEPTH, D_MODEL, IN_DIM), D_MODEL ** -0.5)
    q_gain = 1.0 + normal(ks[3], (DEPTH, ATT_HEAD_DIM), 0.02)
    k_gain = 1.0 + normal(ks[4], (DEPTH, ATT_HEAD_DIM), 0.02)
    conv_w = normal(ks[5], (DEPTH, CONV_WIDTH, CONV_DIM), CONV_WIDTH ** -0.5)
    conv_b = normal(ks[6], (DEPTH, CONV_DIM), 0.01)
    dt0 = jnp.exp(jax.random.uniform(ks[7], (DEPTH, SSM_HEADS), f32, math.log(1e-3), math.log(1e-1)))
    dt_bias = dt0 + jnp.log(-jnp.expm1(-dt0))
    a_log = jnp.log(jax.random.uniform(ks[8], (DEPTH, SSM_HEADS), f32, 1.0, 16.0))
    d_skip = 1.0 + normal(ks[9], (DEPTH, SSM_HEADS), 0.02)
    attn_out_gain = 1.0 + normal(ks[10], (DEPTH, ATT_WIDTH), 0.02)
    ssm_out_gain = 1.0 + normal(ks[11], (DEPTH, SSM_WIDTH), 0.02)
    w_out = normal(ks[12], (DEPTH, D_MIX, D_MODEL), D_MIX ** -0.5)
    norm_ffn = 1.0 + normal(ks[13], (DEPTH, D_MODEL), 0.02)
    w_gate = normal(ks[14], (DEPTH, D_MODEL, D_FF), D_MODEL ** -0.5)
    w_up = normal(ks[15], (DEPTH, D_MODEL, D_FF), D_MODEL ** -0.5)
    w_down = normal(ks[16], (DEPTH, D_FF, D_MODEL), D_FF ** -0.5)
    return {'x': x, 'norm_mix': norm_mix, 'w_in': w_in, 'q_gain': q_gain, 'k_gain': k_gain,
            'conv_w': conv_w, 'conv_b': conv_b, 'dt_bias': dt_bias, 'a_log': a_log, 'd_skip': d_skip,
            'attn_out_gain': attn_out_gain, 'ssm_out_gain': ssm_out_gain, 'w_out': w_out,
            'norm_ffn': norm_ffn, 'w_gate': w_gate, 'w_up': w_up, 'w_down': w_down}


def reference(x, norm_mix, w_in, q_gain, k_gain, conv_w, conv_b, dt_bias, a_log, d_skip,
              attn_out_gain, ssm_out_gain, w_out, norm_ffn, w_gate, w_up, w_down):
    for i in range(DEPTH):
        x = hybrid_layer(x, norm_mix[i], w_in[i], q_gain[i], k_gain[i], conv_w[i], conv_b[i],
                         dt_bias[i], a_log[i], d_skip[i], attn_out_gain[i], ssm_out_gain[i],
                         w_out[i], norm_ffn[i], w_gate[i], w_up[i], w_down[i])
    return x
```

```python
from contextlib import ExitStack
import math
import numpy as np
import concourse.bass as bass
import concourse.mybir as mybir
from concourse.bass_utils import run_bass_kernel_spmd

F32, BF16 = mybir.dt.float32, mybir.dt.bfloat16
AF = mybir.ActivationFunctionType
ALU = mybir.AluOpType

S = 2048
D = 2048
NH = 8
DH = 128
DFF = 5632
IN_DIM = 5648
EPS = 1e-6
NT = S // 128
NB = S // 512
NFC = DFF // 128
PP_NMIX, PP_NFFN, PP_QG, PP_KG, PP_CW, PP_CB, PP_DTB, PP_ALOG, PP_DSK, PP_AOG, PP_SOG, NP_ = \
    0, 16, 32, 33, 34, 82, 94, 110, 126, 142, 150, 1174
C_ID, C_ONE, C_US, C_TRI, C_LGE, C_AM, NCST = 0, 128, 256, 384, 512, 640, 1536

ENG = ["pe", "act", "dve", "pool", "sp"]


class Res:
    __slots__ = ("name", "lw", "rd")

    def __init__(self, name):
        self.name = name
        self.lw = None
        self.rd = []


class Op:
    __slots__ = ("eng", "fn", "kind", "key", "need", "val", "deps")


class Sched:
    def __init__(self, nc, es):
        self.nc = nc
        self.sem = {e: es.enter_context(nc.semaphore("s_" + e)) for e in ENG}
        self.cnt = {e: 0 for e in ENG}
        self.seen = {e: {} for e in ENG}
        self.dsem = {}
        self.dcnt = {}
        self.es = es
        self.ops = {e: [] for e in ENG}
        self.pending = {e: [] for e in ENG}
        self.allres = []
        self.dirty = {}
        self.n_inst = 0

    def res(self, name):
        r = Res(name)
        self.allres.append(r)
        return r

    def _mk(self, eng, fn, r, w, kind, key):
        o = Op()
        o.eng, o.fn, o.kind, o.key, o.need, o.val = eng, fn, kind, key, False, None
        deps = {}
        for d in self.pending[eng]:
            deps[d] = True
        self.pending[eng] = []
        for x in r:
            if x.lw is not None:
                deps[x.lw] = True
        for x in w:
            if x.lw is not None and x.lw not in deps:
                deps[x.lw] = False
            for q in x.rd:
                if q not in deps:
                    deps[q] = False
        keep = []
        for d, raw in deps.items():
            if d is o:
                continue
            if kind == "c" and d.kind == "c" and d.eng == eng and not raw:
                continue
            keep.append(d)
            if d.kind != "d":
                d.need = True
        o.deps = keep
        for x in r:
            if x not in w:
                x.rd = [q for q in x.rd if not (q.eng == eng and q.kind == kind and q.key == key)]
                x.rd.append(o)
        for x in w:
            x.lw = o
            x.rd = []
        self.ops[eng].append(o)
        return o

    def op(self, eng, fn, r=(), w=()):
        return self._mk(eng, fn, r, w, "c", None)

    def dma(self, eng, out, in_, r=(), w=(), key=None):
        assert key is not None
        if key not in self.dsem:
            self.dsem[key] = self.es.enter_context(self.nc.semaphore("d_" + key))
            self.dcnt[key] = 0
        o = self._mk(eng, lambda e: e.dma_start(out=out, in_=in_), r, w, "d", key)
        self.dcnt[key] += 16
        o.val = self.dcnt[key]
        self.dirty[key] = o
        return o

    def flush(self, final=False):
        bar = Op()
        bar.eng, bar.fn, bar.kind, bar.key, bar.need, bar.val = "sp", None, "b", None, True, None
        deps = list(self.pending["sp"])
        self.pending["sp"] = []
        for e in ENG:
            if self.ops[e]:
                lo = None
                for o in reversed(self.ops[e]):
                    if o.kind == "c":
                        lo = o
                        break
                if lo is not None:
                    lo.need = True
                    deps.append(lo)
        deps.extend(self.dirty.values())
        self.dirty = {}
        bar.deps = deps
        self.ops["sp"].append(bar)
        for e in ENG:
            if e != "sp":
                self.pending[e].append(bar)
        for r in self.allres:
            r.lw = None
            r.rd = []
        for e in ENG:
            for o in self.ops[e]:
                if o.kind != "d" and o.need:
                    self.cnt[e] += 1
                    o.val = self.cnt[e]
        nc = self.nc
        me = self

        def replay(e, h):
            seen = me.seen[e]
            for o in me.ops[e]:
                for d in o.deps:
                    if d.kind == "d":
                        sem = me.dsem[d.key]
                    else:
                        sem = me.sem[d.eng]
                    if seen.get(sem.name, 0) >= d.val:
                        continue
                    seen[sem.name] = d.val
                    h.wait_ge(sem, d.val)
                    me.n_inst += 1
                if o.kind == "b":
                    h.sem_inc(me.sem["sp"], 1)
                    continue
                ins = o.fn(h)
                me.n_inst += 1
                if o.kind == "d":
                    ins.then_inc(me.dsem[o.key], 16)
                elif o.need:
                    ins.then_inc(me.sem[e], 1)
            if final and e != "sp":
                for d in me.pending[e]:
                    h.wait_ge(me.sem["sp"], d.val)

        with nc.Block() as block:
            @block.tensor
            def _(h):
                replay("pe", h)

            @block.scalar
            def _(h):
                replay("act", h)

            @block.vector
            def _(h):
                replay("dve", h)

            @block.gpsimd
            def _(h):
                replay("pool", h)

            @block.sync
            def _(h):
                replay("sp", h)
        self.ops = {e: [] for e in ENG}


ALL_PHASES = ('inproj', 'attn', 'ssd', 'outproj', 'ffn')


def build(n_layers, n_seq, debug=False, phases=ALL_PHASES):
    nc = bass.Bass("TRN2", target_bir_lowering=False)

    def dram(name, shape, dt, kind="Internal"):
        return nc.dram_tensor(name, shape, dt, kind=kind).ap()

    sk = "ExternalOutput" if debug else "Internal"
    x_in = dram("x", [n_seq, S, D], F32, "ExternalInput")
    y = dram("y", [n_seq, S, D], F32, "ExternalOutput")
    w_in = dram("w_in", [n_layers, D, IN_DIM], F32, "ExternalInput")
    w_out = dram("w_out", [n_layers, D, D], F32, "ExternalInput")
    w_gate = dram("w_gate", [n_layers, D, DFF], F32, "ExternalInput")
    w_up = dram("w_up", [n_layers, D, DFF], F32, "ExternalInput")
    w_down = dram("w_down", [n_layers, DFF, D], F32, "ExternalInput")
    pp_d = dram("pp", [n_layers, 128, NP_], F32, "ExternalInput")
    cst_d = dram("cst", [128, NCST], F32, "ExternalInput")
    qT_s = dram("qT_s", [NH, 128, S], BF16, sk)
    kT_s = dram("kT_s", [NH, 128, S], BF16, sk)
    v_s = dram("v_s", [S, 1024], BF16, sk)
    sz_s = dram("sz_s", [S, 1024], F32, sk)
    dt_s = dram("dt_s", [S, 16], F32, sk)
    xs_s = dram("xs_s", [S, 1024], F32, sk)
    BT_s = dram("BT_s", [2, 128, S], BF16, sk)
    CT_s = dram("CT_s", [2, 128, S], BF16, sk)
    Bt_s = dram("Bt_s", [S, 256], BF16, sk)

    if debug:
        dbg_on = dram("dbg_on", [128, NH, S], BF16, sk)
        dbg_os = dram("dbg_os", [128, NH, S], BF16, sk)
    es = ExitStack()
    K = Sched(nc, es)

    uniq = [0]

    def sb(stack, name, shape, dt):
        uniq[0] += 1
        return stack.enter_context(nc.sbuf_tensor(f"sb{uniq[0]}_{name}", shape, dt))

    def ps(stack, name, shape, dt):
        uniq[0] += 1
        return stack.enter_context(nc.psum_tensor(f"ps{uniq[0]}_{name}", shape, dt))

    cst = sb(es, "cst", [128, NCST], F32)
    cbf = sb(es, "cbf", [128, 384], BF16)
    pp = sb(es, "pp", [128, NP_], F32)
    abc = sb(es, "abc", [128, 16], F32)
    epsb = sb(es, "epsb", [128, 2], F32)
    R_cst, R_cbf, R_pp, R_abc = K.res("cst"), K.res("cbf"), K.res("pp"), K.res("abc")
    ident = cst[:, C_ID:C_ID + 128]
    ones = cst[:, C_ONE:C_ONE + 128]
    ustr = cst[:, C_US:C_US + 128]
    tri = cst[:, C_TRI:C_TRI + 128]
    ident_bf = cbf[:, 0:128]
    nlge_bf = cbf[:, 128:256]
    nones_bf = cbf[:, 256:384]

    K.dma("sp", cst[:], cst_d[:, :], w=[R_cst], key="cst")
    K.op("act", lambda e: e.activation(out=cbf[:, 0:128], in_=cst[:, C_ID:C_ID + 128], func=AF.Copy),
         r=[R_cst], w=[R_cbf])
    K.op("act", lambda e: e.activation(out=cbf[:, 128:256], in_=cst[:, C_LGE:C_LGE + 128], func=AF.Copy, scale=-1.0),
         r=[R_cst], w=[R_cbf])
    K.op("act", lambda e: e.activation(out=cbf[:, 256:384], in_=cst[:, C_ONE:C_ONE + 128], func=AF.Copy, scale=-1.0),
         r=[R_cst], w=[R_cbf])
    K.op("dve", lambda e: e.memset(epsb[:, 0:1], EPS), w=[R_abc])
    K.op("dve", lambda e: e.memset(epsb[:, 1:2], math.log(DH ** -0.5)), w=[R_abc])
    K.flush()

    P = [ps(es, f"P{i}", [128, 512], F32) for i in range(8)]
    R_P = [K.res(f"P{i}") for i in range(8)]
    Pb = [p[:].bitcast(BF16) for p in P]

    def wview(flat, n):
        c = 8192 // n if n == 512 else 22
        return flat[:, 0:c * n].rearrange("p (c n) -> p c n", n=n)

    def emit_norm(stack, src, ntile, hT, R_hT, gcol, banks):
        xt = [sb(stack, f"n_xt{i}", [128, D], F32) for i in range(2)]
        R_xt = [K.res(f"n_xt{i}") for i in range(2)]
        xn = [sb(stack, f"n_xn{i}", [128, D], BF16) for i in range(2)]
        R_xn = [K.res(f"n_xn{i}") for i in range(2)]
        junk = sb(stack, "n_junk", [128, D], BF16)
        R_junk = K.res("n_junk")
        st = [sb(stack, f"n_st{i}", [128, 4], F32) for i in range(2)]
        R_st = [K.res(f"n_st{i}") for i in range(2)]
        for j in range(ntile):
            b = j % 2
            K.dma("sp", xt[b][:], src(j), w=[R_xt[b]], key=f"n_xt{b}")
            K.op("act", lambda e, b=b: e.activation(out=junk[:], in_=xt[b][:], func=AF.Square,
                                                     accum_out=st[b][:, 0:1]),
                 r=[R_xt[b]], w=[R_junk, R_st[b]])
            K.op("act", lambda e, b=b: e.activation(out=st[b][:, 1:2], in_=st[b][:, 0:1], func=AF.Ln,
                                                     scale=1.0 / D, bias=epsb[:, 0:1]),
                 r=[R_st[b]], w=[R_st[b]])
            K.op("act", lambda e, b=b: e.activation(out=st[b][:, 2:3], in_=st[b][:, 1:2], func=AF.Exp, scale=-0.5),
                 r=[R_st[b]], w=[R_st[b]])
            K.op("dve", lambda e, b=b: e.tensor_scalar(out=xn[b][:], in0=xt[b][:], scalar1=st[b][:, 2:3],
                                                       scalar2=None, op0=ALU.mult),
                 r=[R_xt[b], R_st[b]], w=[R_xn[b]])
            for c2 in range(2):
                pb = banks[(j * 2 + c2) % len(banks)]
                tpv = Pb[pb].rearrange("p (i t) -> p i t", t=128)

                def tr(e, b=b, c2=c2, tpv=tpv):
                    for i in range(8):
                        c = c2 * 8 + i
                        ins = e.transpose(tpv[:, i, :], xn[b][:, c * 128:(c + 1) * 128], ident_bf)
                    return ins
                K.op("pe", tr, r=[R_xn[b], R_cbf], w=[R_P[pb]])

                def ev(e, c2=c2, tpv=tpv, j=j):
                    for i in range(8):
                        c = c2 * 8 + i
                        ins = e.tensor_scalar(out=hT[:, c, j * 128:(j + 1) * 128], in0=tpv[:, i, :],
                                              scalar1=pp[:, gcol + c:gcol + c + 1], scalar2=None, op0=ALU.mult)
                    return ins
                K.op("dve" if c2 == 0 else "pool_no", ev, r=[R_P[pb], R_pp], w=[R_hT[j]]) if False else \
                    K.op("dve", ev, r=[R_P[pb], R_pp], w=[R_hT[j]])

    def phase_inproj(L, q, xsrc):
        with ExitStack() as st:
            hT = sb(st, "hT", [128, 16, S], BF16)
            R_hT = [K.res(f"hT{j}") for j in range(NT)]
            with ExitStack() as st2:
                emit_norm(st2, lambda j: xsrc[q, j * 128:(j + 1) * 128, :], NT, hT, R_hT, PP_NMIX, [6, 7])
                K.flush()
            wbf = [sb(st, f"wb{i}", [128, 8192], BF16) for i in range(3)]
            R_wb = [K.res(f"wb{i}") for i in range(3)]
            wdt = sb(st, "wdt", [128, 16, 16], BF16)
            R_wdt = K.res("wdt")
            w_l = w_in[L].rearrange("(c p) n -> p c n", p=128)
            vst = [sb(st, f"vst{i}", [128, 512], BF16) for i in range(2)]
            R_vst = [K.res(f"vst{i}") for i in range(2)]
            zst = [sb(st, f"zst{i}", [128, 512], F32) for i in range(2)]
            R_zst = [K.res(f"zst{i}") for i in range(2)]
            sqt = [sb(st, f"sqt{i}", [128, 512], BF16) for i in range(2)]
            R_sqt = [K.res(f"sqt{i}") for i in range(2)]
            rt = [sb(st, f"rt{i}", [128, 512], F32) for i in range(2)]
            R_rt = [K.res(f"rt{i}") for i in range(2)]
            qst = [sb(st, f"qst{i}", [128, 512], BF16) for i in range(2)]
            R_qst = [K.res(f"qst{i}") for i in range(2)]
            craw = [sb(st, f"craw{i}", [128, S + 3], F32) for i in range(2)]
            R_craw = [K.res(f"craw{i}") for i in range(2)]
            ct = [sb(st, f"ct{i}", [128, S], F32) for i in range(2)]
            R_ct = [K.res(f"ct{i}") for i in range(2)]
            xsT = [sb(st, f"xsT{i}", [128, S], F32) for i in range(1)]
            R_xsT = [K.res(f"xsT{i}") for i in range(1)]
            bcT = [sb(st, f"bcT{i}", [128, S], BF16) for i in range(2)]
            R_bcT = [K.res(f"bcT{i}") for i in range(2)]
            xst = sb(st, "xst", [128, 16, 128], F32)
            R_xst = K.res("xst")
            bst = sb(st, "bst", [128, 16, 128], BF16)
            R_bst = K.res("bst")
            dtw = sb(st, "dtw", [128, 3, 256], F32)
            R_dtw = K.res("dtw")
            for i in range(2):
                K.op("dve", lambda e, i=i: e.memset(craw[i][:, 0:3], 0.0), w=[R_craw[i]])

            K.dma("pool", wdt[:], w_l[:, :, 5632:5648], w=[R_wdt], key="wdt")
            pdt = P[4][:, 0:256].rearrange("p (t h) -> p t h", h=16)
            for tt in range(NT):
                def mm(e, tt=tt):
                    for c in range(16):
                        ins = e.matmul(pdt[:, tt, :], lhsT=hT[:, c, tt * 128:(tt + 1) * 128], rhs=wdt[:, c, :],
                                       start=(c == 0), stop=(c == 15))
                    return ins
                K.op("pe", mm, r=[R_hT[tt], R_wdt], w=[R_P[4]])
            dtv = [dtw[:, i, :].rearrange("p (t h) -> p t h", h=16) for i in range(3)]
            K.op("dve", lambda e: e.tensor_tensor(out=dtv[0], in0=pdt,
                                                  in1=pp[:, None, PP_DTB:PP_DTB + 16].to_broadcast([128, 16, 16]),
                                                  op=ALU.add), r=[R_P[4], R_pp], w=[R_dtw])
            K.op("act", lambda e: e.activation(out=dtw[:, 1, :], in_=dtw[:, 0, :], func=AF.Exp), r=[R_dtw], w=[R_dtw])
            K.op("act", lambda e: e.activation(out=dtw[:, 2, :], in_=dtw[:, 1, :], func=AF.Ln, bias=1.0),
                 r=[R_dtw], w=[R_dtw])
            K.dma("sp", dt_s.rearrange("(t p) h -> p t h", p=128), dtv[2], r=[R_dtw], key="dtw")

            tiles = [("v", 0), ("v", 1), ("z", 0), ("z", 1), ("q", 0), ("q", 1), ("k", 0), ("k", 1),
                     ("x", 0), ("x", 1), ("x", 2)]
            col0 = {"q": 0, "k": 1024, "v": 2048, "z": 3072, "x": 4096}
            state = {"u": 0, "deferred": None, "ev": 0}

            def run_deferred():
                if state["deferred"] is not None:
                    f = state["deferred"]
                    state["deferred"] = None
                    f()

            for ti, (kind, w) in enumerate(tiles):
                slot = ti % 3
                wv = wview(wbf[slot], 512)
                K.dma("pool", wv, w_l[:, :, col0[kind] + w * 512: col0[kind] + (w + 1) * 512],
                      w=[R_wb[slot]], key=f"wb{slot}")
                if kind in ("v", "z"):
                    for tt in range(NT):
                        u = state["u"]; state["u"] += 1
                        pa = u % 4

                        def mm(e, tt=tt, pa=pa, wv=wv):
                            for c in range(16):
                                ins = e.matmul(P[pa][:], lhsT=hT[:, c, tt * 128:(tt + 1) * 128], rhs=wv[:, c, :],
                                               start=(c == 0), stop=(c == 15))
                            return ins
                        K.op("pe", mm, r=[R_hT[tt], R_wb[slot]], w=[R_P[pa]])
                        run_deferred()
                        k2 = u % 2
                        if kind == "v":
                            K.op("act", lambda e, pa=pa, k2=k2: e.activation(out=vst[k2][:], in_=P[pa][:], func=AF.Copy),
                                 r=[R_P[pa]], w=[R_vst[k2]])
                            K.dma("sp", v_s[tt * 128:(tt + 1) * 128, w * 512:(w + 1) * 512], vst[k2][:],
                                  r=[R_vst[k2]], key=f"vst{k2}")
                        else:
                            K.op("act", lambda e, pa=pa, k2=k2: e.activation(out=zst[k2][:], in_=P[pa][:], func=AF.Silu),
                                 r=[R_P[pa]], w=[R_zst[k2]])
                            K.dma("sp", sz_s[tt * 128:(tt + 1) * 128, w * 512:(w + 1) * 512], zst[k2][:],
                                  r=[R_zst[k2]], key=f"zst{k2}")
                else:
                    for j in range(4):
                        cj = w * 4 + j
                        for tb in range(NB):
                            u = state["u"]; state["u"] += 1
                            pa = u % 4

                            def mm(e, j=j, tb=tb, pa=pa, wv=wv):
                                for c in range(16):
                                    ins = e.matmul(P[pa][:], lhsT=wv[:, c, j * 128:(j + 1) * 128],
                                                   rhs=hT[:, c, tb * 512:(tb + 1) * 512],
                                                   start=(c == 0), stop=(c == 15))
                                return ins
                            K.op("pe", mm, r=[R_hT[4 * tb + i] for i in range(4)] + [R_wb[slot]], w=[R_P[pa]])
                            run_deferred()
                            if kind in ("q", "k"):
                                k2 = state["ev"] % 2; state["ev"] += 1
                                K.op("act", lambda e, pa=pa, k2=k2: e.activation(out=sqt[k2][:], in_=P[pa][:],
                                                                                  func=AF.Square),
                                     r=[R_P[pa]], w=[R_sqt[k2]])

                                def rest(kind=kind, cj=cj, tb=tb, pa=pa, k2=k2):
                                    pn = 4 + k2
                                    K.op("pe", lambda e: e.matmul(P[pn][:], lhsT=nones_bf, rhs=sqt[k2][:],
                                                                  start=True, stop=True),
                                         r=[R_sqt[k2], R_cbf], w=[R_P[pn]])
                                    K.op("act", lambda e: e.activation(out=rt[k2][:], in_=P[pn][:], func=AF.Ln,
                                                                       scale=-1.0 / DH, bias=epsb[:, 0:1]),
                                         r=[R_P[pn]], w=[R_rt[k2]])
                                    if kind == "q":
                                        K.op("act", lambda e: e.activation(out=rt[k2][:], in_=rt[k2][:], func=AF.Exp,
                                                                           scale=-0.5, bias=epsb[:, 1:2]),
                                             r=[R_rt[k2]], w=[R_rt[k2]])
                                    else:
                                        K.op("act", lambda e: e.activation(out=rt[k2][:], in_=rt[k2][:], func=AF.Exp,
                                                                           scale=-0.5),
                                             r=[R_rt[k2]], w=[R_rt[k2]])
                                    gc = PP_QG if kind == "q" else PP_KG
                                    K.op("dve", lambda e: e.scalar_tensor_tensor(
                                        out=qst[k2][:], in0=P[pa][:], scalar=pp[:, gc:gc + 1], in1=rt[k2][:],
                                        op0=ALU.mult, op1=ALU.mult), r=[R_P[pa], R_rt[k2], R_pp], w=[R_qst[k2]])
                                    dst = (qT_s if kind == "q" else kT_s)[cj, :, tb * 512:(tb + 1) * 512]
                                    K.dma("sp", dst, qst[k2][:], r=[R_qst[k2]], key=f"qst{k2}")
                                state["deferred"] = rest
                            else:
                                cb = cj % 2
                                K.op("act", lambda e, pa=pa, cb=cb, tb=tb: e.activation(
                                    out=craw[cb][:, 3 + tb * 512: 3 + (tb + 1) * 512], in_=P[pa][:], func=AF.Copy),
                                    r=[R_P[pa]], w=[R_craw[cb]])
                                if tb == NB - 1:
                                    def conv(cj=cj, cb=cb):
                                        wc = PP_CW + cj * 4
                                        K.op("dve", lambda e: e.tensor_scalar(
                                            out=ct[cb][:], in0=craw[cb][:, 0:S], scalar1=pp[:, wc:wc + 1],
                                            scalar2=pp[:, PP_CB + cj:PP_CB + cj + 1], op0=ALU.mult, op1=ALU.add),
                                            r=[R_craw[cb], R_pp], w=[R_ct[cb]])
                                        for i in range(1, 4):
                                            K.op("dve", lambda e, i=i: e.scalar_tensor_tensor(
                                                out=ct[cb][:], in0=craw[cb][:, i:i + S], scalar=pp[:, wc + i:wc + i + 1],
                                                in1=ct[cb][:], op0=ALU.mult, op1=ALU.add),
                                                r=[R_craw[cb], R_pp, R_ct[cb]], w=[R_ct[cb]])
                                        if cj < 8:
                                            K.op("act", lambda e: e.activation(out=xsT[0][:], in_=ct[cb][:], func=AF.Silu),
                                                 r=[R_ct[cb]], w=[R_xsT[0]])
                                            for t4 in range(4):
                                                pb = 6 + (t4 % 2)
                                                tv = P[pb][:].rearrange("p (i t) -> p i t", t=128)

                                                def tr(e, t4=t4, tv=tv):
                                                    for i in range(4):
                                                        tt = t4 * 4 + i
                                                        ins = e.transpose(tv[:, i, :], xsT[0][:, tt * 128:(tt + 1) * 128],
                                                                          ident)
                                                    return ins
                                                K.op("pe", tr, r=[R_xsT[0], R_cst], w=[R_P[pb]])
                                                K.op("act", lambda e, t4=t4, tv=tv: e.activation(
                                                    out=xst[:, t4 * 4:(t4 + 1) * 4, :], in_=tv, func=AF.Copy),
                                                    r=[R_P[pb]], w=[R_xst])
                                            K.dma("sp", xs_s.rearrange("(t p) c -> p t c", p=128)[:, :, cj * 128:(cj + 1) * 128],
                                                  xst[:], r=[R_xst], key="xst")
                                        else:
                                            g = (cj - 8) % 2
                                            isB = cj < 10
                                            bb = cj % 2
                                            K.op("act", lambda e: e.activation(out=bcT[bb][:], in_=ct[cb][:], func=AF.Silu),
                                                 r=[R_ct[cb]], w=[R_bcT[bb]])
                                            K.dma("sp", (BT_s if isB else CT_s)[g], bcT[bb][:], r=[R_bcT[bb]], key=f"bcT{bb}")
                                            if isB:
                                                for t8 in range(2):
                                                    pb = 6 + t8
                                                    tv = Pb[pb].rearrange("p (i t) -> p i t", t=128)

                                                    def tr(e, t8=t8, tv=tv):
                                                        for i in range(8):
                                                            tt = t8 * 8 + i
                                                            ins = e.transpose(tv[:, i, :], bcT[bb][:, tt * 128:(tt + 1) * 128],
                                                                              ident_bf)
                                                        return ins
                                                    K.op("pe", tr, r=[R_bcT[bb], R_cbf], w=[R_P[pb]])
                                                    K.op("act", lambda e, t8=t8, tv=tv: e.activation(
                                                        out=bst[:, t8 * 8:(t8 + 1) * 8, :], in_=tv, func=AF.Copy),
                                                        r=[R_P[pb]], w=[R_bst])
                                                K.dma("sp", Bt_s.rearrange("(t p) c -> p t c", p=128)[:, :, g * 128:(g + 1) * 128],
                                                      bst[:], r=[R_bst], key="bst")
                                    state["deferred"] = conv
            run_deferred()
        K.flush()

    def phase_attn(L, q, onT, R_onT):
        with ExitStack() as st:
            oT = sb(st, "oT", [128, NH, S], F32)
            R_oT = [[K.res(f"oT{h}_{tb}") for tb in range(NB)] for h in range(NH)]
            qb = [sb(st, f"qb{i}", [128, S], BF16) for i in range(2)]
            kb_ = [sb(st, f"kb{i}", [128, S], BF16) for i in range(2)]
            vb = [sb(st, f"vb{i}", [128, NT, 128], BF16) for i in range(2)]
            R_qb = [K.res(f"qb{i}") for i in range(2)]
            R_kb = [K.res(f"kb{i}") for i in range(2)]
            R_vb = [K.res(f"vb{i}") for i in range(2)]
            et = [sb(st, f"et{i}", [128, 512], F32) for i in range(2)]
            R_et = [K.res(f"et{i}") for i in range(2)]
            spt = [sb(st, f"spt{i}", [128, 512], BF16) for i in range(4)]
            R_spt = [K.res(f"spt{i}") for i in range(4)]
            spf = [sb(st, f"spf{i}", [128, 512], F32) for i in range(2)]
            R_spf = [K.res(f"spf{i}") for i in range(2)]
            Wt = [sb(st, f"Wt{i}", [128, 512], BF16) for i in range(3)]
            R_Wt = [K.res(f"Wt{i}") for i in range(3)]
            Wf = [sb(st, f"Wf{i}", [128, 512], F32) for i in range(2)]
            R_Wf = [K.res(f"Wf{i}") for i in range(2)]
            acc = [sb(st, f"acc{i}", [128, 512], BF16) for i in range(2)]
            R_acc = [K.res(f"acc{i}") for i in range(2)]
            sqb = [sb(st, f"sqb{i}", [128, 512], BF16) for i in range(2)]
            R_sqb = [K.res(f"sqb{i}") for i in range(2)]
            rtn = sb(st, "rtn", [128, 512], F32)
            R_rtn = K.res("rtn")
            gctr = [0]
            vview = v_s.rearrange("(t p) c -> p t c", p=128)
            for h in range(NH):
                b = h % 2
                K.dma("sp", qb[b][:], qT_s[h], w=[R_qb[b]], key=f"qb{b}")
                K.dma("sp", kb_[b][:], kT_s[h], w=[R_kb[b]], key=f"kb{b}")
                K.dma("sp", vb[b][:], vview[:, :, h * 128:(h + 1) * 128], w=[R_vb[b]], key=f"vb{b}")
                def do_unit(h, tb, b):
                    unit = h * NB + tb
                    ob = 3 + unit % 2
                    kbs = list(range(4 * tb + 3, -1, -1))
                    n = len(kbs)
                    g0 = gctr[0]
                    gctr[0] += n
                    accsrc = {}

                    def mask_ap(o):
                        return cst[:, C_AM + (3 - o) * 128: C_AM + (3 - o) * 128 + 512]

                    def stageA(i):
                        g = g0 + i
                        kb = kbs[i]
                        zi = g % 3
                        diag = kb >= 4 * tb
                        o = kb - 4 * tb
                        K.op("pe", lambda e: e.matmul(P[zi][:], lhsT=kb_[b][:, kb * 128:(kb + 1) * 128],
                                                      rhs=qb[b][:, tb * 512:(tb + 1) * 512], start=True, stop=True),
                             r=[R_kb[b], R_qb[b]], w=[R_P[zi]])
                        K.op("act", lambda e: e.activation(out=et[g % 2][:], in_=P[zi][:], func=AF.Exp),
                             r=[R_P[zi]], w=[R_et[g % 2]])
                        if diag:
                            K.op("act", lambda e: e.activation(out=spf[g % 2][:], in_=et[g % 2][:], func=AF.Ln, bias=1.0),
                                 r=[R_et[g % 2]], w=[R_spf[g % 2]])
                            K.op("dve", lambda e: e.tensor_tensor(out=spt[g % 4][:], in0=spf[g % 2][:], in1=mask_ap(o),
                                                                  op=ALU.mult),
                                 r=[R_spf[g % 2], R_cst], w=[R_spt[g % 4]])
                        else:
                            K.op("act", lambda e: e.activation(out=spt[g % 4][:], in_=et[g % 2][:], func=AF.Ln, bias=1.0),
                                 r=[R_et[g % 2]], w=[R_spt[g % 4]])
                        if i == 1:
                            accsrc[1] = (spt[(g - 1) % 4], R_spt[(g - 1) % 4])
                        elif i >= 2:
                            prev, R_prev = accsrc[i - 1]
                            a = i % 2
                            K.op("pool", lambda e: e.tensor_tensor(out=acc[a][:], in0=prev[:], in1=spt[(g - 1) % 4][:],
                                                                   op=ALU.add),
                                 r=[R_prev, R_spt[(g - 1) % 4]], w=[R_acc[a]])
                            accsrc[i] = (acc[a], R_acc[a])

                    def stageB(i):
                        g = g0 + i
                        kb = kbs[i]
                        zi = g % 3
                        diag = kb >= 4 * tb
                        o = kb - 4 * tb

                        def mm(e):
                            ins = e.matmul(P[zi][:], lhsT=nlge_bf, rhs=spt[g % 4][:], start=False, stop=(i == 0),
                                           skip_group_check=True)
                            if i >= 1:
                                ins = e.matmul(P[zi][:], lhsT=nones_bf, rhs=accsrc[i][0][:], start=False, stop=True,
                                               skip_group_check=True)
                            return ins
                        rr = [R_spt[g % 4], R_cbf] + ([accsrc[i][1]] if i >= 1 else [])
                        K.op("pe", mm, r=rr, w=[R_P[zi]])
                        if diag:
                            K.op("act", lambda e: e.activation(out=Wf[g % 2][:], in_=P[zi][:], func=AF.Exp),
                                 r=[R_P[zi]], w=[R_Wf[g % 2]])
                            K.op("dve", lambda e: e.tensor_tensor(out=Wt[g % 3][:], in0=Wf[g % 2][:], in1=mask_ap(o),
                                                                  op=ALU.mult),
                                 r=[R_Wf[g % 2], R_cst], w=[R_Wt[g % 3]])
                        else:
                            K.op("act", lambda e: e.activation(out=Wt[g % 3][:], in_=P[zi][:], func=AF.Exp),
                                 r=[R_P[zi]], w=[R_Wt[g % 3]])

                    def stageC(i):
                        g = g0 + i
                        kb = kbs[i]
                        K.op("pe", lambda e: e.matmul(P[ob][:], lhsT=vb[b][:, kb, :], rhs=Wt[g % 3][:],
                                                      start=(i == 0), stop=(i == n - 1), skip_group_check=True),
                             r=[R_vb[b], R_Wt[g % 3]], w=[R_P[ob]])
                        if i == n - 1:
                            K.op("dve", lambda e: e.tensor_copy(out=oT[:, h, tb * 512:(tb + 1) * 512], in_=P[ob][:]),
                                 r=[R_P[ob]], w=[R_oT[h][tb]])

                    for step in range(n + 2):
                        if step < n:
                            stageA(step)
                        if 1 <= step <= n:
                            stageB(step - 1)
                        if step >= 2:
                            stageC(step - 2)
                for tb in range(NB):
                    do_unit(h, tb, b)
            def do_norm(tb):
                pn = 5 + tb % 2
                for h in range(NH):
                    k2 = (tb * NH + h) % 2
                    K.op("act", lambda e, h=h, k2=k2: e.activation(out=sqb[k2][:], in_=oT[:, h, tb * 512:(tb + 1) * 512],
                                                                    func=AF.Square),
                         r=[R_oT[h][tb]], w=[R_sqb[k2]])
                    K.op("pe", lambda e, h=h, k2=k2: e.matmul(P[pn][:], lhsT=nones_bf, rhs=sqb[k2][:], start=(h == 0),
                                                               stop=(h == NH - 1), skip_group_check=True),
                         r=[R_sqb[k2], R_cbf], w=[R_P[pn]])
                K.op("act", lambda e: e.activation(out=rtn[:], in_=P[pn][:], func=AF.Ln, scale=-1.0 / 1024,
                                                   bias=epsb[:, 0:1]), r=[R_P[pn]], w=[R_rtn])
                K.op("act", lambda e: e.activation(out=rtn[:], in_=rtn[:], func=AF.Exp, scale=-0.5),
                     r=[R_rtn], w=[R_rtn])
                for h in range(NH):
                    K.op("dve", lambda e, h=h: e.scalar_tensor_tensor(
                        out=onT[:, h, tb * 512:(tb + 1) * 512], in0=oT[:, h, tb * 512:(tb + 1) * 512],
                        scalar=pp[:, PP_AOG + h:PP_AOG + h + 1], in1=rtn[:], op0=ALU.mult, op1=ALU.mult),
                        r=[R_oT[h][tb], R_rtn, R_pp], w=[R_onT])
            for tb in range(NB):
                do_norm(tb)
        K.flush()

    def phase_ssd(L, q, osT, R_osT):
        with ExitStack() as st:
            def dbl(name, shape, dt):
                return ([sb(st, f"{name}{i}", shape, dt) for i in range(2)], [K.res(f"{name}{i}") for i in range(2)])
            xs, R_xs = dbl("s_xs", [128, 1024], F32)
            szt, R_sz = dbl("s_sz", [128, 1024], F32)
            dtt, R_dt = dbl("s_dt", [128, 16], F32)
            BT, R_BT = dbl("s_BT", [128, 2, 128], BF16)
            CT, R_CT = dbl("s_CT", [128, 2, 128], BF16)
            Bt, R_Bt = dbl("s_Bt", [128, 256], BF16)
            da, R_da = dbl("s_da", [128, 16], F32)
            ex, R_ex = dbl("s_ex", [128, 64], F32)
            Dm, R_Dm = dbl("s_Dm", [128, 16, 128], F32)
            dec, R_dec = dbl("s_dec", [128, 16, 128], F32)
            cbm, R_cbm = dbl("s_cbm", [128, 2, 128], F32)
            MT, R_MT = dbl("s_MT", [128, 16, 128], BF16)
            xdt, R_xdt = dbl("s_xdt", [128, 16, 64], BF16)
            xde, R_xde = dbl("s_xde", [128, 16, 64], BF16)
            hSb, R_hSb = dbl("s_hSb", [128, 1024], BF16)
            t1, R_t1 = dbl("s_t1", [128, 1024], F32)
            t2, R_t2 = dbl("s_t2", [128, 1024], F32)
            yz, R_yz = dbl("s_yz", [128, 1024], F32)
            osm, R_osm = dbl("s_osm", [128, 1024], BF16)
            ssq, R_ssq = dbl("s_ssq", [128, 8], F32)
            hS = sb(st, "s_hS", [128, 1024], F32)
            R_hS = K.res("s_hS")
            junk = sb(st, "s_junk", [128, 512], BF16)
            R_junk = K.res("s_junk")
            K.op("dve", lambda e: e.memset(hS[:], 0.0), w=[R_hS])
            K.op("dve", lambda e: e.memset(hSb[0][:], 0.0), w=[R_hSb[0]])
            xs_v = xs_s
            for c in range(NT):
                b = c % 2
                tok = slice(c * 128, (c + 1) * 128)
                K.dma("sp", xs[b][:], xs_s[tok, :], w=[R_xs[b]], key=f"s_xs{b}")
                K.dma("sp", szt[b][:], sz_s[tok, :], w=[R_sz[b]], key=f"s_sz{b}")
                K.dma("sp", dtt[b][:], dt_s[tok, :], w=[R_dt[b]], key=f"s_dt{b}")
                K.dma("sp", BT[b][:], BT_s[:, :, tok].rearrange("g n t -> n g t"), w=[R_BT[b]], key=f"s_BT{b}")
                K.dma("sp", CT[b][:], CT_s[:, :, tok].rearrange("g n t -> n g t"), w=[R_CT[b]], key=f"s_CT{b}")
                K.dma("sp", Bt[b][:], Bt_s[tok, :], w=[R_Bt[b]], key=f"s_Bt{b}")
                K.op("dve", lambda e, b=b: e.tensor_tensor(out=da[b][:], in0=dtt[b][:], in1=abc[:], op=ALU.mult),
                     r=[R_dt[b], R_abc], w=[R_da[b]])

                def mm(e, b=b):
                    e.matmul(P[0][:, 0:16], lhsT=ustr, rhs=da[b][:], start=True, stop=True)
                    e.matmul(P[0][:, 16:32], lhsT=tri, rhs=da[b][:], start=True, stop=True)
                    return e.matmul(P[0][:, 32:48], lhsT=ones, rhs=da[b][:], start=True, stop=True)
                K.op("pe", mm, r=[R_da[b], R_cst], w=[R_P[0]])
                K.op("act", lambda e, b=b: e.activation(out=ex[b][:, 0:48], in_=P[0][:, 0:48], func=AF.Exp),
                     r=[R_P[0]], w=[R_ex[b]])
                K.op("dve", lambda e, b=b: e.tensor_tensor(out=ex[b][:, 48:64], in0=ex[b][:, 0:16], in1=dtt[b][:],
                                                           op=ALU.mult), r=[R_ex[b], R_dt[b]], w=[R_ex[b]])
                K.op("dve", lambda e, b=b: e.tensor_tensor(out=Dm[b][:], in0=ustr[:, None, :].to_broadcast([128, 16, 128]),
                                                           in1=da[b][:, :, None].to_broadcast([128, 16, 128]), op=ALU.mult),
                     r=[R_cst, R_da[b]], w=[R_Dm[b]])
                for k in range(4):
                    pk = 1 + k
                    sv = P[pk][:].rearrange("p (h l) -> p h l", l=128)

                    def mm(e, b=b, k=k, sv=sv):
                        for i in range(4):
                            ins = e.matmul(sv[:, i, :], lhsT=Dm[b][:, 4 * k + i, :], rhs=tri, start=True, stop=True)
                        return ins
                    K.op("pe", mm, r=[R_Dm[b], R_cst], w=[R_P[pk]])
                    K.op("act", lambda e, b=b, k=k, sv=sv: e.activation(out=dec[b][:, 4 * k:4 * k + 4, :], in_=sv, func=AF.Exp),
                         r=[R_P[pk]], w=[R_dec[b]])
                cv = P[0][:, 128:384].rearrange("p (g l) -> p g l", l=128)

                def mm(e, b=b, cv=cv):
                    e.matmul(cv[:, 0, :], lhsT=BT[b][:, 0, :], rhs=CT[b][:, 0, :], start=True, stop=True)
                    return e.matmul(cv[:, 1, :], lhsT=BT[b][:, 1, :], rhs=CT[b][:, 1, :], start=True, stop=True)
                K.op("pe", mm, r=[R_BT[b], R_CT[b]], w=[R_P[0]])
                K.op("dve", lambda e, b=b, cv=cv: e.tensor_tensor(out=cbm[b][:], in0=cv,
                                                                  in1=tri[:, None, :].to_broadcast([128, 2, 128]), op=ALU.mult),
                     r=[R_P[0], R_cst], w=[R_cbm[b]])
                for g in range(2):
                    K.op("dve", lambda e, b=b, g=g: e.tensor_tensor(
                        out=MT[b][:, 8 * g:8 * g + 8, :], in0=dec[b][:, 8 * g:8 * g + 8, :],
                        in1=cbm[b][:, g:g + 1, :].to_broadcast([128, 8, 128]), op=ALU.mult),
                        r=[R_dec[b], R_cbm[b]], w=[R_MT[b]])
                xs3 = xs[b][:].rearrange("p (h d) -> p h d", d=64)
                K.op("dve", lambda e, b=b, xs3=xs3: e.tensor_tensor(
                    out=xdt[b][:], in0=xs3, in1=dtt[b][:, :, None].to_broadcast([128, 16, 64]), op=ALU.mult),
                    r=[R_xs[b], R_dt[b]], w=[R_xdt[b]])
                K.op("pool", lambda e, b=b, xs3=xs3: e.tensor_tensor(
                    out=xde[b][:], in0=xs3, in1=ex[b][:, 48:64, None].to_broadcast([128, 16, 64]), op=ALU.mult),
                    r=[R_xs[b], R_ex[b]], w=[R_xde[b]])
                for g in range(2):
                    yv = P[1 + g][:].rearrange("p (h d) -> p h d", d=64)

                    def mm(e, b=b, g=g, yv=yv):
                        for i in range(8):
                            ins = e.matmul(yv[:, i, :], lhsT=MT[b][:, 8 * g + i, :], rhs=xdt[b][:, 8 * g + i, :],
                                           start=True, stop=True)
                        return ins
                    K.op("pe", mm, r=[R_MT[b], R_xdt[b]], w=[R_P[1 + g]])
                    K.op("pe", lambda e, b=b, g=g: e.matmul(P[3 + g][:], lhsT=CT[b][:, g, :],
                                                            rhs=hSb[b][:, g * 512:(g + 1) * 512], start=True, stop=True),
                         r=[R_CT[b], R_hSb[b]], w=[R_P[3 + g]])
                    K.op("pe", lambda e, b=b, g=g: e.matmul(
                        P[5 + g][:], lhsT=Bt[b][:, g * 128:(g + 1) * 128],
                        rhs=xde[b][:, 8 * g:8 * g + 8, :].rearrange("p h d -> p (h d)"), start=True, stop=True),
                        r=[R_Bt[b], R_xde[b]], w=[R_P[5 + g]])
                for g in range(2):
                    gs = slice(g * 512, (g + 1) * 512)
                    K.op("dve", lambda e, b=b, g=g, gs=gs: e.tensor_tensor(
                        out=t1[b][:, gs].rearrange("p (h d) -> p h d", d=64),
                        in0=P[3 + g][:].rearrange("p (h d) -> p h d", d=64),
                        in1=ex[b][:, 16 + 8 * g:16 + 8 * g + 8, None].to_broadcast([128, 8, 64]), op=ALU.mult),
                        r=[R_P[3 + g], R_ex[b]], w=[R_t1[b]])
                    K.op("pool", lambda e, b=b, g=g, gs=gs: e.tensor_tensor(
                        out=t2[b][:, gs].rearrange("p (h d) -> p h d", d=64),
                        in0=xs[b][:, gs].rearrange("p (h d) -> p h d", d=64),
                        in1=pp[:, PP_DSK + 8 * g:PP_DSK + 8 * g + 8, None].to_broadcast([128, 8, 64]), op=ALU.mult),
                        r=[R_xs[b], R_pp], w=[R_t2[b]])
                    K.op("pool", lambda e, b=b, gs=gs: e.tensor_tensor(out=t2[b][:, gs], in0=t2[b][:, gs], in1=t1[b][:, gs],
                                                                       op=ALU.add),
                         r=[R_t2[b], R_t1[b]], w=[R_t2[b]])
                    K.op("dve", lambda e, b=b, g=g, gs=gs: e.tensor_tensor(out=t1[b][:, gs], in0=t2[b][:, gs], in1=P[1 + g][:],
                                                                           op=ALU.add),
                         r=[R_t2[b], R_P[1 + g]], w=[R_t1[b]])
                    K.op("pool", lambda e, b=b, gs=gs: e.tensor_tensor(out=yz[b][:, gs], in0=t1[b][:, gs], in1=szt[b][:, gs],
                                                                       op=ALU.mult),
                         r=[R_t1[b], R_sz[b]], w=[R_yz[b]])
                    K.op("act", lambda e, b=b, g=g, gs=gs: e.activation(out=junk[:], in_=yz[b][:, gs], func=AF.Square,
                                                                        accum_out=ssq[b][:, g:g + 1]),
                         r=[R_yz[b]], w=[R_junk, R_ssq[b]])
                K.op("act", lambda e, b=b: e.activation(out=ssq[b][:, 2:4], in_=ssq[b][:, 0:2], func=AF.Ln, scale=1.0 / 512,
                                                        bias=epsb[:, 0:1]), r=[R_ssq[b]], w=[R_ssq[b]])
                K.op("act", lambda e, b=b: e.activation(out=ssq[b][:, 4:6], in_=ssq[b][:, 2:4], func=AF.Exp, scale=-0.5),
                     r=[R_ssq[b]], w=[R_ssq[b]])
                for g in range(2):
                    gs = slice(g * 512, (g + 1) * 512)
                    K.op("dve", lambda e, b=b, g=g, gs=gs: e.scalar_tensor_tensor(
                        out=osm[b][:, gs], in0=yz[b][:, gs], scalar=ssq[b][:, 4 + g:5 + g],
                        in1=pp[:, PP_SOG + g * 512:PP_SOG + (g + 1) * 512], op0=ALU.mult, op1=ALU.mult),
                        r=[R_yz[b], R_ssq[b], R_pp], w=[R_osm[b]])
                tv = Pb[7].rearrange("p (i t) -> p i t", t=128)

                def tr(e, b=b, tv=tv):
                    for i in range(8):
                        ins = e.transpose(tv[:, i, :], osm[b][:, i * 128:(i + 1) * 128], ident_bf)
                    return ins
                K.op("pe", tr, r=[R_osm[b], R_cbf], w=[R_P[7]])
                K.op("act", lambda e, c=c, tv=tv: e.activation(out=osT[:, :, c * 128:(c + 1) * 128], in_=tv, func=AF.Copy),
                     r=[R_P[7]], w=[R_osT])
                K.op("dve", lambda e, b=b: e.tensor_tensor(
                    out=hS[:].rearrange("p (h d) -> p h d", d=64), in0=hS[:].rearrange("p (h d) -> p h d", d=64),
                    in1=ex[b][:, 32:48, None].to_broadcast([128, 16, 64]), op=ALU.mult),
                    r=[R_hS, R_ex[b]], w=[R_hS])
                for g in range(2):
                    gs = slice(g * 512, (g + 1) * 512)
                    K.op("dve", lambda e, g=g, gs=gs: e.tensor_tensor(out=hS[:, gs], in0=hS[:, gs], in1=P[5 + g][:], op=ALU.add),
                         r=[R_hS, R_P[5 + g]], w=[R_hS])
                K.op("act", lambda e, b=b: e.activation(out=hSb[1 - b][:], in_=hS[:], func=AF.Copy),
                     r=[R_hS], w=[R_hSb[1 - b]])
        K.flush()

    def phase_outproj(L, q, xsrc, onT, R_onT, osT, R_osT):
        with ExitStack() as st:
            wbf = [sb(st, f"wb{i}", [128, 8192], BF16) for i in range(3)]
            R_wb = [K.res(f"wb{i}") for i in range(3)]
            xr = [sb(st, f"xr{i}", [128, 512], F32) for i in range(4)]
            R_xr = [K.res(f"xr{i}") for i in range(4)]
            xo = [sb(st, f"xo{i}", [128, 512], F32) for i in range(3)]
            R_xo = [K.res(f"xo{i}") for i in range(3)]
            w_l = w_out[L].rearrange("(c p) n -> p c n", p=128)
            units = [(db, tt) for db in range(4) for tt in range(NT)]

            def load(u):
                db, tt = units[u]
                K.dma("sp", xr[u % 4][:], xsrc[q, tt * 128:(tt + 1) * 128, db * 512:(db + 1) * 512],
                      w=[R_xr[u % 4]], key=f"xr{u % 4}")
            load(0)
            load(1)
            for u, (db, tt) in enumerate(units):
                slot = db % 3
                wv = wview(wbf[slot], 512)
                if tt == 0:
                    K.dma("pool", wv, w_l[:, :, db * 512:(db + 1) * 512], w=[R_wb[slot]], key=f"wb{slot}")
                if u + 2 < len(units):
                    load(u + 2)
                pa = u % 4

                def mm(e, tt=tt, pa=pa, wv=wv):
                    for c in range(16):
                        src = onT if c < 8 else osT
                        ins = e.matmul(P[pa][:], lhsT=src[:, c % 8, tt * 128:(tt + 1) * 128], rhs=wv[:, c, :],
                                       start=(c == 0), stop=(c == 15))
                    return ins
                K.op("pe", mm, r=[R_onT, R_osT, R_wb[slot]], w=[R_P[pa]])
                K.op("dve", lambda e, u=u, pa=pa: e.tensor_tensor(out=xo[u % 3][:], in0=xr[u % 4][:], in1=P[pa][:], op=ALU.add),
                     r=[R_xr[u % 4], R_P[pa]], w=[R_xo[u % 3]])
                K.dma("sp", y[q, tt * 128:(tt + 1) * 128, db * 512:(db + 1) * 512], xo[u % 3][:],
                      r=[R_xo[u % 3]], key=f"xo{u % 3}")
        K.flush()

    def phase_ffn(L, q):
        TBN = 1024
        for tbk in range(S // TBN):
            t0 = tbk * TBN
            with ExitStack() as st:
                hT = sb(st, "f_hT", [128, 16, TBN], BF16)
                R_hT = [K.res(f"f_hT{j}") for j in range(TBN // 128)]
                with ExitStack() as st2:
                    emit_norm(st2, lambda j: y[q, t0 + j * 128: t0 + (j + 1) * 128, :], TBN // 128, hT, R_hT, PP_NFFN, [6, 7])
                    K.flush()
                actT = sb(st, "f_act", [128, NFC, TBN], BF16)
                R_act = [K.res(f"f_act{i}") for i in range(11)]
                wbf = [sb(st, f"f_wb{i}", [128, 8192], BF16) for i in range(4)]
                R_wb = [K.res(f"f_wb{i}") for i in range(4)]
                sgt = [sb(st, f"f_sg{i}", [128, 512], F32) for i in range(2)]
                R_sg = [K.res(f"f_sg{i}") for i in range(2)]
                xr = [sb(st, f"f_xr{i}", [128, 256], F32) for i in range(4)]
                R_xr = [K.res(f"f_xr{i}") for i in range(4)]
                xo = [sb(st, f"f_xo{i}", [128, 256], F32) for i in range(3)]
                R_xo = [K.res(f"f_xo{i}") for i in range(3)]
                wg_l = w_gate[L].rearrange("(c p) n -> p c n", p=128)
                wu_l = w_up[L].rearrange("(c p) n -> p c n", p=128)
                wd_l = w_down[L].rearrange("(c p) n -> p c n", p=128)
                u = 0
                for fg in range(11):
                    sg_, su_ = fg % 2, 2 + fg % 2
                    wg = wview(wbf[sg_], 512)
                    wu = wview(wbf[su_], 512)
                    K.dma("pool", wg, wg_l[:, :, fg * 512:(fg + 1) * 512], w=[R_wb[sg_]], key=f"f_wb{sg_}")
                    K.dma("pool", wu, wu_l[:, :, fg * 512:(fg + 1) * 512], w=[R_wb[su_]], key=f"f_wb{su_}")
                    for j in range(4):
                        for ts in range(TBN // 512):
                            pg, pu = u % 3, 3 + u % 3
                            k2 = u % 2
                            u += 1

                            def mm(e, j=j, ts=ts, pg=pg, pu=pu, wg=wg, wu=wu):
                                for c in range(16):
                                    e.matmul(P[pg][:], lhsT=wg[:, c, j * 128:(j + 1) * 128],
                                             rhs=hT[:, c, ts * 512:(ts + 1) * 512], start=(c == 0), stop=(c == 15))
                                for c in range(16):
                                    ins = e.matmul(P[pu][:], lhsT=wu[:, c, j * 128:(j + 1) * 128],
                                                   rhs=hT[:, c, ts * 512:(ts + 1) * 512], start=(c == 0), stop=(c == 15))
                                return ins
                            K.op("pe", mm, r=[R_hT[4 * ts + i] for i in range(4)] + [R_wb[sg_], R_wb[su_]],
                                 w=[R_P[pg], R_P[pu]])
                            K.op("act", lambda e, pg=pg, k2=k2: e.activation(out=sgt[k2][:], in_=P[pg][:], func=AF.Silu),
                                 r=[R_P[pg]], w=[R_sg[k2]])
                            K.op("dve", lambda e, fg=fg, j=j, ts=ts, pu=pu, k2=k2: e.tensor_tensor(
                                out=actT[:, fg * 4 + j, ts * 512:(ts + 1) * 512], in0=sgt[k2][:], in1=P[pu][:], op=ALU.mult),
                                r=[R_sg[k2], R_P[pu]], w=[R_act[fg]])
                units = [(db, tt) for db in range(8) for tt in range(TBN // 128)]

                def load(i):
                    db, tt = units[i]
                    K.dma("sp", xr[i % 4][:], y[q, t0 + tt * 128: t0 + (tt + 1) * 128, db * 256:(db + 1) * 256],
                          w=[R_xr[i % 4]], key=f"f_xr{i % 4}")
                load(0)
                load(1)
                for i, (db, tt) in enumerate(units):
                    s0, s1 = (2 * db) % 4, (2 * db + 1) % 4
                    wd0 = wview(wbf[s0], 256)
                    wd1 = wview(wbf[s1], 256)
                    if tt == 0:
                        K.dma("pool", wd0, wd_l[:, 0:22, db * 256:(db + 1) * 256], w=[R_wb[s0]], key=f"f_wb{s0}")
                        K.dma("pool", wd1, wd_l[:, 22:44, db * 256:(db + 1) * 256], w=[R_wb[s1]], key=f"f_wb{s1}")
                    if i + 2 < len(units):
                        load(i + 2)
                    pa = i % 4

                    def mm(e, tt=tt, pa=pa, wd0=wd0, wd1=wd1):
                        for f in range(NFC):
                            wd = wd0 if f < 22 else wd1
                            ins = e.matmul(P[pa][:, 0:256], lhsT=actT[:, f, tt * 128:(tt + 1) * 128], rhs=wd[:, f % 22, :],
                                           start=(f == 0), stop=(f == NFC - 1))
                        return ins
                    K.op("pe", mm, r=R_act + [R_wb[s0], R_wb[s1]], w=[R_P[pa]])
                    K.op("dve", lambda e, i=i, pa=pa: e.tensor_tensor(out=xo[i % 3][:], in0=xr[i % 4][:], in1=P[pa][:, 0:256],
                                                                      op=ALU.add),
                         r=[R_xr[i % 4], R_P[pa]], w=[R_xo[i % 3]])
                    K.dma("sp", y[q, t0 + tt * 128: t0 + (tt + 1) * 128, db * 256:(db + 1) * 256], xo[i % 3][:],
                          r=[R_xo[i % 3]], key=f"f_xo{i % 3}")
            K.flush()

    for L in range(n_layers):
        K.dma("sp", pp[:], pp_d[L], w=[R_pp], key="pp")
        K.op("act", lambda e: e.activation(out=abc[:], in_=pp[:, PP_ALOG:PP_ALOG + 16], func=AF.Exp),
             r=[R_pp], w=[R_abc])
        K.op("dve", lambda e: e.tensor_scalar(out=abc[:], in0=abc[:], scalar1=-1.0, scalar2=None, op0=ALU.mult),
             r=[R_abc], w=[R_abc])
        K.flush()
        for q in range(n_seq):
            xsrc = x_in if L == 0 else y
            if "inproj" in phases:
                phase_inproj(L, q, xsrc)
            with ExitStack() as mst:
                onT = sb(mst, "onT", [128, NH, S], BF16)
                osT = sb(mst, "osT", [128, NH, S], BF16)
                R_onT, R_osT = K.res("onT"), K.res("osT")
                if "attn" in phases:
                    phase_attn(L, q, onT, R_onT)
                if "ssd" in phases:
                    phase_ssd(L, q, osT, R_osT)
                if debug:
                    K.dma("sp", dbg_on, onT[:], r=[R_onT], key="dbg_on")
                    K.dma("sp", dbg_os, osT[:], r=[R_osT], key="dbg_os")
                    K.flush()
                if "outproj" in phases:
                    phase_outproj(L, q, xsrc, onT, R_onT, osT, R_osT)
            if "ffn" in phases:
                phase_ffn(L, q)
    K.flush(final=True)
    es.close()
    return nc, K


def make_consts():
    p = np.arange(128)[:, None]
    f = np.arange(128)[None, :]
    c = np.zeros((128, NCST), np.float32)
    c[:, C_ID:C_ID + 128] = (p == f)
    c[:, C_ONE:C_ONE + 128] = 1.0
    c[:, C_US:C_US + 128] = (p > f)
    c[:, C_TRI:C_TRI + 128] = (p <= f)
    c[:, C_LGE:C_LGE + 128] = (p >= f)
    fa = np.arange(896)[None, :]
    c[:, C_AM:C_AM + 896] = (fa - p > 384)
    return c


def pack_params(inp, n_layers):
    pp = np.zeros((n_layers, 128, NP_), np.float32)
    for l in range(n_layers):
        pp[l, :, PP_NMIX:PP_NMIX + 16] = inp["norm_mix"][l].reshape(16, 128).T
        pp[l, :, PP_NFFN:PP_NFFN + 16] = inp["norm_ffn"][l].reshape(16, 128).T
        pp[l, :, PP_QG] = inp["q_gain"][l]
        pp[l, :, PP_KG] = inp["k_gain"][l]
        cw = inp["conv_w"][l].reshape(4, 12, 128)
        pp[l, :, PP_CW:PP_CW + 48] = cw.transpose(2, 1, 0).reshape(128, 48)
        pp[l, :, PP_CB:PP_CB + 12] = inp["conv_b"][l].reshape(12, 128).T
        pp[l, :, PP_DTB:PP_DTB + 16] = np.broadcast_to(inp["dt_bias"][l][None, :], (128, 16))
        pp[l, :, PP_ALOG:PP_ALOG + 16] = np.broadcast_to(inp["a_log"][l][None, :], (128, 16))
        pp[l, :, PP_DSK:PP_DSK + 16] = np.broadcast_to(inp["d_skip"][l][None, :], (128, 16))
        pp[l, :, PP_AOG:PP_AOG + 8] = inp["attn_out_gain"][l].reshape(8, 128).T
        pp[l, :, PP_SOG:PP_SOG + 1024] = np.broadcast_to(inp["ssm_out_gain"][l][None, :], (128, 1024))
    return pp


_CACHE = {}
N_LAUNCH_LAYERS = 2


def kernel(**inputs):
    inp = {k: np.asarray(v) for k, v in inputs.items()}
    x = np.ascontiguousarray(inp["x"], dtype=np.float32)
    n_cores = 8
    n_seq = x.shape[0] // n_cores
    depth = inp["w_in"].shape[0]
    lpl = N_LAUNCH_LAYERS
    key = (lpl, n_seq)
    if key not in _CACHE:
        _CACHE[key] = build(lpl, n_seq)[0]
    nc = _CACHE[key]
    cst = make_consts()
    pp = pack_params(inp, depth)
    cur = x
    for l0 in range(0, depth, lpl):
        sl = slice(l0, l0 + lpl)
        shared = {
            "w_in": np.ascontiguousarray(inp["w_in"][sl], dtype=np.float32),
            "w_out": np.ascontiguousarray(inp["w_out"][sl], dtype=np.float32),
            "w_gate": np.ascontiguousarray(inp["w_gate"][sl], dtype=np.float32),
            "w_up": np.ascontiguousarray(inp["w_up"][sl], dtype=np.float32),
            "w_down": np.ascontiguousarray(inp["w_down"][sl], dtype=np.float32),
            "pp": np.ascontiguousarray(pp[sl]),
            "cst": cst,
        }
        in_maps = [dict(shared, x=np.ascontiguousarray(cur[c * n_seq:(c + 1) * n_seq])) for c in range(n_cores)]
        res = run_bass_kernel_spmd(nc, in_maps, core_ids=list(range(n_cores)))
        cur = np.concatenate([np.asarray(r["y"]) for r in res.results], axis=0).astype(np.float32)
    return cur
```

```python
from contextlib import ExitStack
import math
import numpy as np
import concourse.bass as bass
import concourse.mybir as mybir
from concourse.bass_utils import run_bass_kernel_spmd

F32, BF16 = mybir.dt.float32, mybir.dt.bfloat16
AF = mybir.ActivationFunctionType
ALU = mybir.AluOpType

S = 2048
D = 2048
NH = 8
DH = 128
DFF = 5632
IN_DIM = 5648
EPS = 1e-6
NT = S // 128
NB = S // 512
NFC = DFF // 128
PP_NMIX, PP_NFFN, PP_QG, PP_KG, PP_CW, PP_CB, PP_DTB, PP_ALOG, PP_DSK, PP_AOG, PP_SOG, NP_ = \
    0, 16, 32, 33, 34, 82, 94, 110, 126, 142, 150, 1174
C_ID, C_ONE, C_US, C_TRI, C_LGE, C_AM, NCST = 0, 128, 256, 384, 512, 640, 1536

ENG = ["pe", "act", "dve", "pool", "sp"]


class Res:
    __slots__ = ("name", "lw", "rd")

    def __init__(self, name):
        self.name = name
        self.lw = None
        self.rd = []


class Op:
    __slots__ = ("eng", "fn", "kind", "key", "need", "val", "deps")


class Sched:
    def __init__(self, nc, es):
        self.nc = nc
        self.sem = {e: es.enter_context(nc.semaphore("s_" + e)) for e in ENG}
        self.cnt = {e: 0 for e in ENG}
        self.seen = {e: {} for e in ENG}
        self.dsem = {}
        self.dcnt = {}
        self.es = es
        self.ops = {e: [] for e in ENG}
        self.pending = {e: [] for e in ENG}
        self.allres = []
        self.dirty = {}
        self.n_inst = 0

    def res(self, name):
        r = Res(name)
        self.allres.append(r)
        return r

    def _mk(self, eng, fn, r, w, kind, key):
        o = Op()
        o.eng, o.fn, o.kind, o.key, o.need, o.val = eng, fn, kind, key, False, None
        deps = {}
        for d in self.pending[eng]:
            deps[d] = True
        self.pending[eng] = []
        for x in r:
            if x.lw is not None:
                deps[x.lw] = True
        for x in w:
            if x.lw is not None and x.lw not in deps:
                deps[x.lw] = False
            for q in x.rd:
                if q not in deps:
                    deps[q] = False
        keep = []
        for d, raw in deps.items():
            if d is o:
                continue
            if kind == "c" and d.kind == "c" and d.eng == eng and not raw:
                continue
            keep.append(d)
            if d.kind != "d":
                d.need = True
        o.deps = keep
        for x in r:
            if x not in w:
                x.rd = [q for q in x.rd if not (q.eng == eng and q.kind == kind and q.key == key)]
                x.rd.append(o)
        for x in w:
            x.lw = o
            x.rd = []
        self.ops[eng].append(o)
        return o

    def op(self, eng, fn, r=(), w=()):
        return self._mk(eng, fn, r, w, "c", None)

    def dma(self, eng, out, in_, r=(), w=(), key=None):
        assert key is not None
        if key not in self.dsem:
            self.dsem[key] = self.es.enter_context(self.nc.semaphore("d_" + key))
            self.dcnt[key] = 0
        o = self._mk(eng, lambda e: e.dma_start(out=out, in_=in_), r, w, "d", key)
        self.dcnt[key] += 16
        o.val = self.dcnt[key]
        self.dirty[key] = o
        return o

    def flush(self, final=False):
        bar = Op()
        bar.eng, bar.fn, bar.kind, bar.key, bar.need, bar.val = "sp", None, "b", None, True, None
        deps = list(self.pending["sp"])
        self.pending["sp"] = []
        for e in ENG:
            if self.ops[e]:
                lo = None
                for o in reversed(self.ops[e]):
                    if o.kind == "c":
                        lo = o
                        break
                if lo is not None:
                    lo.need = True
                    deps.append(lo)
        deps.extend(self.dirty.values())
        self.dirty = {}
        bar.deps = deps
        self.ops["sp"].append(bar)
        for e in ENG:
            if e != "sp":
                self.pending[e].append(bar)
        for r in self.allres:
            r.lw = None
            r.rd = []
        for e in ENG:
            for o in self.ops[e]:
                if o.kind != "d" and o.need:
                    self.cnt[e] += 1
                    o.val = self.cnt[e]
        nc = self.nc
        me = self

        def replay(e, h):
            seen = me.seen[e]
            for o in me.ops[e]:
                for d in o.deps:
                    if d.kind == "d":
                        sem = me.dsem[d.key]
                    else:
                        sem = me.sem[d.eng]
                    if seen.get(sem.name, 0) >= d.val:
                        continue
                    seen[sem.name] = d.val
                    h.wait_ge(sem, d.val)
                    me.n_inst += 1
                if o.kind == "b":
                    h.sem_inc(me.sem["sp"], 1)
                    continue
                ins = o.fn(h)
                me.n_inst += 1
                if o.kind == "d":
                    ins.then_inc(me.dsem[o.key], 16)
                elif o.need:
                    ins.then_inc(me.sem[e], 1)
            if final and e != "sp":
                for d in me.pending[e]:
                    h.wait_ge(me.sem["sp"], d.val)

        with nc.Block() as block:
            @block.tensor
            def _(h):
                replay("pe", h)

            @block.scalar
            def _(h):
                replay("act", h)

            @block.vector
            def _(h):
                replay("dve", h)

            @block.gpsimd
            def _(h):
                replay("pool", h)

            @block.sync
            def _(h):
                replay("sp", h)
        self.ops = {e: [] for e in ENG}


ALL_PHASES = ('inproj', 'attn', 'ssd', 'outproj', 'ffn')


def build(n_layers, n_seq, debug=False, phases=ALL_PHASES):
    nc = bass.Bass("TRN2", target_bir_lowering=False)

    def dram(name, shape, dt, kind="Internal"):
        return nc.dram_tensor(name, shape, dt, kind=kind).ap()

    sk = "ExternalOutput" if debug else "Internal"
    x_in = dram("x", [n_seq, S, D], F32, "ExternalInput")
    y = dram("y", [n_seq, S, D], F32, "ExternalOutput")
    w_in = [dram(f"w_in{l}", [D, IN_DIM], F32, "ExternalInput") for l in range(n_layers)]
    w_out = [dram(f"w_out{l}", [D, D], F32, "ExternalInput") for l in range(n_layers)]
    w_gate = [dram(f"w_gate{l}", [D, DFF], F32, "ExternalInput") for l in range(n_layers)]
    w_up = [dram(f"w_up{l}", [D, DFF], F32, "ExternalInput") for l in range(n_layers)]
    w_down = [dram(f"w_down{l}", [DFF, D], F32, "ExternalInput") for l in range(n_layers)]
    pp_d = dram("pp", [n_layers, 128, NP_], F32, "ExternalInput")
    cst_d = dram("cst", [128, NCST], F32, "ExternalInput")
    qT_s = dram("qT_s", [NH, 128, S], BF16, sk)
    kT_s = dram("kT_s", [NH, 128, S], BF16, sk)
    v_s = dram("v_s", [S, 1024], BF16, sk)
    sz_s = dram("sz_s", [S, 1024], F32, sk)
    dt_s = dram("dt_s", [S, 16], F32, sk)
    xs_s = dram("xs_s", [S, 1024], F32, sk)
    BT_s = dram("BT_s", [2, 128, S], BF16, sk)
    CT_s = dram("CT_s", [2, 128, S], BF16, sk)
    Bt_s = dram("Bt_s", [S, 256], BF16, sk)

    if debug:
        dbg_on = dram("dbg_on", [128, NH, S], BF16, sk)
        dbg_os = dram("dbg_os", [128, NH, S], BF16, sk)
    es = ExitStack()
    K = Sched(nc, es)

    uniq = [0]

    def sb(stack, name, shape, dt):
        uniq[0] += 1
        return stack.enter_context(nc.sbuf_tensor(f"sb{uniq[0]}_{name}", shape, dt))

    def ps(stack, name, shape, dt):
        uniq[0] += 1
        return stack.enter_context(nc.psum_tensor(f"ps{uniq[0]}_{name}", shape, dt))

    cst = sb(es, "cst", [128, NCST], F32)
    cbf = sb(es, "cbf", [128, 384], BF16)
    pp = sb(es, "pp", [128, NP_], F32)
    abc = sb(es, "abc", [128, 16], F32)
    epsb = sb(es, "epsb", [128, 2], F32)
    R_cst, R_cbf, R_pp, R_abc = K.res("cst"), K.res("cbf"), K.res("pp"), K.res("abc")
    ident = cst[:, C_ID:C_ID + 128]
    ones = cst[:, C_ONE:C_ONE + 128]
    ustr = cst[:, C_US:C_US + 128]
    tri = cst[:, C_TRI:C_TRI + 128]
    ident_bf = cbf[:, 0:128]
    nlge_bf = cbf[:, 128:256]
    nones_bf = cbf[:, 256:384]

    K.dma("sp", cst[:], cst_d[:, :], w=[R_cst], key="cst")
    K.op("act", lambda e: e.activation(out=cbf[:, 0:128], in_=cst[:, C_ID:C_ID + 128], func=AF.Copy),
         r=[R_cst], w=[R_cbf])
    K.op("act", lambda e: e.activation(out=cbf[:, 128:256], in_=cst[:, C_LGE:C_LGE + 128], func=AF.Copy, scale=-1.0),
         r=[R_cst], w=[R_cbf])
    K.op("act", lambda e: e.activation(out=cbf[:, 256:384], in_=cst[:, C_ONE:C_ONE + 128], func=AF.Copy, scale=-1.0),
         r=[R_cst], w=[R_cbf])
    K.op("dve", lambda e: e.memset(epsb[:, 0:1], EPS), w=[R_abc])
    K.op("dve", lambda e: e.memset(epsb[:, 1:2], math.log(DH ** -0.5)), w=[R_abc])
    K.flush()

    P = [ps(es, f"P{i}", [128, 512], F32) for i in range(8)]
    R_P = [K.res(f"P{i}") for i in range(8)]
    Pb = [p[:].bitcast(BF16) for p in P]

    def wview(flat, n):
        c = 8192 // n if n == 512 else 22
        return flat[:, 0:c * n].rearrange("p (c n) -> p c n", n=n)

    def emit_norm(stack, src, ntile, hT, R_hT, gcol, banks):
        xt = [sb(stack, f"n_xt{i}", [128, D], F32) for i in range(2)]
        R_xt = [K.res(f"n_xt{i}") for i in range(2)]
        xn = [sb(stack, f"n_xn{i}", [128, D], BF16) for i in range(2)]
        R_xn = [K.res(f"n_xn{i}") for i in range(2)]
        junk = sb(stack, "n_junk", [128, D], BF16)
        R_junk = K.res("n_junk")
        st = [sb(stack, f"n_st{i}", [128, 4], F32) for i in range(2)]
        R_st = [K.res(f"n_st{i}") for i in range(2)]
        for j in range(ntile):
            b = j % 2
            K.dma("sp", xt[b][:], src(j), w=[R_xt[b]], key=f"n_xt{b}")
            K.op("act", lambda e, b=b: e.activation(out=junk[:], in_=xt[b][:], func=AF.Square,
                                                     accum_out=st[b][:, 0:1]),
                 r=[R_xt[b]], w=[R_junk, R_st[b]])
            K.op("act", lambda e, b=b: e.activation(out=st[b][:, 1:2], in_=st[b][:, 0:1], func=AF.Ln,
                                                     scale=1.0 / D, bias=epsb[:, 0:1]),
                 r=[R_st[b]], w=[R_st[b]])
            K.op("act", lambda e, b=b: e.activation(out=st[b][:, 2:3], in_=st[b][:, 1:2], func=AF.Exp, scale=-0.5),
                 r=[R_st[b]], w=[R_st[b]])
            K.op("dve", lambda e, b=b: e.tensor_scalar(out=xn[b][:], in0=xt[b][:], scalar1=st[b][:, 2:3],
                                                       scalar2=None, op0=ALU.mult),
                 r=[R_xt[b], R_st[b]], w=[R_xn[b]])
            for c2 in range(2):
                pb = banks[(j * 2 + c2) % len(banks)]
                tpv = Pb[pb].rearrange("p (i t) -> p i t", t=128)

                def tr(e, b=b, c2=c2, tpv=tpv):
                    for i in range(8):
                        c = c2 * 8 + i
                        ins = e.transpose(tpv[:, i, :], xn[b][:, c * 128:(c + 1) * 128], ident_bf)
                    return ins
                K.op("pe", tr, r=[R_xn[b], R_cbf], w=[R_P[pb]])

                def ev(e, c2=c2, tpv=tpv, j=j):
                    for i in range(8):
                        c = c2 * 8 + i
                        ins = e.tensor_scalar(out=hT[:, c, j * 128:(j + 1) * 128], in0=tpv[:, i, :],
                                              scalar1=pp[:, gcol + c:gcol + c + 1], scalar2=None, op0=ALU.mult)
                    return ins
                K.op("dve" if c2 == 0 else "pool_no", ev, r=[R_P[pb], R_pp], w=[R_hT[j]]) if False else \
                    K.op("dve", ev, r=[R_P[pb], R_pp], w=[R_hT[j]])

    def phase_inproj(L, q, xsrc):
        with ExitStack() as st:
            hT = sb(st, "hT", [128, 16, S], BF16)
            R_hT = [K.res(f"hT{j}") for j in range(NT)]
            with ExitStack() as st2:
                emit_norm(st2, lambda j: xsrc[q, j * 128:(j + 1) * 128, :], NT, hT, R_hT, PP_NMIX, [6, 7])
                K.flush()
            wbf = [sb(st, f"wb{i}", [128, 8192], BF16) for i in range(3)]
            R_wb = [K.res(f"wb{i}") for i in range(3)]
            wdt = sb(st, "wdt", [128, 16, 16], BF16)
            R_wdt = K.res("wdt")
            w_l = w_in[L].rearrange("(c p) n -> p c n", p=128)
            vst = [sb(st, f"vst{i}", [128, 512], BF16) for i in range(2)]
            R_vst = [K.res(f"vst{i}") for i in range(2)]
            zst = [sb(st, f"zst{i}", [128, 512], F32) for i in range(2)]
            R_zst = [K.res(f"zst{i}") for i in range(2)]
            sqt = [sb(st, f"sqt{i}", [128, 512], BF16) for i in range(2)]
            R_sqt = [K.res(f"sqt{i}") for i in range(2)]
            rt = [sb(st, f"rt{i}", [128, 512], F32) for i in range(2)]
            R_rt = [K.res(f"rt{i}") for i in range(2)]
            qst = [sb(st, f"qst{i}", [128, 512], BF16) for i in range(2)]
            R_qst = [K.res(f"qst{i}") for i in range(2)]
            craw = [sb(st, f"craw{i}", [128, S + 3], F32) for i in range(2)]
            R_craw = [K.res(f"craw{i}") for i in range(2)]
            ct = [sb(st, f"ct{i}", [128, S], F32) for i in range(2)]
            R_ct = [K.res(f"ct{i}") for i in range(2)]
            xsT = [sb(st, f"xsT{i}", [128, S], F32) for i in range(1)]
            R_xsT = [K.res(f"xsT{i}") for i in range(1)]
            bcT = [sb(st, f"bcT{i}", [128, S], BF16) for i in range(2)]
            R_bcT = [K.res(f"bcT{i}") for i in range(2)]
            xst = sb(st, "xst", [128, 16, 128], F32)
            R_xst = K.res("xst")
            bst = sb(st, "bst", [128, 16, 128], BF16)
            R_bst = K.res("bst")
            dtw = sb(st, "dtw", [128, 3, 256], F32)
            R_dtw = K.res("dtw")
            for i in range(2):
                K.op("dve", lambda e, i=i: e.memset(craw[i][:, 0:3], 0.0), w=[R_craw[i]])

            K.dma("pool", wdt[:], w_l[:, :, 5632:5648], w=[R_wdt], key="wdt")
            pdt = P[4][:, 0:256].rearrange("p (t h) -> p t h", h=16)
            for tt in range(NT):
                def mm(e, tt=tt):
                    for c in range(16):
                        ins = e.matmul(pdt[:, tt, :], lhsT=hT[:, c, tt * 128:(tt + 1) * 128], rhs=wdt[:, c, :],
                                       start=(c == 0), stop=(c == 15))
                    return ins
                K.op("pe", mm, r=[R_hT[tt], R_wdt], w=[R_P[4]])
            dtv = [dtw[:, i, :].rearrange("p (t h) -> p t h", h=16) for i in range(3)]
            K.op("dve", lambda e: e.tensor_tensor(out=dtv[0], in0=pdt,
                                                  in1=pp[:, None, PP_DTB:PP_DTB + 16].to_broadcast([128, 16, 16]),
                                                  op=ALU.add), r=[R_P[4], R_pp], w=[R_dtw])
            K.op("act", lambda e: e.activation(out=dtw[:, 1, :], in_=dtw[:, 0, :], func=AF.Exp), r=[R_dtw], w=[R_dtw])
            K.op("act", lambda e: e.activation(out=dtw[:, 2, :], in_=dtw[:, 1, :], func=AF.Ln, bias=1.0),
                 r=[R_dtw], w=[R_dtw])
            K.dma("sp", dt_s.rearrange("(t p) h -> p t h", p=128), dtv[2], r=[R_dtw], key="dtw")

            tiles = [("v", 0), ("v", 1), ("z", 0), ("z", 1), ("q", 0), ("q", 1), ("k", 0), ("k", 1),
                     ("x", 0), ("x", 1), ("x", 2)]
            col0 = {"q": 0, "k": 1024, "v": 2048, "z": 3072, "x": 4096}
            state = {"u": 0, "deferred": None, "ev": 0}

            def run_deferred():
                if state["deferred"] is not None:
                    f = state["deferred"]
                    state["deferred"] = None
                    f()

            for ti, (kind, w) in enumerate(tiles):
                slot = ti % 3
                wv = wview(wbf[slot], 512)
                K.dma("pool", wv, w_l[:, :, col0[kind] + w * 512: col0[kind] + (w + 1) * 512],
                      w=[R_wb[slot]], key=f"wb{slot}")
                if kind in ("v", "z"):
                    for tt in range(NT):
                        u = state["u"]; state["u"] += 1
                        pa = u % 4

                        def mm(e, tt=tt, pa=pa, wv=wv):
                            for c in range(16):
                                ins = e.matmul(P[pa][:], lhsT=hT[:, c, tt * 128:(tt + 1) * 128], rhs=wv[:, c, :],
                                               start=(c == 0), stop=(c == 15))
                            return ins
                        K.op("pe", mm, r=[R_hT[tt], R_wb[slot]], w=[R_P[pa]])
                        run_deferred()
                        k2 = u % 2
                        if kind == "v":
                            K.op("act", lambda e, pa=pa, k2=k2: e.activation(out=vst[k2][:], in_=P[pa][:], func=AF.Copy),
                                 r=[R_P[pa]], w=[R_vst[k2]])
                            K.dma("sp", v_s[tt * 128:(tt + 1) * 128, w * 512:(w + 1) * 512], vst[k2][:],
                                  r=[R_vst[k2]], key=f"vst{k2}")
                        else:
                            K.op("act", lambda e, pa=pa, k2=k2: e.activation(out=zst[k2][:], in_=P[pa][:], func=AF.Silu),
                                 r=[R_P[pa]], w=[R_zst[k2]])
                            K.dma("sp", sz_s[tt * 128:(tt + 1) * 128, w * 512:(w + 1) * 512], zst[k2][:],
                                  r=[R_zst[k2]], key=f"zst{k2}")
                else:
                    for j in range(4):
                        cj = w * 4 + j
                        for tb in range(NB):
                            u = state["u"]; state["u"] += 1
                            pa = u % 4

                            def mm(e, j=j, tb=tb, pa=pa, wv=wv):
                                for c in range(16):
                                    ins = e.matmul(P[pa][:], lhsT=wv[:, c, j * 128:(j + 1) * 128],
                                                   rhs=hT[:, c, tb * 512:(tb + 1) * 512],
                                                   start=(c == 0), stop=(c == 15))
                                return ins
                            K.op("pe", mm, r=[R_hT[4 * tb + i] for i in range(4)] + [R_wb[slot]], w=[R_P[pa]])
                            run_deferred()
                            if kind in ("q", "k"):
                                k2 = state["ev"] % 2; state["ev"] += 1
                                K.op("act", lambda e, pa=pa, k2=k2: e.activation(out=sqt[k2][:], in_=P[pa][:],
                                                                                  func=AF.Square),
                                     r=[R_P[pa]], w=[R_sqt[k2]])

                                def rest(kind=kind, cj=cj, tb=tb, pa=pa, k2=k2):
                                    pn = 4 + k2
                                    K.op("pe", lambda e: e.matmul(P[pn][:], lhsT=nones_bf, rhs=sqt[k2][:],
                                                                  start=True, stop=True),
                                         r=[R_sqt[k2], R_cbf], w=[R_P[pn]])
                                    K.op("act", lambda e: e.activation(out=rt[k2][:], in_=P[pn][:], func=AF.Ln,
                                                                       scale=-1.0 / DH, bias=epsb[:, 0:1]),
                                         r=[R_P[pn]], w=[R_rt[k2]])
                                    if kind == "q":
                                        K.op("act", lambda e: e.activation(out=rt[k2][:], in_=rt[k2][:], func=AF.Exp,
                                                                           scale=-0.5, bias=epsb[:, 1:2]),
                                             r=[R_rt[k2]], w=[R_rt[k2]])
                                    else:
                                        K.op("act", lambda e: e.activation(out=rt[k2][:], in_=rt[k2][:], func=AF.Exp,
                                                                           scale=-0.5),
                                             r=[R_rt[k2]], w=[R_rt[k2]])
                                    gc = PP_QG if kind == "q" else PP_KG
                                    K.op("dve", lambda e: e.scalar_tensor_tensor(
                                        out=qst[k2][:], in0=P[pa][:], scalar=pp[:, gc:gc + 1], in1=rt[k2][:],
                                        op0=ALU.mult, op1=ALU.mult), r=[R_P[pa], R_rt[k2], R_pp], w=[R_qst[k2]])
                                    dst = (qT_s if kind == "q" else kT_s)[cj, :, tb * 512:(tb + 1) * 512]
                                    K.dma("sp", dst, qst[k2][:], r=[R_qst[k2]], key=f"qst{k2}")
                                state["deferred"] = rest
                            else:
                                cb = cj % 2
                                K.op("act", lambda e, pa=pa, cb=cb, tb=tb: e.activation(
                                    out=craw[cb][:, 3 + tb * 512: 3 + (tb + 1) * 512], in_=P[pa][:], func=AF.Copy),
                                    r=[R_P[pa]], w=[R_craw[cb]])
                                if tb == NB - 1:
                                    def conv(cj=cj, cb=cb):
                                        wc = PP_CW + cj * 4
                                        K.op("dve", lambda e: e.tensor_scalar(
                                            out=ct[cb][:], in0=craw[cb][:, 0:S], scalar1=pp[:, wc:wc + 1],
                                            scalar2=pp[:, PP_CB + cj:PP_CB + cj + 1], op0=ALU.mult, op1=ALU.add),
                                            r=[R_craw[cb], R_pp], w=[R_ct[cb]])
                                        for i in range(1, 4):
                                            K.op("dve", lambda e, i=i: e.scalar_tensor_tensor(
                                                out=ct[cb][:], in0=craw[cb][:, i:i + S], scalar=pp[:, wc + i:wc + i + 1],
                                                in1=ct[cb][:], op0=ALU.mult, op1=ALU.add),
                                                r=[R_craw[cb], R_pp, R_ct[cb]], w=[R_ct[cb]])
                                        if cj < 8:
                                            K.op("act", lambda e: e.activation(out=xsT[0][:], in_=ct[cb][:], func=AF.Silu),
                                                 r=[R_ct[cb]], w=[R_xsT[0]])
                                            for t4 in range(4):
                                                pb = 6 + (t4 % 2)
                                                tv = P[pb][:].rearrange("p (i t) -> p i t", t=128)

                                                def tr(e, t4=t4, tv=tv):
                                                    for i in range(4):
                                                        tt = t4 * 4 + i
                                                        ins = e.transpose(tv[:, i, :], xsT[0][:, tt * 128:(tt + 1) * 128],
                                                                          ident)
                                                    return ins
                                                K.op("pe", tr, r=[R_xsT[0], R_cst], w=[R_P[pb]])
                                                K.op("act", lambda e, t4=t4, tv=tv: e.activation(
                                                    out=xst[:, t4 * 4:(t4 + 1) * 4, :], in_=tv, func=AF.Copy),
                                                    r=[R_P[pb]], w=[R_xst])
                                            K.dma("sp", xs_s.rearrange("(t p) c -> p t c", p=128)[:, :, cj * 128:(cj + 1) * 128],
                                                  xst[:], r=[R_xst], key="xst")
                                        else:
                                            g = (cj - 8) % 2
                                            isB = cj < 10
                                            bb = cj % 2
                                            K.op("act", lambda e: e.activation(out=bcT[bb][:], in_=ct[cb][:], func=AF.Silu),
                                                 r=[R_ct[cb]], w=[R_bcT[bb]])
                                            K.dma("sp", (BT_s if isB else CT_s)[g], bcT[bb][:], r=[R_bcT[bb]], key=f"bcT{bb}")
                                            if isB:
                                                for t8 in range(2):
                                                    pb = 6 + t8
                                                    tv = Pb[pb].rearrange("p (i t) -> p i t", t=128)

                                                    def tr(e, t8=t8, tv=tv):
                                                        for i in range(8):
                                                            tt = t8 * 8 + i
                                                            ins = e.transpose(tv[:, i, :], bcT[bb][:, tt * 128:(tt + 1) * 128],
                                                                              ident_bf)
                                                        return ins
                                                    K.op("pe", tr, r=[R_bcT[bb], R_cbf], w=[R_P[pb]])
                                                    K.op("act", lambda e, t8=t8, tv=tv: e.activation(
                                                        out=bst[:, t8 * 8:(t8 + 1) * 8, :], in_=tv, func=AF.Copy),
                                                        r=[R_P[pb]], w=[R_bst])
                                                K.dma("sp", Bt_s.rearrange("(t p) c -> p t c", p=128)[:, :, g * 128:(g + 1) * 128],
                                                      bst[:], r=[R_bst], key="bst")
                                    state["deferred"] = conv
            run_deferred()
        K.flush()

    def phase_attn(L, q, onT, R_onT):
        with ExitStack() as st:
            oT = sb(st, "oT", [128, NH, S], F32)
            R_oT = [[K.res(f"oT{h}_{tb}") for tb in range(NB)] for h in range(NH)]
            qb = [sb(st, f"qb{i}", [128, S], BF16) for i in range(2)]
            kb_ = [sb(st, f"kb{i}", [128, S], BF16) for i in range(2)]
            vb = [sb(st, f"vb{i}", [128, NT, 128], BF16) for i in range(2)]
            R_qb = [K.res(f"qb{i}") for i in range(2)]
            R_kb = [K.res(f"kb{i}") for i in range(2)]
            R_vb = [K.res(f"vb{i}") for i in range(2)]
            et = [sb(st, f"et{i}", [128, 512], F32) for i in range(2)]
            R_et = [K.res(f"et{i}") for i in range(2)]
            spt = [sb(st, f"spt{i}", [128, 512], BF16) for i in range(4)]
            R_spt = [K.res(f"spt{i}") for i in range(4)]
            spf = [sb(st, f"spf{i}", [128, 512], F32) for i in range(2)]
            R_spf = [K.res(f"spf{i}") for i in range(2)]
            Wt = [sb(st, f"Wt{i}", [128, 512], BF16) for i in range(3)]
            R_Wt = [K.res(f"Wt{i}") for i in range(3)]
            Wf = [sb(st, f"Wf{i}", [128, 512], F32) for i in range(2)]
            R_Wf = [K.res(f"Wf{i}") for i in range(2)]
            acc = [sb(st, f"acc{i}", [128, 512], BF16) for i in range(2)]
            R_acc = [K.res(f"acc{i}") for i in range(2)]
            sqb = [sb(st, f"sqb{i}", [128, 512], BF16) for i in range(2)]
            R_sqb = [K.res(f"sqb{i}") for i in range(2)]
            rtn = sb(st, "rtn", [128, 512], F32)
            R_rtn = K.res("rtn")
            gctr = [0]
            vview = v_s.rearrange("(t p) c -> p t c", p=128)
            for h in range(NH):
                b = h % 2
                K.dma("sp", qb[b][:], qT_s[h], w=[R_qb[b]], key=f"qb{b}")
                K.dma("sp", kb_[b][:], kT_s[h], w=[R_kb[b]], key=f"kb{b}")
                K.dma("sp", vb[b][:], vview[:, :, h * 128:(h + 1) * 128], w=[R_vb[b]], key=f"vb{b}")
                def do_unit(h, tb, b):
                    unit = h * NB + tb
                    ob = 3 + unit % 2
                    kbs = list(range(4 * tb + 3, -1, -1))
                    n = len(kbs)
                    g0 = gctr[0]
                    gctr[0] += n
                    accsrc = {}

                    def mask_ap(o):
                        return cst[:, C_AM + (3 - o) * 128: C_AM + (3 - o) * 128 + 512]

                    def stageA(i):
                        g = g0 + i
                        kb = kbs[i]
                        zi = g % 3
                        diag = kb >= 4 * tb
                        o = kb - 4 * tb
                        K.op("pe", lambda e: e.matmul(P[zi][:], lhsT=kb_[b][:, kb * 128:(kb + 1) * 128],
                                                      rhs=qb[b][:, tb * 512:(tb + 1) * 512], start=True, stop=True),
                             r=[R_kb[b], R_qb[b]], w=[R_P[zi]])
                        K.op("act", lambda e: e.activation(out=et[g % 2][:], in_=P[zi][:], func=AF.Exp),
                             r=[R_P[zi]], w=[R_et[g % 2]])
                        if diag:
                            K.op("act", lambda e: e.activation(out=spf[g % 2][:], in_=et[g % 2][:], func=AF.Ln, bias=1.0),
                                 r=[R_et[g % 2]], w=[R_spf[g % 2]])
                            K.op("dve", lambda e: e.tensor_tensor(out=spt[g % 4][:], in0=spf[g % 2][:], in1=mask_ap(o),
                                                                  op=ALU.mult),
                                 r=[R_spf[g % 2], R_cst], w=[R_spt[g % 4]])
                        else:
                            K.op("act", lambda e: e.activation(out=spt[g % 4][:], in_=et[g % 2][:], func=AF.Ln, bias=1.0),
                                 r=[R_et[g % 2]], w=[R_spt[g % 4]])
                        if i == 1:
                            accsrc[1] = (spt[(g - 1) % 4], R_spt[(g - 1) % 4])
                        elif i >= 2:
                            prev, R_prev = accsrc[i - 1]
                            a = i % 2
                            K.op("pool", lambda e: e.tensor_tensor(out=acc[a][:], in0=prev[:], in1=spt[(g - 1) % 4][:],
                                                                   op=ALU.add),
                                 r=[R_prev, R_spt[(g - 1) % 4]], w=[R_acc[a]])
                            accsrc[i] = (acc[a], R_acc[a])

                    def stageB(i):
                        g = g0 + i
                        kb = kbs[i]
                        zi = g % 3
                        diag = kb >= 4 * tb
                        o = kb - 4 * tb

                        def mm(e):
                            ins = e.matmul(P[zi][:], lhsT=nlge_bf, rhs=spt[g % 4][:], start=False, stop=(i == 0),
                                           skip_group_check=True)
                            if i >= 1:
                                ins = e.matmul(P[zi][:], lhsT=nones_bf, rhs=accsrc[i][0][:], start=False, stop=True,
                                               skip_group_check=True)
                            return ins
                        rr = [R_spt[g % 4], R_cbf] + ([accsrc[i][1]] if i >= 1 else [])
                        K.op("pe", mm, r=rr, w=[R_P[zi]])
                        if diag:
                            K.op("act", lambda e: e.activation(out=Wf[g % 2][:], in_=P[zi][:], func=AF.Exp),
                                 r=[R_P[zi]], w=[R_Wf[g % 2]])
                            K.op("dve", lambda e: e.tensor_tensor(out=Wt[g % 3][:], in0=Wf[g % 2][:], in1=mask_ap(o),
                                                                  op=ALU.mult),
                                 r=[R_Wf[g % 2], R_cst], w=[R_Wt[g % 3]])
                        else:
                            K.op("act", lambda e: e.activation(out=Wt[g % 3][:], in_=P[zi][:], func=AF.Exp),
                                 r=[R_P[zi]], w=[R_Wt[g % 3]])

                    def stageC(i):
                        g = g0 + i
                        kb = kbs[i]
                        K.op("pe", lambda e: e.matmul(P[ob][:], lhsT=vb[b][:, kb, :], rhs=Wt[g % 3][:],
                                                      start=(i == 0), stop=(i == n - 1), skip_group_check=True),
                             r=[R_vb[b], R_Wt[g % 3]], w=[R_P[ob]])
                        if i == n - 1:
                            K.op("dve", lambda e: e.tensor_copy(out=oT[:, h, tb * 512:(tb + 1) * 512], in_=P[ob][:]),
                                 r=[R_P[ob]], w=[R_oT[h][tb]])

                    for step in range(n + 2):
                        if step < n:
                            stageA(step)
                        if 1 <= step <= n:
                            stageB(step - 1)
                        if step >= 2:
                            stageC(step - 2)
                for tb in range(NB):
                    do_unit(h, tb, b)
            def do_norm(tb):
                pn = 5 + tb % 2
                for h in range(NH):
                    k2 = (tb * NH + h) % 2
                    K.op("act", lambda e, h=h, k2=k2: e.activation(out=sqb[k2][:], in_=oT[:, h, tb * 512:(tb + 1) * 512],
                                                                    func=AF.Square),
                         r=[R_oT[h][tb]], w=[R_sqb[k2]])
                    K.op("pe", lambda e, h=h, k2=k2: e.matmul(P[pn][:], lhsT=nones_bf, rhs=sqb[k2][:], start=(h == 0),
                                                               stop=(h == NH - 1), skip_group_check=True),
                         r=[R_sqb[k2], R_cbf], w=[R_P[pn]])
                K.op("act", lambda e: e.activation(out=rtn[:], in_=P[pn][:], func=AF.Ln, scale=-1.0 / 1024,
                                                   bias=epsb[:, 0:1]), r=[R_P[pn]], w=[R_rtn])
                K.op("act", lambda e: e.activation(out=rtn[:], in_=rtn[:], func=AF.Exp, scale=-0.5),
                     r=[R_rtn], w=[R_rtn])
                for h in range(NH):
                    K.op("dve", lambda e, h=h: e.scalar_tensor_tensor(
                        out=onT[:, h, tb * 512:(tb + 1) * 512], in0=oT[:, h, tb * 512:(tb + 1) * 512],
                        scalar=pp[:, PP_AOG + h:PP_AOG + h + 1], in1=rtn[:], op0=ALU.mult, op1=ALU.mult),
                        r=[R_oT[h][tb], R_rtn, R_pp], w=[R_onT])
            for tb in range(NB):
                do_norm(tb)
        K.flush()

    def phase_ssd(L, q, osT, R_osT):
        with ExitStack() as st:
            def dbl(name, shape, dt):
                return ([sb(st, f"{name}{i}", shape, dt) for i in range(2)], [K.res(f"{name}{i}") for i in range(2)])
            xs, R_xs = dbl("s_xs", [128, 1024], F32)
            szt, R_sz = dbl("s_sz", [128, 1024], F32)
            dtt, R_dt = dbl("s_dt", [128, 16], F32)
            BT, R_BT = dbl("s_BT", [128, 2, 128], BF16)
            CT, R_CT = dbl("s_CT", [128, 2, 128], BF16)
            Bt, R_Bt = dbl("s_Bt", [128, 256], BF16)
            da, R_da = dbl("s_da", [128, 16], F32)
            ex, R_ex = dbl("s_ex", [128, 64], F32)
            Dm, R_Dm = dbl("s_Dm", [128, 16, 128], F32)
            dec, R_dec = dbl("s_dec", [128, 16, 128], F32)
            cbm, R_cbm = dbl("s_cbm", [128, 2, 128], F32)
            MT, R_MT = dbl("s_MT", [128, 16, 128], BF16)
            xdt, R_xdt = dbl("s_xdt", [128, 16, 64], BF16)
            xde, R_xde = dbl("s_xde", [128, 16, 64], BF16)
            hSb, R_hSb = dbl("s_hSb", [128, 1024], BF16)
            t1, R_t1 = dbl("s_t1", [128, 1024], F32)
            t2, R_t2 = dbl("s_t2", [128, 1024], F32)
            yz, R_yz = dbl("s_yz", [128, 1024], F32)
            osm, R_osm = dbl("s_osm", [128, 1024], BF16)
            ssq, R_ssq = dbl("s_ssq", [128, 8], F32)
            hS = sb(st, "s_hS", [128, 1024], F32)
            R_hS = K.res("s_hS")
            junk = sb(st, "s_junk", [128, 512], BF16)
            R_junk = K.res("s_junk")
            K.op("dve", lambda e: e.memset(hS[:], 0.0), w=[R_hS])
            K.op("dve", lambda e: e.memset(hSb[0][:], 0.0), w=[R_hSb[0]])
            xs_v = xs_s
            for c in range(NT):
                b = c % 2
                tok = slice(c * 128, (c + 1) * 128)
                K.dma("sp", xs[b][:], xs_s[tok, :], w=[R_xs[b]], key=f"s_xs{b}")
                K.dma("sp", szt[b][:], sz_s[tok, :], w=[R_sz[b]], key=f"s_sz{b}")
                K.dma("sp", dtt[b][:], dt_s[tok, :], w=[R_dt[b]], key=f"s_dt{b}")
                K.dma("sp", BT[b][:], BT_s[:, :, tok].rearrange("g n t -> n g t"), w=[R_BT[b]], key=f"s_BT{b}")
                K.dma("sp", CT[b][:], CT_s[:, :, tok].rearrange("g n t -> n g t"), w=[R_CT[b]], key=f"s_CT{b}")
                K.dma("sp", Bt[b][:], Bt_s[tok, :], w=[R_Bt[b]], key=f"s_Bt{b}")
                K.op("dve", lambda e, b=b: e.tensor_tensor(out=da[b][:], in0=dtt[b][:], in1=abc[:], op=ALU.mult),
                     r=[R_dt[b], R_abc], w=[R_da[b]])

                def mm(e, b=b):
                    e.matmul(P[0][:, 0:16], lhsT=ustr, rhs=da[b][:], start=True, stop=True)
                    e.matmul(P[0][:, 16:32], lhsT=tri, rhs=da[b][:], start=True, stop=True)
                    return e.matmul(P[0][:, 32:48], lhsT=ones, rhs=da[b][:], start=True, stop=True)
                K.op("pe", mm, r=[R_da[b], R_cst], w=[R_P[0]])
                K.op("act", lambda e, b=b: e.activation(out=ex[b][:, 0:48], in_=P[0][:, 0:48], func=AF.Exp),
                     r=[R_P[0]], w=[R_ex[b]])
                K.op("dve", lambda e, b=b: e.tensor_tensor(out=ex[b][:, 48:64], in0=ex[b][:, 0:16], in1=dtt[b][:],
                                                           op=ALU.mult), r=[R_ex[b], R_dt[b]], w=[R_ex[b]])
                K.op("dve", lambda e, b=b: e.tensor_tensor(out=Dm[b][:], in0=ustr[:, None, :].to_broadcast([128, 16, 128]),
                                                           in1=da[b][:, :, None].to_broadcast([128, 16, 128]), op=ALU.mult),
                     r=[R_cst, R_da[b]], w=[R_Dm[b]])
                for k in range(4):
                    pk = 1 + k
                    sv = P[pk][:].rearrange("p (h l) -> p h l", l=128)

                    def mm(e, b=b, k=k, sv=sv):
                        for i in range(4):
                            ins = e.matmul(sv[:, i, :], lhsT=Dm[b][:, 4 * k + i, :], rhs=tri, start=True, stop=True)
                        return ins
                    K.op("pe", mm, r=[R_Dm[b], R_cst], w=[R_P[pk]])
                    K.op("act", lambda e, b=b, k=k, sv=sv: e.activation(out=dec[b][:, 4 * k:4 * k + 4, :], in_=sv, func=AF.Exp),
                         r=[R_P[pk]], w=[R_dec[b]])
                cv = P[0][:, 128:384].rearrange("p (g l) -> p g l", l=128)

                def mm(e, b=b, cv=cv):
                    e.matmul(cv[:, 0, :], lhsT=BT[b][:, 0, :], rhs=CT[b][:, 0, :], start=True, stop=True)
                    return e.matmul(cv[:, 1, :], lhsT=BT[b][:, 1, :], rhs=CT[b][:, 1, :], start=True, stop=True)
                K.op("pe", mm, r=[R_BT[b], R_CT[b]], w=[R_P[0]])
                K.op("dve", lambda e, b=b, cv=cv: e.tensor_tensor(out=cbm[b][:], in0=cv,
                                                                  in1=tri[:, None, :].to_broadcast([128, 2, 128]), op=ALU.mult),
                     r=[R_P[0], R_cst], w=[R_cbm[b]])
                for g in range(2):
                    K.op("dve", lambda e, b=b, g=g: e.tensor_tensor(
                        out=MT[b][:, 8 * g:8 * g + 8, :], in0=dec[b][:, 8 * g:8 * g + 8, :],
                        in1=cbm[b][:, g:g + 1, :].to_broadcast([128, 8, 128]), op=ALU.mult),
                        r=[R_dec[b], R_cbm[b]], w=[R_MT[b]])
                xs3 = xs[b][:].rearrange("p (h d) -> p h d", d=64)
                K.op("dve", lambda e, b=b, xs3=xs3: e.tensor_tensor(
                    out=xdt[b][:], in0=xs3, in1=dtt[b][:, :, None].to_broadcast([128, 16, 64]), op=ALU.mult),
                    r=[R_xs[b], R_dt[b]], w=[R_xdt[b]])
                K.op("pool", lambda e, b=b, xs3=xs3: e.tensor_tensor(
                    out=xde[b][:], in0=xs3, in1=ex[b][:, 48:64, None].to_broadcast([128, 16, 64]), op=ALU.mult),
                    r=[R_xs[b], R_ex[b]], w=[R_xde[b]])
                for g in range(2):
                    yv = P[1 + g][:].rearrange("p (h d) -> p h d", d=64)

                    def mm(e, b=b, g=g, yv=yv):
                        for i in range(8):
                            ins = e.matmul(yv[:, i, :], lhsT=MT[b][:, 8 * g + i, :], rhs=xdt[b][:, 8 * g + i, :],
                                           start=True, stop=True)
                        return ins
                    K.op("pe", mm, r=[R_MT[b], R_xdt[b]], w=[R_P[1 + g]])
                    K.op("pe", lambda e, b=b, g=g: e.matmul(P[3 + g][:], lhsT=CT[b][:, g, :],
                                                            rhs=hSb[b][:, g * 512:(g + 1) * 512], start=True, stop=True),
                         r=[R_CT[b], R_hSb[b]], w=[R_P[3 + g]])
                    K.op("pe", lambda e, b=b, g=g: e.matmul(
                        P[5 + g][:], lhsT=Bt[b][:, g * 128:(g + 1) * 128],
                        rhs=xde[b][:, 8 * g:8 * g + 8, :].rearrange("p h d -> p (h d)"), start=True, stop=True),
                        r=[R_Bt[b], R_xde[b]], w=[R_P[5 + g]])
                for g in range(2):
                    gs = slice(g * 512, (g + 1) * 512)
                    K.op("dve", lambda e, b=b, g=g, gs=gs: e.tensor_tensor(
                        out=t1[b][:, gs].rearrange("p (h d) -> p h d", d=64),
                        in0=P[3 + g][:].rearrange("p (h d) -> p h d", d=64),
                        in1=ex[b][:, 16 + 8 * g:16 + 8 * g + 8, None].to_broadcast([128, 8, 64]), op=ALU.mult),
                        r=[R_P[3 + g], R_ex[b]], w=[R_t1[b]])
                    K.op("pool", lambda e, b=b, g=g, gs=gs: e.tensor_tensor(
                        out=t2[b][:, gs].rearrange("p (h d) -> p h d", d=64),
                        in0=xs[b][:, gs].rearrange("p (h d) -> p h d", d=64),
                        in1=pp[:, PP_DSK + 8 * g:PP_DSK + 8 * g + 8, None].to_broadcast([128, 8, 64]), op=ALU.mult),
                        r=[R_xs[b], R_pp], w=[R_t2[b]])
                    K.op("pool", lambda e, b=b, gs=gs: e.tensor_tensor(out=t2[b][:, gs], in0=t2[b][:, gs], in1=t1[b][:, gs],
                                                                       op=ALU.add),
                         r=[R_t2[b], R_t1[b]], w=[R_t2[b]])
                    K.op("dve", lambda e, b=b, g=g, gs=gs: e.tensor_tensor(out=t1[b][:, gs], in0=t2[b][:, gs], in1=P[1 + g][:],
                                                                           op=ALU.add),
                         r=[R_t2[b], R_P[1 + g]], w=[R_t1[b]])
                    K.op("pool", lambda e, b=b, gs=gs: e.tensor_tensor(out=yz[b][:, gs], in0=t1[b][:, gs], in1=szt[b][:, gs],
                                                                       op=ALU.mult),
                         r=[R_t1[b], R_sz[b]], w=[R_yz[b]])
                    K.op("act", lambda e, b=b, g=g, gs=gs: e.activation(out=junk[:], in_=yz[b][:, gs], func=AF.Square,
                                                                        accum_out=ssq[b][:, g:g + 1]),
                         r=[R_yz[b]], w=[R_junk, R_ssq[b]])
                K.op("act", lambda e, b=b: e.activation(out=ssq[b][:, 2:4], in_=ssq[b][:, 0:2], func=AF.Ln, scale=1.0 / 512,
                                                        bias=epsb[:, 0:1]), r=[R_ssq[b]], w=[R_ssq[b]])
                K.op("act", lambda e, b=b: e.activation(out=ssq[b][:, 4:6], in_=ssq[b][:, 2:4], func=AF.Exp, scale=-0.5),
                     r=[R_ssq[b]], w=[R_ssq[b]])
                for g in range(2):
                    gs = slice(g * 512, (g + 1) * 512)
                    K.op("dve", lambda e, b=b, g=g, gs=gs: e.scalar_tensor_tensor(
                        out=osm[b][:, gs], in0=yz[b][:, gs], scalar=ssq[b][:, 4 + g:5 + g],
                        in1=pp[:, PP_SOG + g * 512:PP_SOG + (g + 1) * 512], op0=ALU.mult, op1=ALU.mult),
                        r=[R_yz[b], R_ssq[b], R_pp], w=[R_osm[b]])
                tv = Pb[7].rearrange("p (i t) -> p i t", t=128)

                def tr(e, b=b, tv=tv):
                    for i in range(8):
                        ins = e.transpose(tv[:, i, :], osm[b][:, i * 128:(i + 1) * 128], ident_bf)
                    return ins
                K.op("pe", tr, r=[R_osm[b], R_cbf], w=[R_P[7]])
                K.op("act", lambda e, c=c, tv=tv: e.activation(out=osT[:, :, c * 128:(c + 1) * 128], in_=tv, func=AF.Copy),
                     r=[R_P[7]], w=[R_osT])
                K.op("dve", lambda e, b=b: e.tensor_tensor(
                    out=hS[:].rearrange("p (h d) -> p h d", d=64), in0=hS[:].rearrange("p (h d) -> p h d", d=64),
                    in1=ex[b][:, 32:48, None].to_broadcast([128, 16, 64]), op=ALU.mult),
                    r=[R_hS, R_ex[b]], w=[R_hS])
                for g in range(2):
                    gs = slice(g * 512, (g + 1) * 512)
                    K.op("dve", lambda e, g=g, gs=gs: e.tensor_tensor(out=hS[:, gs], in0=hS[:, gs], in1=P[5 + g][:], op=ALU.add),
                         r=[R_hS, R_P[5 + g]], w=[R_hS])
                K.op("act", lambda e, b=b: e.activation(out=hSb[1 - b][:], in_=hS[:], func=AF.Copy),
                     r=[R_hS], w=[R_hSb[1 - b]])
        K.flush()

    def phase_outproj(L, q, xsrc, onT, R_onT, osT, R_osT):
        with ExitStack() as st:
            wbf = [sb(st, f"wb{i}", [128, 8192], BF16) for i in range(3)]
            R_wb = [K.res(f"wb{i}") for i in range(3)]
            xr = [sb(st, f"xr{i}", [128, 512], F32) for i in range(4)]
            R_xr = [K.res(f"xr{i}") for i in range(4)]
            xo = [sb(st, f"xo{i}", [128, 512], F32) for i in range(3)]
            R_xo = [K.res(f"xo{i}") for i in range(3)]
            w_l = w_out[L].rearrange("(c p) n -> p c n", p=128)
            units = [(db, tt) for db in range(4) for tt in range(NT)]

            def load(u):
                db, tt = units[u]
                K.dma("sp", xr[u % 4][:], xsrc[q, tt * 128:(tt + 1) * 128, db * 512:(db + 1) * 512],
                      w=[R_xr[u % 4]], key=f"xr{u % 4}")
            load(0)
            load(1)
            for u, (db, tt) in enumerate(units):
                slot = db % 3
                wv = wview(wbf[slot], 512)
                if tt == 0:
                    K.dma("pool", wv, w_l[:, :, db * 512:(db + 1) * 512], w=[R_wb[slot]], key=f"wb{slot}")
                if u + 2 < len(units):
                    load(u + 2)
                pa = u % 4

                def mm(e, tt=tt, pa=pa, wv=wv):
                    for c in range(16):
                        src = onT if c < 8 else osT
                        ins = e.matmul(P[pa][:], lhsT=src[:, c % 8, tt * 128:(tt + 1) * 128], rhs=wv[:, c, :],
                                       start=(c == 0), stop=(c == 15))
                    return ins
                K.op("pe", mm, r=[R_onT, R_osT, R_wb[slot]], w=[R_P[pa]])
                K.op("dve", lambda e, u=u, pa=pa: e.tensor_tensor(out=xo[u % 3][:], in0=xr[u % 4][:], in1=P[pa][:], op=ALU.add),
                     r=[R_xr[u % 4], R_P[pa]], w=[R_xo[u % 3]])
                K.dma("sp", y[q, tt * 128:(tt + 1) * 128, db * 512:(db + 1) * 512], xo[u % 3][:],
                      r=[R_xo[u % 3]], key=f"xo{u % 3}")
        K.flush()

    def phase_ffn(L, q):
        TBN = 1024
        for tbk in range(S // TBN):
            t0 = tbk * TBN
            with ExitStack() as st:
                hT = sb(st, "f_hT", [128, 16, TBN], BF16)
                R_hT = [K.res(f"f_hT{j}") for j in range(TBN // 128)]
                with ExitStack() as st2:
                    emit_norm(st2, lambda j: y[q, t0 + j * 128: t0 + (j + 1) * 128, :], TBN // 128, hT, R_hT, PP_NFFN, [6, 7])
                    K.flush()
                actT = sb(st, "f_act", [128, NFC, TBN], BF16)
                R_act = [K.res(f"f_act{i}") for i in range(11)]
                wbf = [sb(st, f"f_wb{i}", [128, 8192], BF16) for i in range(4)]
                R_wb = [K.res(f"f_wb{i}") for i in range(4)]
                sgt = [sb(st, f"f_sg{i}", [128, 512], F32) for i in range(2)]
                R_sg = [K.res(f"f_sg{i}") for i in range(2)]
                xr = [sb(st, f"f_xr{i}", [128, 256], F32) for i in range(4)]
                R_xr = [K.res(f"f_xr{i}") for i in range(4)]
                xo = [sb(st, f"f_xo{i}", [128, 256], F32) for i in range(3)]
                R_xo = [K.res(f"f_xo{i}") for i in range(3)]
                wg_l = w_gate[L].rearrange("(c p) n -> p c n", p=128)
                wu_l = w_up[L].rearrange("(c p) n -> p c n", p=128)
                wd_l = w_down[L].rearrange("(c p) n -> p c n", p=128)
                u = 0
                for fg in range(11):
                    sg_, su_ = fg % 2, 2 + fg % 2
                    wg = wview(wbf[sg_], 512)
                    wu = wview(wbf[su_], 512)
                    K.dma("pool", wg, wg_l[:, :, fg * 512:(fg + 1) * 512], w=[R_wb[sg_]], key=f"f_wb{sg_}")
                    K.dma("pool", wu, wu_l[:, :, fg * 512:(fg + 1) * 512], w=[R_wb[su_]], key=f"f_wb{su_}")
                    for j in range(4):
                        for ts in range(TBN // 512):
                            pg, pu = u % 3, 3 + u % 3
                            k2 = u % 2
                            u += 1

                            def mm(e, j=j, ts=ts, pg=pg, pu=pu, wg=wg, wu=wu):
                                for c in range(16):
                                    e.matmul(P[pg][:], lhsT=wg[:, c, j * 128:(j + 1) * 128],
                                             rhs=hT[:, c, ts * 512:(ts + 1) * 512], start=(c == 0), stop=(c == 15))
                                for c in range(16):
                                    ins = e.matmul(P[pu][:], lhsT=wu[:, c, j * 128:(j + 1) * 128],
                                                   rhs=hT[:, c, ts * 512:(ts + 1) * 512], start=(c == 0), stop=(c == 15))
                                return ins
                            K.op("pe", mm, r=[R_hT[4 * ts + i] for i in range(4)] + [R_wb[sg_], R_wb[su_]],
                                 w=[R_P[pg], R_P[pu]])
                            K.op("act", lambda e, pg=pg, k2=k2: e.activation(out=sgt[k2][:], in_=P[pg][:], func=AF.Silu),
                                 r=[R_P[pg]], w=[R_sg[k2]])
                            K.op("dve", lambda e, fg=fg, j=j, ts=ts, pu=pu, k2=k2: e.tensor_tensor(
                                out=actT[:, fg * 4 + j, ts * 512:(ts + 1) * 512], in0=sgt[k2][:], in1=P[pu][:], op=ALU.mult),
                                r=[R_sg[k2], R_P[pu]], w=[R_act[fg]])
                units = [(db, tt) for db in range(8) for tt in range(TBN // 128)]

                def load(i):
                    db, tt = units[i]
                    K.dma("sp", xr[i % 4][:], y[q, t0 + tt * 128: t0 + (tt + 1) * 128, db * 256:(db + 1) * 256],
                          w=[R_xr[i % 4]], key=f"f_xr{i % 4}")
                load(0)
                load(1)
                for i, (db, tt) in enumerate(units):
                    s0, s1 = (2 * db) % 4, (2 * db + 1) % 4
                    wd0 = wview(wbf[s0], 256)
                    wd1 = wview(wbf[s1], 256)
                    if tt == 0:
                        K.dma("pool", wd0, wd_l[:, 0:22, db * 256:(db + 1) * 256], w=[R_wb[s0]], key=f"f_wb{s0}")
                        K.dma("pool", wd1, wd_l[:, 22:44, db * 256:(db + 1) * 256], w=[R_wb[s1]], key=f"f_wb{s1}")
                    if i + 2 < len(units):
                        load(i + 2)
                    pa = i % 4

                    def mm(e, tt=tt, pa=pa, wd0=wd0, wd1=wd1):
                        for f in range(NFC):
                            wd = wd0 if f < 22 else wd1
                            ins = e.matmul(P[pa][:, 0:256], lhsT=actT[:, f, tt * 128:(tt + 1) * 128], rhs=wd[:, f % 22, :],
                                           start=(f == 0), stop=(f == NFC - 1))
                        return ins
                    K.op("pe", mm, r=R_act + [R_wb[s0], R_wb[s1]], w=[R_P[pa]])
                    K.op("dve", lambda e, i=i, pa=pa: e.tensor_tensor(out=xo[i % 3][:], in0=xr[i % 4][:], in1=P[pa][:, 0:256],
                                                                      op=ALU.add),
                         r=[R_xr[i % 4], R_P[pa]], w=[R_xo[i % 3]])
                    K.dma("sp", y[q, t0 + tt * 128: t0 + (tt + 1) * 128, db * 256:(db + 1) * 256], xo[i % 3][:],
                          r=[R_xo[i % 3]], key=f"f_xo{i % 3}")
            K.flush()

    for L in range(n_layers):
        K.dma("sp", pp[:], pp_d[L], w=[R_pp], key="pp")
        K.op("act", lambda e: e.activation(out=abc[:], in_=pp[:, PP_ALOG:PP_ALOG + 16], func=AF.Exp),
             r=[R_pp], w=[R_abc])
        K.op("dve", lambda e: e.tensor_scalar(out=abc[:], in0=abc[:], scalar1=-1.0, scalar2=None, op0=ALU.mult),
             r=[R_abc], w=[R_abc])
        K.flush()
        for q in range(n_seq):
            xsrc = x_in if L == 0 else y
            if "inproj" in phases:
                phase_inproj(L, q, xsrc)
            with ExitStack() as mst:
                onT = sb(mst, "onT", [128, NH, S], BF16)
                osT = sb(mst, "osT", [128, NH, S], BF16)
                R_onT, R_osT = K.res("onT"), K.res("osT")
                if "attn" in phases:
                    phase_attn(L, q, onT, R_onT)
                if "ssd" in phases:
                    phase_ssd(L, q, osT, R_osT)
                if debug:
                    K.dma("sp", dbg_on, onT[:], r=[R_onT], key="dbg_on")
                    K.dma("sp", dbg_os, osT[:], r=[R_osT], key="dbg_os")
                    K.flush()
                if "outproj" in phases:
                    phase_outproj(L, q, xsrc, onT, R_onT, osT, R_osT)
            if "ffn" in phases:
                phase_ffn(L, q)
    K.flush(final=True)
    es.close()
    return nc, K


def make_consts():
    p = np.arange(128)[:, None]
    f = np.arange(128)[None, :]
    c = np.zeros((128, NCST), np.float32)
    c[:, C_ID:C_ID + 128] = (p == f)
    c[:, C_ONE:C_ONE + 128] = 1.0
    c[:, C_US:C_US + 128] = (p > f)
    c[:, C_TRI:C_TRI + 128] = (p <= f)
    c[:, C_LGE:C_LGE + 128] = (p >= f)
    fa = np.arange(896)[None, :]
    c[:, C_AM:C_AM + 896] = (fa - p > 384)
    return c


def pack_params(inp, n_layers):
    pp = np.zeros((n_layers, 128, NP_), np.float32)
    for l in range(n_layers):
        pp[l, :, PP_NMIX:PP_NMIX + 16] = inp["norm_mix"][l].reshape(16, 128).T
        pp[l, :, PP_NFFN:PP_NFFN + 16] = inp["norm_ffn"][l].reshape(16, 128).T
        pp[l, :, PP_QG] = inp["q_gain"][l]
        pp[l, :, PP_KG] = inp["k_gain"][l]
        cw = inp["conv_w"][l].reshape(4, 12, 128)
        pp[l, :, PP_CW:PP_CW + 48] = cw.transpose(2, 1, 0).reshape(128, 48)
        pp[l, :, PP_CB:PP_CB + 12] = inp["conv_b"][l].reshape(12, 128).T
        pp[l, :, PP_DTB:PP_DTB + 16] = np.broadcast_to(inp["dt_bias"][l][None, :], (128, 16))
        pp[l, :, PP_ALOG:PP_ALOG + 16] = np.broadcast_to(inp["a_log"][l][None, :], (128, 16))
        pp[l, :, PP_DSK:PP_DSK + 16] = np.broadcast_to(inp["d_skip"][l][None, :], (128, 16))
        pp[l, :, PP_AOG:PP_AOG + 8] = inp["attn_out_gain"][l].reshape(8, 128).T
        pp[l, :, PP_SOG:PP_SOG + 1024] = np.broadcast_to(inp["ssm_out_gain"][l][None, :], (128, 1024))
    return pp


_CACHE = {}
N_LAUNCH_LAYERS = 4


def kernel(**inputs):
    inp = {k: np.asarray(v) for k, v in inputs.items()}
    x = np.ascontiguousarray(inp["x"], dtype=np.float32)
    n_cores = 8
    n_seq = x.shape[0] // n_cores
    depth = inp["w_in"].shape[0]
    lpl = N_LAUNCH_LAYERS
    key = (lpl, n_seq)
    if key not in _CACHE:
        _CACHE[key] = build(lpl, n_seq)[0]
    nc = _CACHE[key]
    cst = make_consts()
    pp = pack_params(inp, depth)
    cur = x
    for l0 in range(0, depth, lpl):
        sl = slice(l0, l0 + lpl)
        shared = {"pp": np.ascontiguousarray(pp[sl]), "cst": cst}
        for l in range(lpl):
            for nm in ("w_in", "w_out", "w_gate", "w_up", "w_down"):
                shared[f"{nm}{l}"] = np.ascontiguousarray(inp[nm][l0 + l], dtype=np.float32)
        in_maps = [dict(shared, x=np.ascontiguousarray(cur[c * n_seq:(c + 1) * n_seq])) for c in range(n_cores)]
        res = run_bass_kernel_spmd(nc, in_maps, core_ids=list(range(n_cores)))
        cur = np.concatenate([np.asarray(r["y"]) for r in res.results], axis=0).astype(np.float32)
    return cur
```

```python
from contextlib import ExitStack
import math
import numpy as np
import concourse.bass as bass
import concourse.mybir as mybir
from concourse.bass_utils import run_bass_kernel_spmd

F32, BF16 = mybir.dt.float32, mybir.dt.bfloat16
AF = mybir.ActivationFunctionType
ALU = mybir.AluOpType

S = 2048
D = 2048
NH = 8
DH = 128
DFF = 5632
IN_DIM = 5648
EPS = 1e-6
NT = S // 128
NB = S // 512
NFC = DFF // 128
PP_NMIX, PP_NFFN, PP_QG, PP_KG, PP_CW, PP_CB, PP_DTB, PP_ALOG, PP_DSK, PP_AOG, PP_SOG, NP_ = \
    0, 16, 32, 33, 34, 82, 94, 110, 126, 142, 150, 1174
C_ID, C_ONE, C_US, C_TRI, C_LGE, C_AM, NCST = 0, 128, 256, 384, 512, 640, 1536

ENG = ["pe", "act", "dve", "pool", "sp"]


class Res:
    __slots__ = ("name", "lw", "rd")

    def __init__(self, name):
        self.name = name
        self.lw = None
        self.rd = []


class Op:
    __slots__ = ("eng", "fn", "kind", "key", "need", "val", "deps")


class Sched:
    def __init__(self, nc, es):
        self.nc = nc
        self.sem = {e: es.enter_context(nc.semaphore("s_" + e)) for e in ENG}
        self.cnt = {e: 0 for e in ENG}
        self.seen = {e: {} for e in ENG}
        self.dsem = {}
        self.dcnt = {}
        self.es = es
        self.ops = {e: [] for e in ENG}
        self.pending = {e: [] for e in ENG}
        self.allres = []
        self.dirty = {}
        self.n_inst = 0

    def res(self, name):
        r = Res(name)
        self.allres.append(r)
        return r

    def _mk(self, eng, fn, r, w, kind, key):
        o = Op()
        o.eng, o.fn, o.kind, o.key, o.need, o.val = eng, fn, kind, key, False, None
        deps = {}
        for d in self.pending[eng]:
            deps[d] = True
        self.pending[eng] = []
        for x in r:
            if x.lw is not None:
                deps[x.lw] = True
        for x in w:
            if x.lw is not None and x.lw not in deps:
                deps[x.lw] = False
            for q in x.rd:
                if q not in deps:
                    deps[q] = False
        keep = []
        for d, raw in deps.items():
            if d is o:
                continue
            if kind == "c" and d.kind == "c" and d.eng == eng and not raw:
                continue
            keep.append(d)
            if d.kind != "d":
                d.need = True
        o.deps = keep
        for x in r:
            if x not in w:
                x.rd = [q for q in x.rd if not (q.eng == eng and q.kind == kind and q.key == key)]
                x.rd.append(o)
        for x in w:
            x.lw = o
            x.rd = []
        self.ops[eng].append(o)
        return o

    def op(self, eng, fn, r=(), w=()):
        return self._mk(eng, fn, r, w, "c", None)

    def dma(self, eng, out, in_, r=(), w=(), key=None):
        assert key is not None
        if key not in self.dsem:
            self.dsem[key] = self.es.enter_context(self.nc.semaphore("d_" + key))
            self.dcnt[key] = 0
        o = self._mk(eng, lambda e: e.dma_start(out=out, in_=in_), r, w, "d", key)
        self.dcnt[key] += 16
        o.val = self.dcnt[key]
        self.dirty[key] = o
        return o

    def flush(self, final=False):
        bar = Op()
        bar.eng, bar.fn, bar.kind, bar.key, bar.need, bar.val = "sp", None, "b", None, True, None
        deps = list(self.pending["sp"])
        self.pending["sp"] = []
        for e in ENG:
            if self.ops[e]:
                lo = None
                for o in reversed(self.ops[e]):
                    if o.kind == "c":
                        lo = o
                        break
                if lo is not None:
                    lo.need = True
                    deps.append(lo)
        deps.extend(self.dirty.values())
        self.dirty = {}
        bar.deps = deps
        self.ops["sp"].append(bar)
        for e in ENG:
            if e != "sp":
                self.pending[e].append(bar)
        for r in self.allres:
            r.lw = None
            r.rd = []
        for e in ENG:
            for o in self.ops[e]:
                if o.kind != "d" and o.need:
                    self.cnt[e] += 1
                    o.val = self.cnt[e]
        nc = self.nc
        me = self

        def replay(e, h):
            seen = me.seen[e]
            for o in me.ops[e]:
                for d in o.deps:
                    if d.kind == "d":
                        sem = me.dsem[d.key]
                    else:
                        sem = me.sem[d.eng]
                    if seen.get(sem.name, 0) >= d.val:
                        continue
                    seen[sem.name] = d.val
                    h.wait_ge(sem, d.val)
                    me.n_inst += 1
                if o.kind == "b":
                    h.sem_inc(me.sem["sp"], 1)
                    continue
                ins = o.fn(h)
                me.n_inst += 1
                if o.kind == "d":
                    ins.then_inc(me.dsem[o.key], 16)
                elif o.need:
                    ins.then_inc(me.sem[e], 1)
            if final and e != "sp":
                for d in me.pending[e]:
                    h.wait_ge(me.sem["sp"], d.val)

        with nc.Block() as block:
            @block.tensor
            def _(h):
                replay("pe", h)

            @block.scalar
            def _(h):
                replay("act", h)

            @block.vector
            def _(h):
                replay("dve", h)

            @block.gpsimd
            def _(h):
                replay("pool", h)

            @block.sync
            def _(h):
                replay("sp", h)
        self.ops = {e: [] for e in ENG}


ALL_PHASES = ('inproj', 'attn', 'ssd', 'outproj', 'ffn')


def build(n_layers, n_seq, debug=False, phases=ALL_PHASES):
    nc = bass.Bass("TRN2", target_bir_lowering=False)

    def dram(name, shape, dt, kind="Internal"):
        return nc.dram_tensor(name, shape, dt, kind=kind).ap()

    sk = "ExternalOutput" if debug else "Internal"
    x_in = dram("x", [n_seq, S, D], F32, "ExternalInput")
    y = dram("y", [n_seq, S, D], F32, "ExternalOutput")
    w_in = [dram(f"w_in{l}", [D, IN_DIM], F32, "ExternalInput") for l in range(n_layers)]
    w_out = [dram(f"w_out{l}", [D, D], F32, "ExternalInput") for l in range(n_layers)]
    w_gate = [dram(f"w_gate{l}", [D, DFF], F32, "ExternalInput") for l in range(n_layers)]
    w_up = [dram(f"w_up{l}", [D, DFF], F32, "ExternalInput") for l in range(n_layers)]
    w_down = [dram(f"w_down{l}", [DFF, D], F32, "ExternalInput") for l in range(n_layers)]
    pp_d = dram("pp", [n_layers, 128, NP_], F32, "ExternalInput")
    cst_d = dram("cst", [128, NCST], F32, "ExternalInput")
    qT_s = dram("qT_s", [NH, 128, S], BF16, sk)
    kT_s = dram("kT_s", [NH, 128, S], BF16, sk)
    v_s = dram("v_s", [S, 1024], BF16, sk)
    sz_s = dram("sz_s", [S, 1024], F32, sk)
    dt_s = dram("dt_s", [S, 16], F32, sk)
    xs_s = dram("xs_s", [S, 1024], F32, sk)
    BT_s = dram("BT_s", [2, 128, S], BF16, sk)
    CT_s = dram("CT_s", [2, 128, S], BF16, sk)
    Bt_s = dram("Bt_s", [S, 256], BF16, sk)

    if debug:
        dbg_on = dram("dbg_on", [128, NH, S], BF16, sk)
        dbg_os = dram("dbg_os", [128, NH, S], BF16, sk)
    es = ExitStack()
    K = Sched(nc, es)

    uniq = [0]

    def sb(stack, name, shape, dt):
        uniq[0] += 1
        return stack.enter_context(nc.sbuf_tensor(f"sb{uniq[0]}_{name}", shape, dt))

    def ps(stack, name, shape, dt):
        uniq[0] += 1
        return stack.enter_context(nc.psum_tensor(f"ps{uniq[0]}_{name}", shape, dt))

    cst = sb(es, "cst", [128, NCST], F32)
    cbf = sb(es, "cbf", [128, 384], BF16)
    pp = sb(es, "pp", [128, NP_], F32)
    abc = sb(es, "abc", [128, 16], F32)
    epsb = sb(es, "epsb", [128, 2], F32)
    R_cst, R_cbf, R_pp, R_abc = K.res("cst"), K.res("cbf"), K.res("pp"), K.res("abc")
    ident = cst[:, C_ID:C_ID + 128]
    ones = cst[:, C_ONE:C_ONE + 128]
    ustr = cst[:, C_US:C_US + 128]
    tri = cst[:, C_TRI:C_TRI + 128]
    ident_bf = cbf[:, 0:128]
    nlge_bf = cbf[:, 128:256]
    nones_bf = cbf[:, 256:384]

    K.dma("sp", cst[:], cst_d[:, :], w=[R_cst], key="cst")
    K.op("act", lambda e: e.activation(out=cbf[:, 0:128], in_=cst[:, C_ID:C_ID + 128], func=AF.Copy),
         r=[R_cst], w=[R_cbf])
    K.op("act", lambda e: e.activation(out=cbf[:, 128:256], in_=cst[:, C_LGE:C_LGE + 128], func=AF.Copy, scale=-1.0),
         r=[R_cst], w=[R_cbf])
    K.op("act", lambda e: e.activation(out=cbf[:, 256:384], in_=cst[:, C_ONE:C_ONE + 128], func=AF.Copy, scale=-1.0),
         r=[R_cst], w=[R_cbf])
    K.op("dve", lambda e: e.memset(epsb[:, 0:1], EPS), w=[R_abc])
    K.op("dve", lambda e: e.memset(epsb[:, 1:2], math.log(DH ** -0.5)), w=[R_abc])
    K.flush()

    P = [ps(es, f"P{i}", [128, 512], F32) for i in range(8)]
    R_P = [K.res(f"P{i}") for i in range(8)]
    Pb = [p[:].bitcast(BF16) for p in P]

    def wview(flat, n):
        c = 8192 // n if n == 512 else 22
        return flat[:, 0:c * n].rearrange("p (c n) -> p c n", n=n)

    def emit_norm(stack, src, ntile, hT, R_hT, gcol, banks):
        xt = [sb(stack, f"n_xt{i}", [128, D], F32) for i in range(2)]
        R_xt = [K.res(f"n_xt{i}") for i in range(2)]
        xn = [sb(stack, f"n_xn{i}", [128, D], BF16) for i in range(2)]
        R_xn = [K.res(f"n_xn{i}") for i in range(2)]
        junk = sb(stack, "n_junk", [128, D], BF16)
        R_junk = K.res("n_junk")
        st = [sb(stack, f"n_st{i}", [128, 4], F32) for i in range(2)]
        R_st = [K.res(f"n_st{i}") for i in range(2)]
        for j in range(ntile):
            b = j % 2
            K.dma("sp", xt[b][:], src(j), w=[R_xt[b]], key=f"n_xt{b}")
            K.op("act", lambda e, b=b: e.activation(out=junk[:], in_=xt[b][:], func=AF.Square,
                                                     accum_out=st[b][:, 0:1]),
                 r=[R_xt[b]], w=[R_junk, R_st[b]])
            K.op("act", lambda e, b=b: e.activation(out=st[b][:, 1:2], in_=st[b][:, 0:1], func=AF.Ln,
                                                     scale=1.0 / D, bias=epsb[:, 0:1]),
                 r=[R_st[b]], w=[R_st[b]])
            K.op("act", lambda e, b=b: e.activation(out=st[b][:, 2:3], in_=st[b][:, 1:2], func=AF.Exp, scale=-0.5),
                 r=[R_st[b]], w=[R_st[b]])
            K.op("dve", lambda e, b=b: e.tensor_scalar(out=xn[b][:], in0=xt[b][:], scalar1=st[b][:, 2:3],
                                                       scalar2=None, op0=ALU.mult),
                 r=[R_xt[b], R_st[b]], w=[R_xn[b]])
            for c2 in range(2):
                pb = banks[(j * 2 + c2) % len(banks)]
                tpv = Pb[pb].rearrange("p (i t) -> p i t", t=128)

                def tr(e, b=b, c2=c2, tpv=tpv):
                    for i in range(8):
                        c = c2 * 8 + i
                        ins = e.transpose(tpv[:, i, :], xn[b][:, c * 128:(c + 1) * 128], ident_bf)
                    return ins
                K.op("pe", tr, r=[R_xn[b], R_cbf], w=[R_P[pb]])

                def ev(e, c2=c2, tpv=tpv, j=j):
                    for i in range(8):
                        c = c2 * 8 + i
                        ins = e.tensor_scalar(out=hT[:, c, j * 128:(j + 1) * 128], in0=tpv[:, i, :],
                                              scalar1=pp[:, gcol + c:gcol + c + 1], scalar2=None, op0=ALU.mult)
                    return ins
                K.op("dve" if c2 == 0 else "pool_no", ev, r=[R_P[pb], R_pp], w=[R_hT[j]]) if False else \
                    K.op("dve", ev, r=[R_P[pb], R_pp], w=[R_hT[j]])

    def phase_inproj(L, q, xsrc):
        with ExitStack() as st:
            hT = sb(st, "hT", [128, 16, S], BF16)
            R_hT = [K.res(f"hT{j}") for j in range(NT)]
            wbf = [sb(st, f"wb{i}", [128, 8192], BF16) for i in range(3)]
            R_wb = [K.res(f"wb{i}") for i in range(3)]
            wdt = sb(st, "wdt", [128, 16, 16], BF16)
            R_wdt = K.res("wdt")
            w_l = w_in[L].rearrange("(c p) n -> p c n", p=128)
            tiles = [("v", 0), ("v", 1), ("z", 0), ("z", 1), ("q", 0), ("q", 1), ("k", 0), ("k", 1),
                     ("x", 0), ("x", 1), ("x", 2)]
            col0 = {"q": 0, "k": 1024, "v": 2048, "z": 3072, "x": 4096}

            def wload(ti):
                kind, w = tiles[ti]
                slot = ti % 3
                K.dma("pool", wview(wbf[slot], 512), w_l[:, :, col0[kind] + w * 512: col0[kind] + (w + 1) * 512],
                      w=[R_wb[slot]], key=f"wb{slot}")
            K.dma("pool", wdt[:], w_l[:, :, 5632:5648], w=[R_wdt], key="wdt")
            for ti in range(3):
                wload(ti)
            with ExitStack() as st2:
                emit_norm(st2, lambda j: xsrc[q, j * 128:(j + 1) * 128, :], NT, hT, R_hT, PP_NMIX, [6, 7])
                K.flush()
            vst = [sb(st, f"vst{i}", [128, 512], BF16) for i in range(2)]
            R_vst = [K.res(f"vst{i}") for i in range(2)]
            zst = [sb(st, f"zst{i}", [128, 512], F32) for i in range(2)]
            R_zst = [K.res(f"zst{i}") for i in range(2)]
            sqt = [sb(st, f"sqt{i}", [128, 512], BF16) for i in range(2)]
            R_sqt = [K.res(f"sqt{i}") for i in range(2)]
            rt = [sb(st, f"rt{i}", [128, 512], F32) for i in range(2)]
            R_rt = [K.res(f"rt{i}") for i in range(2)]
            qst = [sb(st, f"qst{i}", [128, 512], BF16) for i in range(2)]
            R_qst = [K.res(f"qst{i}") for i in range(2)]
            craw = [sb(st, f"craw{i}", [128, S + 3], F32) for i in range(2)]
            R_craw = [K.res(f"craw{i}") for i in range(2)]
            ct = [sb(st, f"ct{i}", [128, S], F32) for i in range(2)]
            R_ct = [K.res(f"ct{i}") for i in range(2)]
            xsT = [sb(st, f"xsT{i}", [128, S], F32) for i in range(1)]
            R_xsT = [K.res(f"xsT{i}") for i in range(1)]
            bcT = [sb(st, f"bcT{i}", [128, S], BF16) for i in range(2)]
            R_bcT = [K.res(f"bcT{i}") for i in range(2)]
            xst = sb(st, "xst", [128, 16, 128], F32)
            R_xst = K.res("xst")
            bst = sb(st, "bst", [128, 16, 128], BF16)
            R_bst = K.res("bst")
            dtw = sb(st, "dtw", [128, 3, 256], F32)
            R_dtw = K.res("dtw")
            for i in range(2):
                K.op("dve", lambda e, i=i: e.memset(craw[i][:, 0:3], 0.0), w=[R_craw[i]])

            pdt = P[4][:, 0:256].rearrange("p (t h) -> p t h", h=16)
            for tt in range(NT):
                def mm(e, tt=tt):
                    for c in range(16):
                        ins = e.matmul(pdt[:, tt, :], lhsT=hT[:, c, tt * 128:(tt + 1) * 128], rhs=wdt[:, c, :],
                                       start=(c == 0), stop=(c == 15))
                    return ins
                K.op("pe", mm, r=[R_hT[tt], R_wdt], w=[R_P[4]])
            dtv = [dtw[:, i, :].rearrange("p (t h) -> p t h", h=16) for i in range(3)]
            K.op("dve", lambda e: e.tensor_tensor(out=dtv[0], in0=pdt,
                                                  in1=pp[:, None, PP_DTB:PP_DTB + 16].to_broadcast([128, 16, 16]),
                                                  op=ALU.add), r=[R_P[4], R_pp], w=[R_dtw])
            K.op("act", lambda e: e.activation(out=dtw[:, 1, :], in_=dtw[:, 0, :], func=AF.Exp), r=[R_dtw], w=[R_dtw])
            K.op("act", lambda e: e.activation(out=dtw[:, 2, :], in_=dtw[:, 1, :], func=AF.Ln, bias=1.0),
                 r=[R_dtw], w=[R_dtw])
            K.dma("sp", dt_s.rearrange("(t p) h -> p t h", p=128), dtv[2], r=[R_dtw], key="dtw")

            state = {"u": 0, "ev": 0}
            dq = []

            def defer(fn, delay):
                dq.append([delay, fn])

            def run_deferred(all_=False):
                for it in list(dq):
                    it[0] -= 1
                    if it[0] <= 0 or all_:
                        dq.remove(it)
                        it[1]()

            for ti, (kind, w) in enumerate(tiles):
                slot = ti % 3
                wv = wview(wbf[slot], 512)
                if ti >= 3:
                    wload(ti)
                if kind in ("v", "z"):
                    for tt in range(NT):
                        u = state["u"]; state["u"] += 1
                        pa = u % 4

                        def mm(e, tt=tt, pa=pa, wv=wv):
                            for c in range(16):
                                ins = e.matmul(P[pa][:], lhsT=hT[:, c, tt * 128:(tt + 1) * 128], rhs=wv[:, c, :],
                                               start=(c == 0), stop=(c == 15))
                            return ins
                        K.op("pe", mm, r=[R_hT[tt], R_wb[slot]], w=[R_P[pa]])
                        run_deferred()
                        k2 = u % 2
                        if kind == "v":
                            K.op("act", lambda e, pa=pa, k2=k2: e.activation(out=vst[k2][:], in_=P[pa][:], func=AF.Copy),
                                 r=[R_P[pa]], w=[R_vst[k2]])
                            K.dma("sp", v_s[tt * 128:(tt + 1) * 128, w * 512:(w + 1) * 512], vst[k2][:],
                                  r=[R_vst[k2]], key=f"vst{k2}")
                        else:
                            K.op("act", lambda e, pa=pa, k2=k2: e.activation(out=zst[k2][:], in_=P[pa][:], func=AF.Silu),
                                 r=[R_P[pa]], w=[R_zst[k2]])
                            K.dma("sp", sz_s[tt * 128:(tt + 1) * 128, w * 512:(w + 1) * 512], zst[k2][:],
                                  r=[R_zst[k2]], key=f"zst{k2}")
                else:
                    for j in range(4):
                        cj = w * 4 + j
                        for tb in range(NB):
                            u = state["u"]; state["u"] += 1
                            pa = u % 4

                            def mm(e, j=j, tb=tb, pa=pa, wv=wv):
                                for c in range(16):
                                    ins = e.matmul(P[pa][:], lhsT=wv[:, c, j * 128:(j + 1) * 128],
                                                   rhs=hT[:, c, tb * 512:(tb + 1) * 512],
                                                   start=(c == 0), stop=(c == 15))
                                return ins
                            K.op("pe", mm, r=[R_hT[4 * tb + i] for i in range(4)] + [R_wb[slot]], w=[R_P[pa]])
                            run_deferred()
                            if kind in ("q", "k"):
                                k2 = state["ev"] % 2; state["ev"] += 1
                                K.op("act", lambda e, pa=pa, k2=k2: e.activation(out=sqt[k2][:], in_=P[pa][:],
                                                                                  func=AF.Square),
                                     r=[R_P[pa]], w=[R_sqt[k2]])

                                def rest(kind=kind, cj=cj, tb=tb, pa=pa, k2=k2):
                                    pn = 4 + k2
                                    K.op("pe", lambda e: e.matmul(P[pn][:], lhsT=nones_bf, rhs=sqt[k2][:],
                                                                  start=True, stop=True),
                                         r=[R_sqt[k2], R_cbf], w=[R_P[pn]])
                                    K.op("act", lambda e: e.activation(out=rt[k2][:], in_=P[pn][:], func=AF.Ln,
                                                                       scale=-1.0 / DH, bias=epsb[:, 0:1]),
                                         r=[R_P[pn]], w=[R_rt[k2]])
                                    if kind == "q":
                                        K.op("act", lambda e: e.activation(out=rt[k2][:], in_=rt[k2][:], func=AF.Exp,
                                                                           scale=-0.5, bias=epsb[:, 1:2]),
                                             r=[R_rt[k2]], w=[R_rt[k2]])
                                    else:
                                        K.op("act", lambda e: e.activation(out=rt[k2][:], in_=rt[k2][:], func=AF.Exp,
                                                                           scale=-0.5),
                                             r=[R_rt[k2]], w=[R_rt[k2]])
                                    gc = PP_QG if kind == "q" else PP_KG
                                    K.op("dve", lambda e: e.scalar_tensor_tensor(
                                        out=qst[k2][:], in0=P[pa][:], scalar=pp[:, gc:gc + 1], in1=rt[k2][:],
                                        op0=ALU.mult, op1=ALU.mult), r=[R_P[pa], R_rt[k2], R_pp], w=[R_qst[k2]])
                                    dst = (qT_s if kind == "q" else kT_s)[cj, :, tb * 512:(tb + 1) * 512]
                                    K.dma("sp", dst, qst[k2][:], r=[R_qst[k2]], key=f"qst{k2}")
                                defer(rest, 1)
                            else:
                                cb = cj % 2
                                K.op("act", lambda e, pa=pa, cb=cb, tb=tb: e.activation(
                                    out=craw[cb][:, 3 + tb * 512: 3 + (tb + 1) * 512], in_=P[pa][:], func=AF.Copy),
                                    r=[R_P[pa]], w=[R_craw[cb]])
                                if tb == NB - 1:
                                    def conv(cj=cj, cb=cb):
                                        wc = PP_CW + cj * 4
                                        pe_part = []
                                        K.op("dve", lambda e: e.tensor_scalar(
                                            out=ct[cb][:], in0=craw[cb][:, 0:S], scalar1=pp[:, wc:wc + 1],
                                            scalar2=pp[:, PP_CB + cj:PP_CB + cj + 1], op0=ALU.mult, op1=ALU.add),
                                            r=[R_craw[cb], R_pp], w=[R_ct[cb]])
                                        for i in range(1, 4):
                                            K.op("dve", lambda e, i=i: e.scalar_tensor_tensor(
                                                out=ct[cb][:], in0=craw[cb][:, i:i + S], scalar=pp[:, wc + i:wc + i + 1],
                                                in1=ct[cb][:], op0=ALU.mult, op1=ALU.add),
                                                r=[R_craw[cb], R_pp, R_ct[cb]], w=[R_ct[cb]])
                                        if cj < 8:
                                            K.op("act", lambda e: e.activation(out=xsT[0][:], in_=ct[cb][:], func=AF.Silu),
                                                 r=[R_ct[cb]], w=[R_xsT[0]])

                                            def xs_pe():
                                              for t4 in range(4):
                                                pb = 6 + (t4 % 2)
                                                tv = P[pb][:].rearrange("p (i t) -> p i t", t=128)

                                                def tr(e, t4=t4, tv=tv):
                                                    for i in range(4):
                                                        tt = t4 * 4 + i
                                                        ins = e.transpose(tv[:, i, :], xsT[0][:, tt * 128:(tt + 1) * 128],
                                                                          ident)
                                                    return ins
                                                K.op("pe", tr, r=[R_xsT[0], R_cst], w=[R_P[pb]])
                                                K.op("act", lambda e, t4=t4, tv=tv: e.activation(
                                                    out=xst[:, t4 * 4:(t4 + 1) * 4, :], in_=tv, func=AF.Copy),
                                                    r=[R_P[pb]], w=[R_xst])
                                              K.dma("sp", xs_s.rearrange("(t p) c -> p t c", p=128)[:, :, cj * 128:(cj + 1) * 128],
                                                    xst[:], r=[R_xst], key="xst")
                                            defer(xs_pe, 3)
                                        else:
                                            g = (cj - 8) % 2
                                            isB = cj < 10
                                            bb = cj % 2
                                            K.op("act", lambda e: e.activation(out=bcT[bb][:], in_=ct[cb][:], func=AF.Silu),
                                                 r=[R_ct[cb]], w=[R_bcT[bb]])
                                            K.dma("sp", (BT_s if isB else CT_s)[g], bcT[bb][:], r=[R_bcT[bb]], key=f"bcT{bb}")
                                            def b_pe():
                                                for t8 in range(2):
                                                    pb = 6 + t8
                                                    tv = Pb[pb].rearrange("p (i t) -> p i t", t=128)

                                                    def tr(e, t8=t8, tv=tv):
                                                        for i in range(8):
                                                            tt = t8 * 8 + i
                                                            ins = e.transpose(tv[:, i, :], bcT[bb][:, tt * 128:(tt + 1) * 128],
                                                                              ident_bf)
                                                        return ins
                                                    K.op("pe", tr, r=[R_bcT[bb], R_cbf], w=[R_P[pb]])
                                                    K.op("act", lambda e, t8=t8, tv=tv: e.activation(
                                                        out=bst[:, t8 * 8:(t8 + 1) * 8, :], in_=tv, func=AF.Copy),
                                                        r=[R_P[pb]], w=[R_bst])
                                                K.dma("sp", Bt_s.rearrange("(t p) c -> p t c", p=128)[:, :, g * 128:(g + 1) * 128],
                                                      bst[:], r=[R_bst], key="bst")
                                            if isB:
                                                defer(b_pe, 3)
                                    conv()
            run_deferred(all_=True)
        K.flush()

    def phase_attn(L, q, onT, R_onT):
        with ExitStack() as st:
            oT = sb(st, "oT", [128, NH, S], F32)
            R_oT = [[K.res(f"oT{h}_{tb}") for tb in range(NB)] for h in range(NH)]
            qb = [sb(st, f"qb{i}", [128, S], BF16) for i in range(2)]
            kb_ = [sb(st, f"kb{i}", [128, S], BF16) for i in range(2)]
            vb = [sb(st, f"vb{i}", [128, NT, 128], BF16) for i in range(2)]
            R_qb = [K.res(f"qb{i}") for i in range(2)]
            R_kb = [K.res(f"kb{i}") for i in range(2)]
            R_vb = [K.res(f"vb{i}") for i in range(2)]
            et = [sb(st, f"et{i}", [128, 512], F32) for i in range(2)]
            R_et = [K.res(f"et{i}") for i in range(2)]
            spt = [sb(st, f"spt{i}", [128, 512], BF16) for i in range(4)]
            R_spt = [K.res(f"spt{i}") for i in range(4)]
            spf = [sb(st, f"spf{i}", [128, 512], F32) for i in range(2)]
            R_spf = [K.res(f"spf{i}") for i in range(2)]
            Wt = [sb(st, f"Wt{i}", [128, 512], BF16) for i in range(3)]
            R_Wt = [K.res(f"Wt{i}") for i in range(3)]
            Wf = [sb(st, f"Wf{i}", [128, 512], F32) for i in range(2)]
            R_Wf = [K.res(f"Wf{i}") for i in range(2)]
            acc = [sb(st, f"acc{i}", [128, 512], BF16) for i in range(2)]
            R_acc = [K.res(f"acc{i}") for i in range(2)]
            sqb = [sb(st, f"sqb{i}", [128, 512], BF16) for i in range(2)]
            R_sqb = [K.res(f"sqb{i}") for i in range(2)]
            rtn = sb(st, "rtn", [128, 512], F32)
            R_rtn = K.res("rtn")
            gctr = [0]
            vview = v_s.rearrange("(t p) c -> p t c", p=128)
            for h in range(NH):
                b = h % 2
                K.dma("sp", qb[b][:], qT_s[h], w=[R_qb[b]], key=f"qb{b}")
                K.dma("sp", kb_[b][:], kT_s[h], w=[R_kb[b]], key=f"kb{b}")
                K.dma("sp", vb[b][:], vview[:, :, h * 128:(h + 1) * 128], w=[R_vb[b]], key=f"vb{b}")
                def do_unit(h, tb, b):
                    unit = h * NB + tb
                    ob = 3 + unit % 2
                    kbs = list(range(4 * tb + 3, -1, -1))
                    n = len(kbs)
                    g0 = gctr[0]
                    gctr[0] += n
                    accsrc = {}

                    def mask_ap(o):
                        return cst[:, C_AM + (3 - o) * 128: C_AM + (3 - o) * 128 + 512]

                    def stageA(i):
                        g = g0 + i
                        kb = kbs[i]
                        zi = g % 3
                        diag = kb >= 4 * tb
                        o = kb - 4 * tb
                        K.op("pe", lambda e: e.matmul(P[zi][:], lhsT=kb_[b][:, kb * 128:(kb + 1) * 128],
                                                      rhs=qb[b][:, tb * 512:(tb + 1) * 512], start=True, stop=True),
                             r=[R_kb[b], R_qb[b]], w=[R_P[zi]])
                        K.op("act", lambda e: e.activation(out=et[g % 2][:], in_=P[zi][:], func=AF.Exp),
                             r=[R_P[zi]], w=[R_et[g % 2]])
                        if diag:
                            K.op("act", lambda e: e.activation(out=spf[g % 2][:], in_=et[g % 2][:], func=AF.Ln, bias=1.0),
                                 r=[R_et[g % 2]], w=[R_spf[g % 2]])
                            K.op("dve", lambda e: e.tensor_tensor(out=spt[g % 4][:], in0=spf[g % 2][:], in1=mask_ap(o),
                                                                  op=ALU.mult),
                                 r=[R_spf[g % 2], R_cst], w=[R_spt[g % 4]])
                        else:
                            K.op("act", lambda e: e.activation(out=spt[g % 4][:], in_=et[g % 2][:], func=AF.Ln, bias=1.0),
                                 r=[R_et[g % 2]], w=[R_spt[g % 4]])
                        if i == 1:
                            accsrc[1] = (spt[(g - 1) % 4], R_spt[(g - 1) % 4])
                        elif i >= 2:
                            prev, R_prev = accsrc[i - 1]
                            a = i % 2
                            K.op("pool", lambda e: e.tensor_tensor(out=acc[a][:], in0=prev[:], in1=spt[(g - 1) % 4][:],
                                                                   op=ALU.add),
                                 r=[R_prev, R_spt[(g - 1) % 4]], w=[R_acc[a]])
                            accsrc[i] = (acc[a], R_acc[a])

                    def stageB(i):
                        g = g0 + i
                        kb = kbs[i]
                        zi = g % 3
                        diag = kb >= 4 * tb
                        o = kb - 4 * tb

                        def mm(e):
                            ins = e.matmul(P[zi][:], lhsT=nlge_bf, rhs=spt[g % 4][:], start=False, stop=(i == 0),
                                           skip_group_check=True)
                            if i >= 1:
                                ins = e.matmul(P[zi][:], lhsT=nones_bf, rhs=accsrc[i][0][:], start=False, stop=True,
                                               skip_group_check=True)
                            return ins
                        rr = [R_spt[g % 4], R_cbf] + ([accsrc[i][1]] if i >= 1 else [])
                        K.op("pe", mm, r=rr, w=[R_P[zi]])
                        if diag:
                            K.op("act", lambda e: e.activation(out=Wf[g % 2][:], in_=P[zi][:], func=AF.Exp),
                                 r=[R_P[zi]], w=[R_Wf[g % 2]])
                            K.op("dve", lambda e: e.tensor_tensor(out=Wt[g % 3][:], in0=Wf[g % 2][:], in1=mask_ap(o),
                                                                  op=ALU.mult),
                                 r=[R_Wf[g % 2], R_cst], w=[R_Wt[g % 3]])
                        else:
                            K.op("act", lambda e: e.activation(out=Wt[g % 3][:], in_=P[zi][:], func=AF.Exp),
                                 r=[R_P[zi]], w=[R_Wt[g % 3]])

                    def stageC(i):
                        g = g0 + i
                        kb = kbs[i]
                        K.op("pe", lambda e: e.matmul(P[ob][:], lhsT=vb[b][:, kb, :], rhs=Wt[g % 3][:],
                                                      start=(i == 0), stop=(i == n - 1), skip_group_check=True),
                             r=[R_vb[b], R_Wt[g % 3]], w=[R_P[ob]])
                        if i == n - 1:
                            K.op("dve", lambda e: e.tensor_copy(out=oT[:, h, tb * 512:(tb + 1) * 512], in_=P[ob][:]),
                                 r=[R_P[ob]], w=[R_oT[h][tb]])

                    for step in range(n + 2):
                        if step < n:
                            stageA(step)
                        if 1 <= step <= n:
                            stageB(step - 1)
                        if step >= 2:
                            stageC(step - 2)
                for tb in range(NB):
                    do_unit(h, tb, b)
            def do_norm(tb):
                pn = 5 + tb % 2
                for h in range(NH):
                    k2 = (tb * NH + h) % 2
                    K.op("act", lambda e, h=h, k2=k2: e.activation(out=sqb[k2][:], in_=oT[:, h, tb * 512:(tb + 1) * 512],
                                                                    func=AF.Square),
                         r=[R_oT[h][tb]], w=[R_sqb[k2]])
                    K.op("pe", lambda e, h=h, k2=k2: e.matmul(P[pn][:], lhsT=nones_bf, rhs=sqb[k2][:], start=(h == 0),
                                                               stop=(h == NH - 1), skip_group_check=True),
                         r=[R_sqb[k2], R_cbf], w=[R_P[pn]])
                K.op("act", lambda e: e.activation(out=rtn[:], in_=P[pn][:], func=AF.Ln, scale=-1.0 / 1024,
                                                   bias=epsb[:, 0:1]), r=[R_P[pn]], w=[R_rtn])
                K.op("act", lambda e: e.activation(out=rtn[:], in_=rtn[:], func=AF.Exp, scale=-0.5),
                     r=[R_rtn], w=[R_rtn])
                for h in range(NH):
                    K.op("dve", lambda e, h=h: e.scalar_tensor_tensor(
                        out=onT[:, h, tb * 512:(tb + 1) * 512], in0=oT[:, h, tb * 512:(tb + 1) * 512],
                        scalar=pp[:, PP_AOG + h:PP_AOG + h + 1], in1=rtn[:], op0=ALU.mult, op1=ALU.mult),
                        r=[R_oT[h][tb], R_rtn, R_pp], w=[R_onT])
            for tb in range(NB):
                do_norm(tb)
        K.flush()

    def phase_ssd(L, q, osT, R_osT):
        with ExitStack() as st:
            def dbl(name, shape, dt):
                return ([sb(st, f"{name}{i}", shape, dt) for i in range(2)], [K.res(f"{name}{i}") for i in range(2)])
            xs, R_xs = dbl("s_xs", [128, 1024], F32)
            szt, R_sz = dbl("s_sz", [128, 1024], F32)
            dtt, R_dt = dbl("s_dt", [128, 16], F32)
            BT, R_BT = dbl("s_BT", [128, 2, 128], BF16)
            CT, R_CT = dbl("s_CT", [128, 2, 128], BF16)
            Bt, R_Bt = dbl("s_Bt", [128, 256], BF16)
            da, R_da = dbl("s_da", [128, 16], F32)
            ex, R_ex = dbl("s_ex", [128, 64], F32)
            Dm, R_Dm = dbl("s_Dm", [128, 16, 128], F32)
            dec, R_dec = dbl("s_dec", [128, 16, 128], F32)
            cbm, R_cbm = dbl("s_cbm", [128, 2, 128], F32)
            MT, R_MT = dbl("s_MT", [128, 16, 128], BF16)
            xdt, R_xdt = dbl("s_xdt", [128, 16, 64], BF16)
            xde, R_xde = dbl("s_xde", [128, 16, 64], BF16)
            hSb, _ = dbl("s_hSb", [128, 1024], BF16)
            R_hSb = [[K.res(f"s_hSb{i}_{g}") for g in range(2)] for i in range(2)]
            t1, R_t1 = dbl("s_t1", [128, 1024], F32)
            t2, R_t2 = dbl("s_t2", [128, 1024], F32)
            yz, R_yz = dbl("s_yz", [128, 1024], F32)
            yd, R_yd = dbl("s_yd", [128, 1024], F32)
            osm, R_osm = dbl("s_osm", [128, 1024], BF16)
            ssq, R_ssq = dbl("s_ssq", [128, 8], F32)
            hS = sb(st, "s_hS", [128, 1024], F32)
            R_hS = [K.res(f"s_hS{g}") for g in range(2)]
            junk = sb(st, "s_junk", [128, 512], BF16)
            R_junk = K.res("s_junk")
            K.op("dve", lambda e: e.memset(hS[:], 0.0), w=R_hS)
            K.op("dve", lambda e: e.memset(hSb[0][:], 0.0), w=R_hSb[0])

            def S1(c):
                b = c % 2
                tok = slice(c * 128, (c + 1) * 128)
                K.dma("sp", xs[b][:], xs_s[tok, :], w=[R_xs[b]], key=f"s_xs{b}")
                K.dma("sp", dtt[b][:], dt_s[tok, :], w=[R_dt[b]], key=f"s_dt{b}")
                K.dma("sp", BT[b][:], BT_s[:, :, tok].rearrange("g n t -> n g t"), w=[R_BT[b]], key=f"s_BT{b}")
                K.dma("sp", CT[b][:], CT_s[:, :, tok].rearrange("g n t -> n g t"), w=[R_CT[b]], key=f"s_CT{b}")
                K.dma("sp", Bt[b][:], Bt_s[tok, :], w=[R_Bt[b]], key=f"s_Bt{b}")
                K.op("dve", lambda e: e.tensor_tensor(out=da[b][:], in0=dtt[b][:], in1=abc[:], op=ALU.mult),
                     r=[R_dt[b], R_abc], w=[R_da[b]])

                def mm(e):
                    e.matmul(P[0][:, 0:16], lhsT=ustr, rhs=da[b][:], start=True, stop=True)
                    e.matmul(P[0][:, 16:32], lhsT=tri, rhs=da[b][:], start=True, stop=True)
                    return e.matmul(P[0][:, 32:48], lhsT=ones, rhs=da[b][:], start=True, stop=True)
                K.op("pe", mm, r=[R_da[b], R_cst], w=[R_P[0]])
                K.op("act", lambda e: e.activation(out=ex[b][:, 0:48], in_=P[0][:, 0:48], func=AF.Exp),
                     r=[R_P[0]], w=[R_ex[b]])
                K.op("dve", lambda e: e.tensor_tensor(out=ex[b][:, 48:64], in0=ex[b][:, 0:16], in1=dtt[b][:],
                                                      op=ALU.mult), r=[R_ex[b], R_dt[b]], w=[R_ex[b]])
                K.op("dve", lambda e: e.tensor_tensor(out=Dm[b][:], in0=ustr[:, None, :].to_broadcast([128, 16, 128]),
                                                      in1=da[b][:, :, None].to_broadcast([128, 16, 128]), op=ALU.mult),
                     r=[R_cst, R_da[b]], w=[R_Dm[b]])
                cv = P[0][:, 128:384].rearrange("p (g l) -> p g l", l=128)

                def mmc(e):
                    e.matmul(cv[:, 0, :], lhsT=BT[b][:, 0, :], rhs=CT[b][:, 0, :], start=True, stop=True)
                    return e.matmul(cv[:, 1, :], lhsT=BT[b][:, 1, :], rhs=CT[b][:, 1, :], start=True, stop=True)
                K.op("pe", mmc, r=[R_BT[b], R_CT[b]], w=[R_P[0]])
                K.op("dve", lambda e: e.tensor_tensor(out=cbm[b][:], in0=cv,
                                                      in1=tri[:, None, :].to_broadcast([128, 2, 128]), op=ALU.mult),
                     r=[R_P[0], R_cst], w=[R_cbm[b]])
                for k in range(4):
                    pk = 1 + k % 2
                    sv = P[pk][:].rearrange("p (h l) -> p h l", l=128)

                    def mms(e, k=k, sv=sv):
                        for i in range(4):
                            ins = e.matmul(sv[:, i, :], lhsT=Dm[b][:, 4 * k + i, :], rhs=tri, start=True, stop=True)
                        return ins
                    K.op("pe", mms, r=[R_Dm[b], R_cst], w=[R_P[pk]])
                    K.op("act", lambda e, k=k, sv=sv: e.activation(out=dec[b][:, 4 * k:4 * k + 4, :], in_=sv, func=AF.Exp),
                         r=[R_P[pk]], w=[R_dec[b]])
                for g in range(2):
                    K.op("dve", lambda e, g=g: e.tensor_tensor(
                        out=MT[b][:, 8 * g:8 * g + 8, :], in0=dec[b][:, 8 * g:8 * g + 8, :],
                        in1=cbm[b][:, g:g + 1, :].to_broadcast([128, 8, 128]), op=ALU.mult),
                        r=[R_dec[b], R_cbm[b]], w=[R_MT[b]])
                xs3 = xs[b][:].rearrange("p (h d) -> p h d", d=64)
                K.op("dve", lambda e: e.tensor_tensor(
                    out=xdt[b][:], in0=xs3, in1=dtt[b][:, :, None].to_broadcast([128, 16, 64]), op=ALU.mult),
                    r=[R_xs[b], R_dt[b]], w=[R_xdt[b]])
                K.op("pool", lambda e: e.tensor_tensor(
                    out=xde[b][:], in0=xs3, in1=ex[b][:, 48:64, None].to_broadcast([128, 16, 64]), op=ALU.mult),
                    r=[R_xs[b], R_ex[b]], w=[R_xde[b]])

                K.op("pool", lambda e: e.tensor_tensor(
                    out=t2[b][:].rearrange("p (h d) -> p h d", d=64), in0=xs3,
                    in1=pp[:, PP_DSK:PP_DSK + 16, None].to_broadcast([128, 16, 64]), op=ALU.mult),
                    r=[R_xs[b], R_pp], w=[R_t2[b]])

            def S2a(c):
                b = c % 2
                tok = slice(c * 128, (c + 1) * 128)
                K.dma("sp", szt[b][:], sz_s[tok, :], w=[R_sz[b]], key=f"s_sz{b}")
                for g in range(2):
                    gs = slice(g * 512, (g + 1) * 512)
                    yv = P[3 + g][:].rearrange("p (h d) -> p h d", d=64)

                    def mmy(e, g=g, yv=yv):
                        for i in range(8):
                            ins = e.matmul(yv[:, i, :], lhsT=MT[b][:, 8 * g + i, :], rhs=xdt[b][:, 8 * g + i, :],
                                           start=True, stop=True)
                        return ins
                    K.op("pe", mmy, r=[R_MT[b], R_xdt[b]], w=[R_P[3 + g]])
                    K.op("pe", lambda e, g=g, gs=gs: e.matmul(P[5 + g][:], lhsT=CT[b][:, g, :],
                                                              rhs=hSb[b][:, gs], start=True, stop=True),
                         r=[R_CT[b], R_hSb[b][g]], w=[R_P[5 + g]])
                    K.op("pe", lambda e, g=g: e.matmul(
                        P[7][:], lhsT=Bt[b][:, g * 128:(g + 1) * 128],
                        rhs=xde[b][:, 8 * g:8 * g + 8, :].rearrange("p h d -> p (h d)"), start=True, stop=True),
                        r=[R_Bt[b], R_xde[b]], w=[R_P[7]])
                    K.op("pool", lambda e, g=g, gs=gs: e.tensor_tensor(
                        out=hS[:, gs].rearrange("p (h d) -> p h d", d=64), in0=hS[:, gs].rearrange("p (h d) -> p h d", d=64),
                        in1=ex[b][:, 32 + 8 * g:40 + 8 * g, None].to_broadcast([128, 8, 64]), op=ALU.mult),
                        r=[R_hS[g], R_ex[b]], w=[R_hS[g]])
                    K.op("dve", lambda e, gs=gs: e.tensor_tensor(out=hS[:, gs], in0=hS[:, gs], in1=P[7][:], op=ALU.add),
                         r=[R_hS[g], R_P[7]], w=[R_hS[g]])
                    K.op("act", lambda e, gs=gs: e.activation(out=hSb[1 - b][:, gs], in_=hS[:, gs], func=AF.Copy),
                         r=[R_hS[g]], w=[R_hSb[1 - b][g]])
                    K.op("dve", lambda e, g=g, gs=gs: e.tensor_tensor(
                        out=t1[b][:, gs].rearrange("p (h d) -> p h d", d=64),
                        in0=P[5 + g][:].rearrange("p (h d) -> p h d", d=64),
                        in1=ex[b][:, 16 + 8 * g:16 + 8 * g + 8, None].to_broadcast([128, 8, 64]), op=ALU.mult),
                        r=[R_P[5 + g], R_ex[b]], w=[R_t1[b]])
                    K.op("dve", lambda e, g=g, gs=gs: e.tensor_tensor(out=yd[b][:, gs], in0=t2[b][:, gs], in1=P[3 + g][:],
                                                                      op=ALU.add),
                         r=[R_t2[b], R_P[3 + g]], w=[R_yd[b]])

            def S2b(c):
                b = c % 2
                for g in range(2):
                    gs = slice(g * 512, (g + 1) * 512)
                    K.op("pool", lambda e, gs=gs: e.tensor_tensor(out=t1[b][:, gs], in0=t1[b][:, gs], in1=yd[b][:, gs],
                                                                  op=ALU.add),
                         r=[R_t1[b], R_yd[b]], w=[R_t1[b]])
                    K.op("pool", lambda e, gs=gs: e.tensor_tensor(out=yz[b][:, gs], in0=t1[b][:, gs], in1=szt[b][:, gs],
                                                                  op=ALU.mult),
                         r=[R_t1[b], R_sz[b]], w=[R_yz[b]])
                    K.op("act", lambda e, g=g, gs=gs: e.activation(out=junk[:], in_=yz[b][:, gs], func=AF.Square,
                                                                   accum_out=ssq[b][:, g:g + 1]),
                         r=[R_yz[b]], w=[R_junk, R_ssq[b]])
                K.op("act", lambda e: e.activation(out=ssq[b][:, 2:4], in_=ssq[b][:, 0:2], func=AF.Ln, scale=1.0 / 512,
                                                   bias=epsb[:, 0:1]), r=[R_ssq[b]], w=[R_ssq[b]])
                K.op("act", lambda e: e.activation(out=ssq[b][:, 4:6], in_=ssq[b][:, 2:4], func=AF.Exp, scale=-0.5),
                     r=[R_ssq[b]], w=[R_ssq[b]])
                for g in range(2):
                    gs = slice(g * 512, (g + 1) * 512)
                    K.op("dve", lambda e, g=g, gs=gs: e.scalar_tensor_tensor(
                        out=osm[b][:, gs], in0=yz[b][:, gs], scalar=ssq[b][:, 4 + g:5 + g],
                        in1=pp[:, PP_SOG + g * 512:PP_SOG + (g + 1) * 512], op0=ALU.mult, op1=ALU.mult),
                        r=[R_yz[b], R_ssq[b], R_pp], w=[R_osm[b]])
                tv = Pb[7].rearrange("p (i t) -> p i t", t=128)

                def tr(e):
                    for i in range(8):
                        ins = e.transpose(tv[:, i, :], osm[b][:, i * 128:(i + 1) * 128], ident_bf)
                    return ins
                K.op("pe", tr, r=[R_osm[b], R_cbf], w=[R_P[7]])
                K.op("act", lambda e: e.activation(out=osT[:, :, c * 128:(c + 1) * 128], in_=tv, func=AF.Copy),
                     r=[R_P[7]], w=[R_osT])

            S1(0)
            S1(1)
            S2a(0)
            for c in range(NT):
                if c + 2 < NT:
                    S1(c + 2)
                if c + 1 < NT:
                    S2a(c + 1)
                S2b(c)
        K.flush()

    def phase_outproj(L, q, xsrc, onT, R_onT, osT, R_osT):
        with ExitStack() as st:
            wbf = [sb(st, f"wb{i}", [128, 8192], BF16) for i in range(3)]
            R_wb = [K.res(f"wb{i}") for i in range(3)]
            xr = [sb(st, f"xr{i}", [128, 512], F32) for i in range(4)]
            R_xr = [K.res(f"xr{i}") for i in range(4)]
            xo = [sb(st, f"xo{i}", [128, 512], F32) for i in range(3)]
            R_xo = [K.res(f"xo{i}") for i in range(3)]
            w_l = w_out[L].rearrange("(c p) n -> p c n", p=128)
            units = [(db, tt) for db in range(4) for tt in range(NT)]

            def load(u):
                db, tt = units[u]
                K.dma("sp", xr[u % 4][:], xsrc[q, tt * 128:(tt + 1) * 128, db * 512:(db + 1) * 512],
                      w=[R_xr[u % 4]], key=f"xr{u % 4}")
            load(0)
            load(1)
            for u, (db, tt) in enumerate(units):
                slot = db % 3
                wv = wview(wbf[slot], 512)
                if tt == 0:
                    K.dma("pool", wv, w_l[:, :, db * 512:(db + 1) * 512], w=[R_wb[slot]], key=f"wb{slot}")
                if u + 2 < len(units):
                    load(u + 2)
                pa = u % 4

                def mm(e, tt=tt, pa=pa, wv=wv):
                    for c in range(16):
                        src = onT if c < 8 else osT
                        ins = e.matmul(P[pa][:], lhsT=src[:, c % 8, tt * 128:(tt + 1) * 128], rhs=wv[:, c, :],
                                       start=(c == 0), stop=(c == 15))
                    return ins
                K.op("pe", mm, r=[R_onT, R_osT, R_wb[slot]], w=[R_P[pa]])
                K.op("dve", lambda e, u=u, pa=pa: e.tensor_tensor(out=xo[u % 3][:], in0=xr[u % 4][:], in1=P[pa][:], op=ALU.add),
                     r=[R_xr[u % 4], R_P[pa]], w=[R_xo[u % 3]])
                K.dma("sp", y[q, tt * 128:(tt + 1) * 128, db * 512:(db + 1) * 512], xo[u % 3][:],
                      r=[R_xo[u % 3]], key=f"xo{u % 3}")
        K.flush()

    def phase_ffn(L, q):
        TBN = 1024
        for tbk in range(S // TBN):
            t0 = tbk * TBN
            with ExitStack() as st:
                hT = sb(st, "f_hT", [128, 16, TBN], BF16)
                R_hT = [K.res(f"f_hT{j}") for j in range(TBN // 128)]
                wbf = [sb(st, f"f_wb{i}", [128, 8192], BF16) for i in range(4)]
                R_wb = [K.res(f"f_wb{i}") for i in range(4)]
                wg_l = w_gate[L].rearrange("(c p) n -> p c n", p=128)
                wu_l = w_up[L].rearrange("(c p) n -> p c n", p=128)
                wd_l = w_down[L].rearrange("(c p) n -> p c n", p=128)

                def guload(fg):
                    sg_, su_ = fg % 2, 2 + fg % 2
                    K.dma("pool", wview(wbf[sg_], 512), wg_l[:, :, fg * 512:(fg + 1) * 512], w=[R_wb[sg_]], key=f"f_wb{sg_}")
                    K.dma("pool", wview(wbf[su_], 512), wu_l[:, :, fg * 512:(fg + 1) * 512], w=[R_wb[su_]], key=f"f_wb{su_}")
                guload(0)
                guload(1)
                with ExitStack() as st2:
                    emit_norm(st2, lambda j: y[q, t0 + j * 128: t0 + (j + 1) * 128, :], TBN // 128, hT, R_hT, PP_NFFN, [6, 7])
                    K.flush()
                actT = sb(st, "f_act", [128, NFC, TBN], BF16)
                R_act = [K.res(f"f_act{i}") for i in range(11)]
                sgt = [sb(st, f"f_sg{i}", [128, 512], F32) for i in range(2)]
                R_sg = [K.res(f"f_sg{i}") for i in range(2)]
                xr = [sb(st, f"f_xr{i}", [128, 256], F32) for i in range(4)]
                R_xr = [K.res(f"f_xr{i}") for i in range(4)]
                xo = [sb(st, f"f_xo{i}", [128, 256], F32) for i in range(3)]
                R_xo = [K.res(f"f_xo{i}") for i in range(3)]
                u = 0
                for fg in range(11):
                    sg_, su_ = fg % 2, 2 + fg % 2
                    wg = wview(wbf[sg_], 512)
                    wu = wview(wbf[su_], 512)
                    if fg >= 2:
                        guload(fg)
                    for j in range(4):
                        for ts in range(TBN // 512):
                            pg, pu = u % 3, 3 + u % 3
                            k2 = u % 2
                            u += 1

                            def mm(e, j=j, ts=ts, pg=pg, pu=pu, wg=wg, wu=wu):
                                for c in range(16):
                                    e.matmul(P[pg][:], lhsT=wg[:, c, j * 128:(j + 1) * 128],
                                             rhs=hT[:, c, ts * 512:(ts + 1) * 512], start=(c == 0), stop=(c == 15))
                                for c in range(16):
                                    ins = e.matmul(P[pu][:], lhsT=wu[:, c, j * 128:(j + 1) * 128],
                                                   rhs=hT[:, c, ts * 512:(ts + 1) * 512], start=(c == 0), stop=(c == 15))
                                return ins
                            K.op("pe", mm, r=[R_hT[4 * ts + i] for i in range(4)] + [R_wb[sg_], R_wb[su_]],
                                 w=[R_P[pg], R_P[pu]])
                            K.op("act", lambda e, pg=pg, k2=k2: e.activation(out=sgt[k2][:], in_=P[pg][:], func=AF.Silu),
                                 r=[R_P[pg]], w=[R_sg[k2]])
                            K.op("dve", lambda e, fg=fg, j=j, ts=ts, pu=pu, k2=k2: e.tensor_tensor(
                                out=actT[:, fg * 4 + j, ts * 512:(ts + 1) * 512], in0=sgt[k2][:], in1=P[pu][:], op=ALU.mult),
                                r=[R_sg[k2], R_P[pu]], w=[R_act[fg]])
                units = [(db, tt) for db in range(8) for tt in range(TBN // 128)]

                def load(i):
                    db, tt = units[i]
                    K.dma("sp", xr[i % 4][:], y[q, t0 + tt * 128: t0 + (tt + 1) * 128, db * 256:(db + 1) * 256],
                          w=[R_xr[i % 4]], key=f"f_xr{i % 4}")
                load(0)
                load(1)
                for i, (db, tt) in enumerate(units):
                    s0, s1 = (2 * db) % 4, (2 * db + 1) % 4
                    wd0 = wview(wbf[s0], 256)
                    wd1 = wview(wbf[s1], 256)
                    if tt == 0:
                        K.dma("pool", wd0, wd_l[:, 0:22, db * 256:(db + 1) * 256], w=[R_wb[s0]], key=f"f_wb{s0}")
                        K.dma("pool", wd1, wd_l[:, 22:44, db * 256:(db + 1) * 256], w=[R_wb[s1]], key=f"f_wb{s1}")
                    if i + 2 < len(units):
                        load(i + 2)
                    pa = i % 4

                    def mm(e, tt=tt, pa=pa, wd0=wd0, wd1=wd1):
                        for f in range(NFC):
                            wd = wd0 if f < 22 else wd1
                            ins = e.matmul(P[pa][:, 0:256], lhsT=actT[:, f, tt * 128:(tt + 1) * 128], rhs=wd[:, f % 22, :],
                                           start=(f == 0), stop=(f == NFC - 1))
                        return ins
                    K.op("pe", mm, r=R_act + [R_wb[s0], R_wb[s1]], w=[R_P[pa]])
                    K.op("dve", lambda e, i=i, pa=pa: e.tensor_tensor(out=xo[i % 3][:], in0=xr[i % 4][:], in1=P[pa][:, 0:256],
                                                                      op=ALU.add),
                         r=[R_xr[i % 4], R_P[pa]], w=[R_xo[i % 3]])
                    K.dma("sp", y[q, t0 + tt * 128: t0 + (tt + 1) * 128, db * 256:(db + 1) * 256], xo[i % 3][:],
                          r=[R_xo[i % 3]], key=f"f_xo{i % 3}")
            K.flush()

    for L in range(n_layers):
        K.dma("sp", pp[:], pp_d[L], w=[R_pp], key="pp")
        K.op("act", lambda e: e.activation(out=abc[:], in_=pp[:, PP_ALOG:PP_ALOG + 16], func=AF.Exp),
             r=[R_pp], w=[R_abc])
        K.op("dve", lambda e: e.tensor_scalar(out=abc[:], in0=abc[:], scalar1=-1.0, scalar2=None, op0=ALU.mult),
             r=[R_abc], w=[R_abc])
        K.flush()
        for q in range(n_seq):
            xsrc = x_in if L == 0 else y
            if "inproj" in phases:
                phase_inproj(L, q, xsrc)
            with ExitStack() as mst:
                onT = sb(mst, "onT", [128, NH, S], BF16)
                osT = sb(mst, "osT", [128, NH, S], BF16)
                R_onT, R_osT = K.res("onT"), K.res("osT")
                if "attn" in phases:
                    phase_attn(L, q, onT, R_onT)
                if "ssd" in phases:
                    phase_ssd(L, q, osT, R_osT)
                if debug:
                    K.dma("sp", dbg_on, onT[:], r=[R_onT], key="dbg_on")
                    K.dma("sp", dbg_os, osT[:], r=[R_osT], key="dbg_os")
                    K.flush()
                if "outproj" in phases:
                    phase_outproj(L, q, xsrc, onT, R_onT, osT, R_osT)
            if "ffn" in phases:
                phase_ffn(L, q)
    K.flush(final=True)
    es.close()
    return nc, K


def make_consts():
    p = np.arange(128)[:, None]
    f = np.arange(128)[None, :]
    c = np.zeros((128, NCST), np.float32)
    c[:, C_ID:C_ID + 128] = (p == f)
    c[:, C_ONE:C_ONE + 128] = 1.0
    c[:, C_US:C_US + 128] = (p > f)
    c[:, C_TRI:C_TRI + 128] = (p <= f)
    c[:, C_LGE:C_LGE + 128] = (p >= f)
    fa = np.arange(896)[None, :]
    c[:, C_AM:C_AM + 896] = (fa - p > 384)
    return c


def pack_params(inp, n_layers):
    pp = np.zeros((n_layers, 128, NP_), np.float32)
    for l in range(n_layers):
        pp[l, :, PP_NMIX:PP_NMIX + 16] = inp["norm_mix"][l].reshape(16, 128).T
        pp[l, :, PP_NFFN:PP_NFFN + 16] = inp["norm_ffn"][l].reshape(16, 128).T
        pp[l, :, PP_QG] = inp["q_gain"][l]
        pp[l, :, PP_KG] = inp["k_gain"][l]
        cw = inp["conv_w"][l].reshape(4, 12, 128)
        pp[l, :, PP_CW:PP_CW + 48] = cw.transpose(2, 1, 0).reshape(128, 48)
        pp[l, :, PP_CB:PP_CB + 12] = inp["conv_b"][l].reshape(12, 128).T
        pp[l, :, PP_DTB:PP_DTB + 16] = np.broadcast_to(inp["dt_bias"][l][None, :], (128, 16))
        pp[l, :, PP_ALOG:PP_ALOG + 16] = np.broadcast_to(inp["a_log"][l][None, :], (128, 16))
        pp[l, :, PP_DSK:PP_DSK + 16] = np.broadcast_to(inp["d_skip"][l][None, :], (128, 16))
        pp[l, :, PP_AOG:PP_AOG + 8] = inp["attn_out_gain"][l].reshape(8, 128).T
        pp[l, :, PP_SOG:PP_SOG + 1024] = np.broadcast_to(inp["ssm_out_gain"][l][None, :], (128, 1024))
    return pp


_CACHE = {}
N_LAUNCH_LAYERS = 4


def kernel(**inputs):
    inp = {k: np.asarray(v) for k, v in inputs.items()}
    x = np.ascontiguousarray(inp["x"], dtype=np.float32)
    n_cores = 8
    n_seq = x.shape[0] // n_cores
    depth = inp["w_in"].shape[0]
    lpl = N_LAUNCH_LAYERS
    key = (lpl, n_seq)
    if key not in _CACHE:
        _CACHE[key] = build(lpl, n_seq)[0]
    nc = _CACHE[key]
    cst = make_consts()
    pp = pack_params(inp, depth)
    cur = x
    for l0 in range(0, depth, lpl):
        sl = slice(l0, l0 + lpl)
        shared = {"pp": np.ascontiguousarray(pp[sl]), "cst": cst}
        for l in range(lpl):
            for nm in ("w_in", "w_out", "w_gate", "w_up", "w_down"):
                shared[f"{nm}{l}"] = np.ascontiguousarray(inp[nm][l0 + l], dtype=np.float32)
        in_maps = [dict(shared, x=np.ascontiguousarray(cur[c * n_seq:(c + 1) * n_seq])) for c in range(n_cores)]
        res = run_bass_kernel_spmd(nc, in_maps, core_ids=list(range(n_cores)))
        cur = np.concatenate([np.asarray(r["y"]) for r in res.results], axis=0).astype(np.float32)
    return cur
```

```python
from contextlib import ExitStack
import math
import numpy as np
import concourse.bass as bass
import concourse.mybir as mybir
from concourse.bass_utils import run_bass_kernel_spmd

F32, BF16 = mybir.dt.float32, mybir.dt.bfloat16
AF = mybir.ActivationFunctionType
ALU = mybir.AluOpType

S = 2048
D = 2048
NH = 8
DH = 128
DFF = 5632
IN_DIM = 5648
EPS = 1e-6
NT = S // 128
NB = S // 512
NFC = DFF // 128
PP_NMIX, PP_NFFN, PP_QG, PP_KG, PP_CW, PP_CB, PP_DTB, PP_ALOG, PP_DSK, PP_AOG, PP_SOG, NP_ = \
    0, 16, 32, 33, 34, 82, 94, 110, 126, 142, 150, 1174
C_ID, C_ONE, C_US, C_TRI, C_LGE, C_AM, NCST = 0, 128, 256, 384, 512, 640, 1536

ENG = ["pe", "act", "dve", "pool", "sp"]


class Res:
    __slots__ = ("name", "lw", "rd")

    def __init__(self, name):
        self.name = name
        self.lw = None
        self.rd = []


class Op:
    __slots__ = ("eng", "fn", "kind", "key", "need", "val", "deps")


class Sched:
    def __init__(self, nc, es):
        self.nc = nc
        self.sem = {e: es.enter_context(nc.semaphore("s_" + e)) for e in ENG}
        self.cnt = {e: 0 for e in ENG}
        self.seen = {e: {} for e in ENG}
        self.dsem = {}
        self.dcnt = {}
        self.es = es
        self.ops = {e: [] for e in ENG}
        self.pending = {e: [] for e in ENG}
        self.allres = []
        self.dirty = {}
        self.n_inst = 0

    def res(self, name):
        r = Res(name)
        self.allres.append(r)
        return r

    def _mk(self, eng, fn, r, w, kind, key):
        o = Op()
        o.eng, o.fn, o.kind, o.key, o.need, o.val = eng, fn, kind, key, False, None
        deps = {}
        for d in self.pending[eng]:
            deps[d] = True
        self.pending[eng] = []
        for x in r:
            if x.lw is not None:
                deps[x.lw] = True
        for x in w:
            if x.lw is not None and x.lw not in deps:
                deps[x.lw] = False
            for q in x.rd:
                if q not in deps:
                    deps[q] = False
        keep = []
        for d, raw in deps.items():
            if d is o:
                continue
            if kind == "c" and d.kind == "c" and d.eng == eng and not raw:
                continue
            keep.append(d)
            if d.kind != "d":
                d.need = True
        o.deps = keep
        for x in r:
            if x not in w:
                x.rd = [q for q in x.rd if not (q.eng == eng and q.kind == kind and q.key == key)]
                x.rd.append(o)
        for x in w:
            x.lw = o
            x.rd = []
        self.ops[eng].append(o)
        return o

    def op(self, eng, fn, r=(), w=()):
        return self._mk(eng, fn, r, w, "c", None)

    def dma(self, eng, out, in_, r=(), w=(), key=None):
        assert key is not None
        if key not in self.dsem:
            self.dsem[key] = self.es.enter_context(self.nc.semaphore("d_" + key))
            self.dcnt[key] = 0
        o = self._mk(eng, lambda e: e.dma_start(out=out, in_=in_), r, w, "d", key)
        self.dcnt[key] += 16
        o.val = self.dcnt[key]
        self.dirty[key] = o
        return o

    def flush(self, final=False):
        bar = Op()
        bar.eng, bar.fn, bar.kind, bar.key, bar.need, bar.val = "sp", None, "b", None, True, None
        deps = list(self.pending["sp"])
        self.pending["sp"] = []
        for e in ENG:
            if self.ops[e]:
                lo = None
                for o in reversed(self.ops[e]):
                    if o.kind == "c":
                        lo = o
                        break
                if lo is not None:
                    lo.need = True
                    deps.append(lo)
        deps.extend(self.dirty.values())
        self.dirty = {}
        bar.deps = deps
        self.ops["sp"].append(bar)
        for e in ENG:
            if e != "sp":
                self.pending[e].append(bar)
        for r in self.allres:
            r.lw = None
            r.rd = []
        for e in ENG:
            for o in self.ops[e]:
                if o.kind != "d" and o.need:
                    self.cnt[e] += 1
                    o.val = self.cnt[e]
        nc = self.nc
        me = self

        def replay(e, h):
            seen = me.seen[e]
            for o in me.ops[e]:
                for d in o.deps:
                    if d.kind == "d":
                        sem = me.dsem[d.key]
                    else:
                        sem = me.sem[d.eng]
                    if seen.get(sem.name, 0) >= d.val:
                        continue
                    seen[sem.name] = d.val
                    h.wait_ge(sem, d.val)
                    me.n_inst += 1
                if o.kind == "b":
                    h.sem_inc(me.sem["sp"], 1)
                    continue
                ins = o.fn(h)
                me.n_inst += 1
                if o.kind == "d":
                    ins.then_inc(me.dsem[o.key], 16)
                elif o.need:
                    ins.then_inc(me.sem[e], 1)
            if final and e != "sp":
                for d in me.pending[e]:
                    h.wait_ge(me.sem["sp"], d.val)

        with nc.Block() as block:
            @block.tensor
            def _(h):
                replay("pe", h)

            @block.scalar
            def _(h):
                replay("act", h)

            @block.vector
            def _(h):
                replay("dve", h)

            @block.gpsimd
            def _(h):
                replay("pool", h)

            @block.sync
            def _(h):
                replay("sp", h)
        self.ops = {e: [] for e in ENG}


ALL_PHASES = ('inproj', 'attn', 'ssd', 'outproj', 'ffn')


def build(n_layers, n_seq, debug=False, phases=ALL_PHASES):
    nc = bass.Bass("TRN2", target_bir_lowering=False)

    def dram(name, shape, dt, kind="Internal"):
        return nc.dram_tensor(name, shape, dt, kind=kind).ap()

    sk = "ExternalOutput" if debug else "Internal"
    x_in = dram("x", [n_seq, S, D], F32, "ExternalInput")
    y = dram("y", [n_seq, S, D], F32, "ExternalOutput")
    w_in = [dram(f"w_in{l}", [D, IN_DIM], F32, "ExternalInput") for l in range(n_layers)]
    w_out = [dram(f"w_out{l}", [D, D], F32, "ExternalInput") for l in range(n_layers)]
    w_gate = [dram(f"w_gate{l}", [D, DFF], F32, "ExternalInput") for l in range(n_layers)]
    w_up = [dram(f"w_up{l}", [D, DFF], F32, "ExternalInput") for l in range(n_layers)]
    w_down = [dram(f"w_down{l}", [DFF, D], F32, "ExternalInput") for l in range(n_layers)]
    pp_d = dram("pp", [n_layers, 128, NP_], F32, "ExternalInput")
    cst_d = dram("cst", [128, NCST], F32, "ExternalInput")
    qT_s = dram("qT_s", [NH, 128, S], BF16, sk)
    kT_s = dram("kT_s", [NH, 128, S], BF16, sk)
    v_s = dram("v_s", [S, 1024], BF16, sk)
    sz_s = dram("sz_s", [S, 1024], F32, sk)
    dt_s = dram("dt_s", [S, 16], F32, sk)
    xs_s = dram("xs_s", [S, 1024], F32, sk)
    BT_s = dram("BT_s", [2, 128, S], BF16, sk)
    CT_s = dram("CT_s", [2, 128, S], BF16, sk)
    Bt_s = dram("Bt_s", [S, 256], BF16, sk)

    if debug:
        dbg_on = dram("dbg_on", [128, NH, S], BF16, sk)
        dbg_os = dram("dbg_os", [128, NH, S], BF16, sk)
    es = ExitStack()
    K = Sched(nc, es)

    uniq = [0]

    def sb(stack, name, shape, dt):
        uniq[0] += 1
        return stack.enter_context(nc.sbuf_tensor(f"sb{uniq[0]}_{name}", shape, dt))

    def ps(stack, name, shape, dt):
        uniq[0] += 1
        return stack.enter_context(nc.psum_tensor(f"ps{uniq[0]}_{name}", shape, dt))

    cst = sb(es, "cst", [128, NCST], F32)
    cbf = sb(es, "cbf", [128, 384], BF16)
    pp = sb(es, "pp", [128, NP_], F32)
    abc = sb(es, "abc", [128, 16], F32)
    epsb = sb(es, "epsb", [128, 2], F32)
    R_cst, R_cbf, R_pp, R_abc = K.res("cst"), K.res("cbf"), K.res("pp"), K.res("abc")
    ident = cst[:, C_ID:C_ID + 128]
    ones = cst[:, C_ONE:C_ONE + 128]
    ustr = cst[:, C_US:C_US + 128]
    tri = cst[:, C_TRI:C_TRI + 128]
    ident_bf = cbf[:, 0:128]
    nlge_bf = cbf[:, 128:256]
    nones_bf = cbf[:, 256:384]

    K.dma("sp", cst[:], cst_d[:, :], w=[R_cst], key="cst")
    K.op("act", lambda e: e.activation(out=cbf[:, 0:128], in_=cst[:, C_ID:C_ID + 128], func=AF.Copy),
         r=[R_cst], w=[R_cbf])
    K.op("act", lambda e: e.activation(out=cbf[:, 128:256], in_=cst[:, C_LGE:C_LGE + 128], func=AF.Copy, scale=-1.0),
         r=[R_cst], w=[R_cbf])
    K.op("act", lambda e: e.activation(out=cbf[:, 256:384], in_=cst[:, C_ONE:C_ONE + 128], func=AF.Copy, scale=-1.0),
         r=[R_cst], w=[R_cbf])
    K.op("dve", lambda e: e.memset(epsb[:, 0:1], EPS), w=[R_abc])
    K.op("dve", lambda e: e.memset(epsb[:, 1:2], math.log(DH ** -0.5)), w=[R_abc])
    K.flush()

    P = [ps(es, f"P{i}", [128, 512], F32) for i in range(8)]
    R_P = [K.res(f"P{i}") for i in range(8)]
    Pb = [p[:].bitcast(BF16) for p in P]

    def wview(flat, n):
        c = 8192 // n if n == 512 else 22
        return flat[:, 0:c * n].rearrange("p (c n) -> p c n", n=n)

    def emit_norm(stack, src, ntile, hT, R_hT, gcol, banks):
        NBUF = 2
        xt = [sb(stack, f"n_xt{i}", [128, D], F32) for i in range(NBUF)]
        R_xt = [K.res(f"n_xt{i}") for i in range(NBUF)]
        xn = [sb(stack, f"n_xn{i}", [128, D], BF16) for i in range(NBUF)]
        R_xn = [K.res(f"n_xn{i}") for i in range(NBUF)]
        junk = sb(stack, "n_junk", [128, D], BF16)
        R_junk = K.res("n_junk")
        st = [sb(stack, f"n_st{i}", [128, 4], F32) for i in range(NBUF)]
        R_st = [K.res(f"n_st{i}") for i in range(NBUF)]
        for j in range(ntile):
            b = j % NBUF
            K.dma("sp", xt[b][:], src(j), w=[R_xt[b]], key=f"n_xt{b}")
            K.op("act", lambda e, b=b: e.activation(out=junk[:], in_=xt[b][:], func=AF.Square,
                                                     accum_out=st[b][:, 0:1]),
                 r=[R_xt[b]], w=[R_junk, R_st[b]])
            K.op("act", lambda e, b=b: e.activation(out=st[b][:, 1:2], in_=st[b][:, 0:1], func=AF.Ln,
                                                     scale=1.0 / D, bias=epsb[:, 0:1]),
                 r=[R_st[b]], w=[R_st[b]])
            K.op("act", lambda e, b=b: e.activation(out=st[b][:, 2:3], in_=st[b][:, 1:2], func=AF.Exp, scale=-0.5),
                 r=[R_st[b]], w=[R_st[b]])
            K.op("dve", lambda e, b=b: e.tensor_scalar(out=xn[b][:], in0=xt[b][:], scalar1=st[b][:, 2:3],
                                                       scalar2=None, op0=ALU.mult),
                 r=[R_xt[b], R_st[b]], w=[R_xn[b]])
            for c2 in range(2):
                pb = banks[(j * 2 + c2) % len(banks)]
                tpv = Pb[pb].rearrange("p (i t) -> p i t", t=128)

                def tr(e, b=b, c2=c2, tpv=tpv):
                    for i in range(8):
                        c = c2 * 8 + i
                        ins = e.transpose(tpv[:, i, :], xn[b][:, c * 128:(c + 1) * 128], ident_bf)
                    return ins
                K.op("pe", tr, r=[R_xn[b], R_cbf], w=[R_P[pb]])

                def ev(e, c2=c2, tpv=tpv, j=j):
                    for i in range(8):
                        c = c2 * 8 + i
                        ins = e.tensor_scalar(out=hT[:, c, j * 128:(j + 1) * 128], in0=tpv[:, i, :],
                                              scalar1=pp[:, gcol + c:gcol + c + 1], scalar2=None, op0=ALU.mult)
                    return ins
                def ev_act(e, c2=c2, tpv=tpv, j=j):
                    for i in range(8):
                        c = c2 * 8 + i
                        ins = e.activation(out=hT[:, c, j * 128:(j + 1) * 128], in_=tpv[:, i, :], func=AF.Copy,
                                           scale=pp[:, gcol + c:gcol + c + 1])
                    return ins
                K.op("dve", ev, r=[R_P[pb], R_pp], w=[R_hT[j]])

    def phase_inproj(L, q, xsrc):
        with ExitStack() as st:
            hT = sb(st, "hT", [128, 16, S], BF16)
            R_hT = [K.res(f"hT{j}") for j in range(NT)]
            wbf = [sb(st, f"wb{i}", [128, 8192], BF16) for i in range(3)]
            R_wb = [K.res(f"wb{i}") for i in range(3)]
            wdt = sb(st, "wdt", [128, 16, 16], BF16)
            R_wdt = K.res("wdt")
            w_l = w_in[L].rearrange("(c p) n -> p c n", p=128)
            tiles = [("v", 0), ("v", 1), ("z", 0), ("z", 1), ("q", 0), ("q", 1), ("k", 0), ("k", 1),
                     ("x", 0), ("x", 1), ("x", 2)]
            col0 = {"q": 0, "k": 1024, "v": 2048, "z": 3072, "x": 4096}

            def wload(ti):
                kind, w = tiles[ti]
                slot = ti % 3
                K.dma("pool", wview(wbf[slot], 512), w_l[:, :, col0[kind] + w * 512: col0[kind] + (w + 1) * 512],
                      w=[R_wb[slot]], key=f"wb{slot}")
            K.dma("pool", wdt[:], w_l[:, :, 5632:5648], w=[R_wdt], key="wdt")
            for ti in range(3):
                wload(ti)
            with ExitStack() as st2:
                emit_norm(st2, lambda j: xsrc[q, j * 128:(j + 1) * 128, :], NT, hT, R_hT, PP_NMIX, [6, 7])
                K.flush()
            vst = [sb(st, f"vst{i}", [128, 512], BF16) for i in range(2)]
            R_vst = [K.res(f"vst{i}") for i in range(2)]
            zst = [sb(st, f"zst{i}", [128, 512], F32) for i in range(2)]
            R_zst = [K.res(f"zst{i}") for i in range(2)]
            sqt = [sb(st, f"sqt{i}", [128, 512], BF16) for i in range(2)]
            R_sqt = [K.res(f"sqt{i}") for i in range(2)]
            rt = [sb(st, f"rt{i}", [128, 512], F32) for i in range(2)]
            R_rt = [K.res(f"rt{i}") for i in range(2)]
            qst = [sb(st, f"qst{i}", [128, 512], BF16) for i in range(2)]
            R_qst = [K.res(f"qst{i}") for i in range(2)]
            craw = [sb(st, f"craw{i}", [128, S + 3], F32) for i in range(2)]
            R_craw = [K.res(f"craw{i}") for i in range(2)]
            ct = [sb(st, f"ct{i}", [128, S], F32) for i in range(2)]
            R_ct = [K.res(f"ct{i}") for i in range(2)]
            xsT = [sb(st, f"xsT{i}", [128, S], F32) for i in range(1)]
            R_xsT = [K.res(f"xsT{i}") for i in range(1)]
            bcT = [sb(st, f"bcT{i}", [128, S], BF16) for i in range(2)]
            R_bcT = [K.res(f"bcT{i}") for i in range(2)]
            xst = sb(st, "xst", [128, 16, 128], F32)
            R_xst = K.res("xst")
            bst = sb(st, "bst", [128, 16, 128], BF16)
            R_bst = K.res("bst")
            dtw = sb(st, "dtw", [128, 3, 256], F32)
            R_dtw = K.res("dtw")
            for i in range(2):
                K.op("dve", lambda e, i=i: e.memset(craw[i][:, 0:3], 0.0), w=[R_craw[i]])

            pdt = P[4][:, 0:256].rearrange("p (t h) -> p t h", h=16)
            for tt in range(NT):
                def mm(e, tt=tt):
                    for c in range(16):
                        ins = e.matmul(pdt[:, tt, :], lhsT=hT[:, c, tt * 128:(tt + 1) * 128], rhs=wdt[:, c, :],
                                       start=(c == 0), stop=(c == 15))
                    return ins
                K.op("pe", mm, r=[R_hT[tt], R_wdt], w=[R_P[4]])
            dtv = [dtw[:, i, :].rearrange("p (t h) -> p t h", h=16) for i in range(3)]
            K.op("dve", lambda e: e.tensor_tensor(out=dtv[0], in0=pdt,
                                                  in1=pp[:, None, PP_DTB:PP_DTB + 16].to_broadcast([128, 16, 16]),
                                                  op=ALU.add), r=[R_P[4], R_pp], w=[R_dtw])
            K.op("act", lambda e: e.activation(out=dtw[:, 1, :], in_=dtw[:, 0, :], func=AF.Exp), r=[R_dtw], w=[R_dtw])
            K.op("act", lambda e: e.activation(out=dtw[:, 2, :], in_=dtw[:, 1, :], func=AF.Ln, bias=1.0),
                 r=[R_dtw], w=[R_dtw])
            K.dma("sp", dt_s.rearrange("(t p) h -> p t h", p=128), dtv[2], r=[R_dtw], key="dtw")

            state = {"u": 0, "ev": 0}
            dq = []

            def defer(fn, delay):
                dq.append([delay, fn])

            def run_deferred(all_=False):
                for it in list(dq):
                    it[0] -= 1
                    if it[0] <= 0 or all_:
                        dq.remove(it)
                        it[1]()

            for ti, (kind, w) in enumerate(tiles):
                slot = ti % 3
                wv = wview(wbf[slot], 512)
                if ti >= 3:
                    wload(ti)
                if kind in ("v", "z"):
                    for tt in range(NT):
                        u = state["u"]; state["u"] += 1
                        pa = u % 4

                        def mm(e, tt=tt, pa=pa, wv=wv):
                            for c in range(16):
                                ins = e.matmul(P[pa][:], lhsT=hT[:, c, tt * 128:(tt + 1) * 128], rhs=wv[:, c, :],
                                               start=(c == 0), stop=(c == 15))
                            return ins
                        K.op("pe", mm, r=[R_hT[tt], R_wb[slot]], w=[R_P[pa]])
                        run_deferred()
                        k2 = u % 2
                        if kind == "v":
                            K.op("act", lambda e, pa=pa, k2=k2: e.activation(out=vst[k2][:], in_=P[pa][:], func=AF.Copy),
                                 r=[R_P[pa]], w=[R_vst[k2]])
                            K.dma("sp", v_s[tt * 128:(tt + 1) * 128, w * 512:(w + 1) * 512], vst[k2][:],
                                  r=[R_vst[k2]], key=f"vst{k2}")
                        else:
                            K.op("act", lambda e, pa=pa, k2=k2: e.activation(out=zst[k2][:], in_=P[pa][:], func=AF.Silu),
                                 r=[R_P[pa]], w=[R_zst[k2]])
                            K.dma("sp", sz_s[tt * 128:(tt + 1) * 128, w * 512:(w + 1) * 512], zst[k2][:],
                                  r=[R_zst[k2]], key=f"zst{k2}")
                else:
                    for j in range(4):
                        cj = w * 4 + j
                        for tb in range(NB):
                            u = state["u"]; state["u"] += 1
                            pa = u % 4

                            def mm(e, j=j, tb=tb, pa=pa, wv=wv):
                                for c in range(16):
                                    ins = e.matmul(P[pa][:], lhsT=wv[:, c, j * 128:(j + 1) * 128],
                                                   rhs=hT[:, c, tb * 512:(tb + 1) * 512],
                                                   start=(c == 0), stop=(c == 15))
                                return ins
                            K.op("pe", mm, r=[R_hT[4 * tb + i] for i in range(4)] + [R_wb[slot]], w=[R_P[pa]])
                            run_deferred()
                            if kind in ("q", "k"):
                                k2 = state["ev"] % 2; state["ev"] += 1
                                K.op("act", lambda e, pa=pa, k2=k2: e.activation(out=sqt[k2][:], in_=P[pa][:],
                                                                                  func=AF.Square),
                                     r=[R_P[pa]], w=[R_sqt[k2]])

                                def rest(kind=kind, cj=cj, tb=tb, pa=pa, k2=k2):
                                    pn = 4 + k2
                                    K.op("pe", lambda e: e.matmul(P[pn][:], lhsT=nones_bf, rhs=sqt[k2][:],
                                                                  start=True, stop=True),
                                         r=[R_sqt[k2], R_cbf], w=[R_P[pn]])
                                    K.op("act", lambda e: e.activation(out=rt[k2][:], in_=P[pn][:], func=AF.Ln,
                                                                       scale=-1.0 / DH, bias=epsb[:, 0:1]),
                                         r=[R_P[pn]], w=[R_rt[k2]])
                                    if kind == "q":
                                        K.op("act", lambda e: e.activation(out=rt[k2][:], in_=rt[k2][:], func=AF.Exp,
                                                                           scale=-0.5, bias=epsb[:, 1:2]),
                                             r=[R_rt[k2]], w=[R_rt[k2]])
                                    else:
                                        K.op("act", lambda e: e.activation(out=rt[k2][:], in_=rt[k2][:], func=AF.Exp,
                                                                           scale=-0.5),
                                             r=[R_rt[k2]], w=[R_rt[k2]])
                                    gc = PP_QG if kind == "q" else PP_KG
                                    K.op("dve", lambda e: e.scalar_tensor_tensor(
                                        out=qst[k2][:], in0=P[pa][:], scalar=pp[:, gc:gc + 1], in1=rt[k2][:],
                                        op0=ALU.mult, op1=ALU.mult), r=[R_P[pa], R_rt[k2], R_pp], w=[R_qst[k2]])
                                    dst = (qT_s if kind == "q" else kT_s)[cj, :, tb * 512:(tb + 1) * 512]
                                    K.dma("sp", dst, qst[k2][:], r=[R_qst[k2]], key=f"qst{k2}")
                                defer(rest, 1)
                            else:
                                cb = cj % 2
                                K.op("act", lambda e, pa=pa, cb=cb, tb=tb: e.activation(
                                    out=craw[cb][:, 3 + tb * 512: 3 + (tb + 1) * 512], in_=P[pa][:], func=AF.Copy),
                                    r=[R_P[pa]], w=[R_craw[cb]])
                                if tb == NB - 1:
                                    def conv(cj=cj, cb=cb):
                                        wc = PP_CW + cj * 4
                                        pe_part = []
                                        K.op("dve", lambda e: e.tensor_scalar(
                                            out=ct[cb][:], in0=craw[cb][:, 0:S], scalar1=pp[:, wc:wc + 1],
                                            scalar2=pp[:, PP_CB + cj:PP_CB + cj + 1], op0=ALU.mult, op1=ALU.add),
                                            r=[R_craw[cb], R_pp], w=[R_ct[cb]])
                                        for i in range(1, 4):
                                            K.op("dve", lambda e, i=i: e.scalar_tensor_tensor(
                                                out=ct[cb][:], in0=craw[cb][:, i:i + S], scalar=pp[:, wc + i:wc + i + 1],
                                                in1=ct[cb][:], op0=ALU.mult, op1=ALU.add),
                                                r=[R_craw[cb], R_pp, R_ct[cb]], w=[R_ct[cb]])
                                        if cj < 8:
                                            K.op("act", lambda e: e.activation(out=xsT[0][:], in_=ct[cb][:], func=AF.Silu),
                                                 r=[R_ct[cb]], w=[R_xsT[0]])

                                            def xs_pe():
                                              for t4 in range(4):
                                                pb = 6 + (t4 % 2)
                                                tv = P[pb][:].rearrange("p (i t) -> p i t", t=128)

                                                def tr(e, t4=t4, tv=tv):
                                                    for i in range(4):
                                                        tt = t4 * 4 + i
                                                        ins = e.transpose(tv[:, i, :], xsT[0][:, tt * 128:(tt + 1) * 128],
                                                                          ident)
                                                    return ins
                                                K.op("pe", tr, r=[R_xsT[0], R_cst], w=[R_P[pb]])
                                                K.op("act", lambda e, t4=t4, tv=tv: e.activation(
                                                    out=xst[:, t4 * 4:(t4 + 1) * 4, :], in_=tv, func=AF.Copy),
                                                    r=[R_P[pb]], w=[R_xst])
                                              K.dma("sp", xs_s.rearrange("(t p) c -> p t c", p=128)[:, :, cj * 128:(cj + 1) * 128],
                                                    xst[:], r=[R_xst], key="xst")
                                            defer(xs_pe, 3)
                                        else:
                                            g = (cj - 8) % 2
                                            isB = cj < 10
                                            bb = cj % 2
                                            K.op("act", lambda e: e.activation(out=bcT[bb][:], in_=ct[cb][:], func=AF.Silu),
                                                 r=[R_ct[cb]], w=[R_bcT[bb]])
                                            K.dma("sp", (BT_s if isB else CT_s)[g], bcT[bb][:], r=[R_bcT[bb]], key=f"bcT{bb}")
                                            def b_pe():
                                                for t8 in range(2):
                                                    pb = 6 + t8
                                                    tv = Pb[pb].rearrange("p (i t) -> p i t", t=128)

                                                    def tr(e, t8=t8, tv=tv):
                                                        for i in range(8):
                                                            tt = t8 * 8 + i
                                                            ins = e.transpose(tv[:, i, :], bcT[bb][:, tt * 128:(tt + 1) * 128],
                                                                              ident_bf)
                                                        return ins
                                                    K.op("pe", tr, r=[R_bcT[bb], R_cbf], w=[R_P[pb]])
                                                    K.op("act", lambda e, t8=t8, tv=tv: e.activation(
                                                        out=bst[:, t8 * 8:(t8 + 1) * 8, :], in_=tv, func=AF.Copy),
                                                        r=[R_P[pb]], w=[R_bst])
                                                K.dma("sp", Bt_s.rearrange("(t p) c -> p t c", p=128)[:, :, g * 128:(g + 1) * 128],
                                                      bst[:], r=[R_bst], key="bst")
                                            if isB:
                                                defer(b_pe, 3)
                                    conv()
            run_deferred(all_=True)
        K.flush()

    def phase_attn(L, q, onT, R_onT):
        with ExitStack() as st:
            oT = sb(st, "oT", [128, NH, S], F32)
            R_oT = [[K.res(f"oT{h}_{tb}") for tb in range(NB)] for h in range(NH)]
            qb = [sb(st, f"qb{i}", [128, S], BF16) for i in range(2)]
            kb_ = [sb(st, f"kb{i}", [128, S], BF16) for i in range(2)]
            vb = [sb(st, f"vb{i}", [128, NT, 128], BF16) for i in range(2)]
            R_qb = [K.res(f"qb{i}") for i in range(2)]
            R_kb = [K.res(f"kb{i}") for i in range(2)]
            R_vb = [K.res(f"vb{i}") for i in range(2)]
            et = [sb(st, f"et{i}", [128, 512], F32) for i in range(2)]
            R_et = [K.res(f"et{i}") for i in range(2)]
            spt = [sb(st, f"spt{i}", [128, 512], BF16) for i in range(4)]
            R_spt = [K.res(f"spt{i}") for i in range(4)]
            spf = [sb(st, f"spf{i}", [128, 512], F32) for i in range(2)]
            R_spf = [K.res(f"spf{i}") for i in range(2)]
            Wt = [sb(st, f"Wt{i}", [128, 512], BF16) for i in range(3)]
            R_Wt = [K.res(f"Wt{i}") for i in range(3)]
            Wf = [sb(st, f"Wf{i}", [128, 512], F32) for i in range(2)]
            R_Wf = [K.res(f"Wf{i}") for i in range(2)]
            acc = [sb(st, f"acc{i}", [128, 512], BF16) for i in range(2)]
            R_acc = [K.res(f"acc{i}") for i in range(2)]
            sqb = [sb(st, f"sqb{i}", [128, 512], BF16) for i in range(2)]
            R_sqb = [K.res(f"sqb{i}") for i in range(2)]
            rtn = sb(st, "rtn", [128, 512], F32)
            R_rtn = K.res("rtn")
            vview = v_s.rearrange("(t p) c -> p t c", p=128)
            for i in range(2):
                K.op("dve", lambda e, i=i: e.memset(spf[i][:], 0.0), w=[R_spf[i]])
                K.op("dve", lambda e, i=i: e.memset(Wf[i][:], 0.0), w=[R_Wf[i]])

            def mask_ap(o):
                return cst[:, C_AM + (3 - o) * 128: C_AM + (3 - o) * 128 + 512]

            def load_head(h):
                b = h % 2
                K.dma("sp", qb[b][:], qT_s[h], w=[R_qb[b]], key=f"qb{b}")
                K.dma("sp", kb_[b][:], kT_s[h], w=[R_kb[b]], key=f"kb{b}")
                K.dma("sp", vb[b][:], vview[:, :, h * 128:(h + 1) * 128], w=[R_vb[b]], key=f"vb{b}")

            tiles = []
            for h in range(NH):
                for tb in range(NB):
                    unit = h * NB + tb
                    kbs = list(range(4 * tb + 3, -1, -1))
                    U = {"h": h, "tb": tb, "b": h % 2, "ob": 3 + unit % 2, "n": len(kbs), "accsrc": {}}
                    for i, kb in enumerate(kbs):
                        tiles.append((U, i, kb, len(tiles)))

            def stageA(T):
                U, i, kb, g = T
                tb, b, accsrc = U["tb"], U["b"], U["accsrc"]
                zi = g % 3
                diag = kb >= 4 * tb
                o = kb - 4 * tb
                K.op("pe", lambda e: e.matmul(P[zi][:], lhsT=kb_[b][:, kb * 128:(kb + 1) * 128],
                                              rhs=qb[b][:, tb * 512:(tb + 1) * 512], start=True, stop=True),
                     r=[R_kb[b], R_qb[b]], w=[R_P[zi]])
                c0 = 128 * o if diag else 0
                K.op("act", lambda e: e.activation(out=et[g % 2][:, c0:512], in_=P[zi][:, c0:512], func=AF.Exp),
                     r=[R_P[zi]], w=[R_et[g % 2]])
                if diag:
                    K.op("act", lambda e: e.activation(out=spf[g % 2][:, c0:512], in_=et[g % 2][:, c0:512],
                                                       func=AF.Ln, bias=1.0),
                         r=[R_et[g % 2]], w=[R_spf[g % 2]])
                    K.op("dve", lambda e: e.tensor_tensor(out=spt[g % 4][:], in0=spf[g % 2][:], in1=mask_ap(o),
                                                          op=ALU.mult),
                         r=[R_spf[g % 2], R_cst], w=[R_spt[g % 4]])
                else:
                    K.op("act", lambda e: e.activation(out=spt[g % 4][:], in_=et[g % 2][:], func=AF.Ln, bias=1.0),
                         r=[R_et[g % 2]], w=[R_spt[g % 4]])
                if i == 1:
                    accsrc[1] = (spt[(g - 1) % 4], R_spt[(g - 1) % 4])
                elif i >= 2:
                    prev, R_prev = accsrc[i - 1]
                    a = i % 2
                    K.op("pool", lambda e: e.tensor_tensor(out=acc[a][:], in0=prev[:], in1=spt[(g - 1) % 4][:],
                                                           op=ALU.add),
                         r=[R_prev, R_spt[(g - 1) % 4]], w=[R_acc[a]])
                    accsrc[i] = (acc[a], R_acc[a])

            def stageB(T):
                U, i, kb, g = T
                tb, accsrc = U["tb"], U["accsrc"]
                zi = g % 3
                diag = kb >= 4 * tb
                o = kb - 4 * tb

                def mm(e):
                    ins = e.matmul(P[zi][:], lhsT=nlge_bf, rhs=spt[g % 4][:], start=False, stop=(i == 0),
                                   skip_group_check=True)
                    if i >= 1:
                        ins = e.matmul(P[zi][:], lhsT=nones_bf, rhs=accsrc[i][0][:], start=False, stop=True,
                                       skip_group_check=True)
                    return ins
                rr = [R_spt[g % 4], R_cbf] + ([accsrc[i][1]] if i >= 1 else [])
                K.op("pe", mm, r=rr, w=[R_P[zi]])
                if diag:
                    c0 = 128 * o
                    K.op("act", lambda e: e.activation(out=Wf[g % 2][:, c0:512], in_=P[zi][:, c0:512], func=AF.Exp),
                         r=[R_P[zi]], w=[R_Wf[g % 2]])
                    K.op("dve", lambda e: e.tensor_tensor(out=Wt[g % 3][:], in0=Wf[g % 2][:], in1=mask_ap(o),
                                                          op=ALU.mult),
                         r=[R_Wf[g % 2], R_cst], w=[R_Wt[g % 3]])
                else:
                    K.op("act", lambda e: e.activation(out=Wt[g % 3][:], in_=P[zi][:], func=AF.Exp),
                         r=[R_P[zi]], w=[R_Wt[g % 3]])

            def stageC(T):
                U, i, kb, g = T
                h, tb, b, ob, n = U["h"], U["tb"], U["b"], U["ob"], U["n"]
                K.op("pe", lambda e: e.matmul(P[ob][:], lhsT=vb[b][:, kb, :], rhs=Wt[g % 3][:],
                                              start=(i == 0), stop=(i == n - 1), skip_group_check=True),
                     r=[R_vb[b], R_Wt[g % 3]], w=[R_P[ob]])
                if i == n - 1:
                    K.op("dve", lambda e: e.tensor_copy(out=oT[:, h, tb * 512:(tb + 1) * 512], in_=P[ob][:]),
                         r=[R_P[ob]], w=[R_oT[h][tb]])

            NTL = len(tiles)
            load_head(0)
            load_head(1)
            for step in range(NTL + 2):
                if step < NTL:
                    U, i, kb, g = tiles[step]
                    if U["tb"] == 0 and i == 3 and 1 <= U["h"] and U["h"] + 1 < NH:
                        load_head(U["h"] + 1)
                    stageA(tiles[step])
                if 1 <= step <= NTL:
                    stageB(tiles[step - 1])
                if step >= 2:
                    stageC(tiles[step - 2])
            def do_norm(tb):
                pn = 5 + tb % 2
                for h in range(NH):
                    k2 = (tb * NH + h) % 2
                    K.op("act", lambda e, h=h, k2=k2: e.activation(out=sqb[k2][:], in_=oT[:, h, tb * 512:(tb + 1) * 512],
                                                                    func=AF.Square),
                         r=[R_oT[h][tb]], w=[R_sqb[k2]])
                    K.op("pe", lambda e, h=h, k2=k2: e.matmul(P[pn][:], lhsT=nones_bf, rhs=sqb[k2][:], start=(h == 0),
                                                               stop=(h == NH - 1), skip_group_check=True),
                         r=[R_sqb[k2], R_cbf], w=[R_P[pn]])
                K.op("act", lambda e: e.activation(out=rtn[:], in_=P[pn][:], func=AF.Ln, scale=-1.0 / 1024,
                                                   bias=epsb[:, 0:1]), r=[R_P[pn]], w=[R_rtn])
                K.op("act", lambda e: e.activation(out=rtn[:], in_=rtn[:], func=AF.Exp, scale=-0.5),
                     r=[R_rtn], w=[R_rtn])
                for h in range(NH):
                    K.op("dve", lambda e, h=h: e.scalar_tensor_tensor(
                        out=onT[:, h, tb * 512:(tb + 1) * 512], in0=oT[:, h, tb * 512:(tb + 1) * 512],
                        scalar=pp[:, PP_AOG + h:PP_AOG + h + 1], in1=rtn[:], op0=ALU.mult, op1=ALU.mult),
                        r=[R_oT[h][tb], R_rtn, R_pp], w=[R_onT])
            for tb in range(NB):
                do_norm(tb)
        K.flush()

    def phase_ssd(L, q, osT, R_osT):
        with ExitStack() as st:
            def dbl(name, shape, dt):
                return ([sb(st, f"{name}{i}", shape, dt) for i in range(2)], [K.res(f"{name}{i}") for i in range(2)])
            xs, R_xs = dbl("s_xs", [128, 1024], F32)
            szt, R_sz = dbl("s_sz", [128, 1024], F32)
            dtt, R_dt = dbl("s_dt", [128, 16], F32)
            BT, R_BT = dbl("s_BT", [128, 2, 128], BF16)
            CT, R_CT = dbl("s_CT", [128, 2, 128], BF16)
            Bt, R_Bt = dbl("s_Bt", [128, 256], BF16)
            da, R_da = dbl("s_da", [128, 16], F32)
            ex, R_ex = dbl("s_ex", [128, 64], F32)
            Dm, R_Dm = dbl("s_Dm", [128, 16, 128], F32)
            dec, R_dec = dbl("s_dec", [128, 16, 128], F32)
            cbm, R_cbm = dbl("s_cbm", [128, 2, 128], F32)
            MT, R_MT = dbl("s_MT", [128, 16, 128], BF16)
            xdt, R_xdt = dbl("s_xdt", [128, 16, 64], BF16)
            xde, R_xde = dbl("s_xde", [128, 16, 64], BF16)
            hSb, _ = dbl("s_hSb", [128, 1024], BF16)
            R_hSb = [[K.res(f"s_hSb{i}_{g}") for g in range(2)] for i in range(2)]
            t1, R_t1 = dbl("s_t1", [128, 1024], F32)
            t2, R_t2 = dbl("s_t2", [128, 1024], F32)
            yz, R_yz = dbl("s_yz", [128, 1024], F32)
            yd, R_yd = dbl("s_yd", [128, 1024], F32)
            osm, R_osm = dbl("s_osm", [128, 1024], BF16)
            ssq, R_ssq = dbl("s_ssq", [128, 8], F32)
            hS = sb(st, "s_hS", [128, 1024], F32)
            R_hS = [K.res(f"s_hS{g}") for g in range(2)]
            junk = sb(st, "s_junk", [128, 512], BF16)
            R_junk = K.res("s_junk")
            K.op("dve", lambda e: e.memset(hS[:], 0.0), w=R_hS)
            K.op("dve", lambda e: e.memset(hSb[0][:], 0.0), w=R_hSb[0])

            def S1(c):
                b = c % 2
                tok = slice(c * 128, (c + 1) * 128)
                K.dma("sp", xs[b][:], xs_s[tok, :], w=[R_xs[b]], key=f"s_xs{b}")
                K.dma("sp", dtt[b][:], dt_s[tok, :], w=[R_dt[b]], key=f"s_dt{b}")
                K.dma("sp", BT[b][:], BT_s[:, :, tok].rearrange("g n t -> n g t"), w=[R_BT[b]], key=f"s_BT{b}")
                K.dma("sp", CT[b][:], CT_s[:, :, tok].rearrange("g n t -> n g t"), w=[R_CT[b]], key=f"s_CT{b}")
                K.dma("sp", Bt[b][:], Bt_s[tok, :], w=[R_Bt[b]], key=f"s_Bt{b}")
                K.op("dve", lambda e: e.tensor_tensor(out=da[b][:], in0=dtt[b][:], in1=abc[:], op=ALU.mult),
                     r=[R_dt[b], R_abc], w=[R_da[b]])

                def mm(e):
                    e.matmul(P[0][:, 0:16], lhsT=ustr, rhs=da[b][:], start=True, stop=True)
                    e.matmul(P[0][:, 16:32], lhsT=tri, rhs=da[b][:], start=True, stop=True)
                    return e.matmul(P[0][:, 32:48], lhsT=ones, rhs=da[b][:], start=True, stop=True)
                K.op("pe", mm, r=[R_da[b], R_cst], w=[R_P[0]])
                K.op("act", lambda e: e.activation(out=ex[b][:, 0:48], in_=P[0][:, 0:48], func=AF.Exp),
                     r=[R_P[0]], w=[R_ex[b]])
                K.op("dve", lambda e: e.tensor_tensor(out=ex[b][:, 48:64], in0=ex[b][:, 0:16], in1=dtt[b][:],
                                                      op=ALU.mult), r=[R_ex[b], R_dt[b]], w=[R_ex[b]])
                K.op("dve", lambda e: e.tensor_tensor(out=Dm[b][:], in0=ustr[:, None, :].to_broadcast([128, 16, 128]),
                                                      in1=da[b][:, :, None].to_broadcast([128, 16, 128]), op=ALU.mult),
                     r=[R_cst, R_da[b]], w=[R_Dm[b]])
                cv = P[0][:, 128:384].rearrange("p (g l) -> p g l", l=128)

                def mmc(e):
                    e.matmul(cv[:, 0, :], lhsT=BT[b][:, 0, :], rhs=CT[b][:, 0, :], start=True, stop=True)
                    return e.matmul(cv[:, 1, :], lhsT=BT[b][:, 1, :], rhs=CT[b][:, 1, :], start=True, stop=True)
                K.op("pe", mmc, r=[R_BT[b], R_CT[b]], w=[R_P[0]])
                K.op("dve", lambda e: e.tensor_tensor(out=cbm[b][:], in0=cv,
                                                      in1=tri[:, None, :].to_broadcast([128, 2, 128]), op=ALU.mult),
                     r=[R_P[0], R_cst], w=[R_cbm[b]])
                for k in range(4):
                    pk = 1 + k % 2
                    sv = P[pk][:].rearrange("p (h l) -> p h l", l=128)

                    def mms(e, k=k, sv=sv):
                        for i in range(4):
                            ins = e.matmul(sv[:, i, :], lhsT=Dm[b][:, 4 * k + i, :], rhs=tri, start=True, stop=True)
                        return ins
                    K.op("pe", mms, r=[R_Dm[b], R_cst], w=[R_P[pk]])
                    K.op("act", lambda e, k=k, sv=sv: e.activation(out=dec[b][:, 4 * k:4 * k + 4, :], in_=sv, func=AF.Exp),
                         r=[R_P[pk]], w=[R_dec[b]])
                for g in range(2):
                    K.op("dve", lambda e, g=g: e.tensor_tensor(
                        out=MT[b][:, 8 * g:8 * g + 8, :], in0=dec[b][:, 8 * g:8 * g + 8, :],
                        in1=cbm[b][:, g:g + 1, :].to_broadcast([128, 8, 128]), op=ALU.mult),
                        r=[R_dec[b], R_cbm[b]], w=[R_MT[b]])
                xs3 = xs[b][:].rearrange("p (h d) -> p h d", d=64)
                K.op("dve", lambda e: e.tensor_tensor(
                    out=xdt[b][:], in0=xs3, in1=dtt[b][:, :, None].to_broadcast([128, 16, 64]), op=ALU.mult),
                    r=[R_xs[b], R_dt[b]], w=[R_xdt[b]])
                K.op("pool", lambda e: e.tensor_tensor(
                    out=xde[b][:], in0=xs3, in1=ex[b][:, 48:64, None].to_broadcast([128, 16, 64]), op=ALU.mult),
                    r=[R_xs[b], R_ex[b]], w=[R_xde[b]])

                K.op("pool", lambda e: e.tensor_tensor(
                    out=t2[b][:].rearrange("p (h d) -> p h d", d=64), in0=xs3,
                    in1=pp[:, PP_DSK:PP_DSK + 16, None].to_broadcast([128, 16, 64]), op=ALU.mult),
                    r=[R_xs[b], R_pp], w=[R_t2[b]])

            def S2a(c):
                b = c % 2
                tok = slice(c * 128, (c + 1) * 128)
                K.dma("sp", szt[b][:], sz_s[tok, :], w=[R_sz[b]], key=f"s_sz{b}")
                for g in range(2):
                    gs = slice(g * 512, (g + 1) * 512)
                    yv = P[3 + g][:].rearrange("p (h d) -> p h d", d=64)

                    def mmy(e, g=g, yv=yv):
                        for i in range(8):
                            ins = e.matmul(yv[:, i, :], lhsT=MT[b][:, 8 * g + i, :], rhs=xdt[b][:, 8 * g + i, :],
                                           start=True, stop=True)
                        return ins
                    K.op("pe", mmy, r=[R_MT[b], R_xdt[b]], w=[R_P[3 + g]])
                    K.op("pe", lambda e, g=g, gs=gs: e.matmul(P[5 + g][:], lhsT=CT[b][:, g, :],
                                                              rhs=hSb[b][:, gs], start=True, stop=True),
                         r=[R_CT[b], R_hSb[b][g]], w=[R_P[5 + g]])
                    K.op("pe", lambda e, g=g: e.matmul(
                        P[7][:], lhsT=Bt[b][:, g * 128:(g + 1) * 128],
                        rhs=xde[b][:, 8 * g:8 * g + 8, :].rearrange("p h d -> p (h d)"), start=True, stop=True),
                        r=[R_Bt[b], R_xde[b]], w=[R_P[7]])
                    K.op("pool", lambda e, g=g, gs=gs: e.tensor_tensor(
                        out=hS[:, gs].rearrange("p (h d) -> p h d", d=64), in0=hS[:, gs].rearrange("p (h d) -> p h d", d=64),
                        in1=ex[b][:, 32 + 8 * g:40 + 8 * g, None].to_broadcast([128, 8, 64]), op=ALU.mult),
                        r=[R_hS[g], R_ex[b]], w=[R_hS[g]])
                    K.op("dve", lambda e, gs=gs: e.tensor_tensor(out=hS[:, gs], in0=hS[:, gs], in1=P[7][:], op=ALU.add),
                         r=[R_hS[g], R_P[7]], w=[R_hS[g]])
                    K.op("act", lambda e, gs=gs: e.activation(out=hSb[1 - b][:, gs], in_=hS[:, gs], func=AF.Copy),
                         r=[R_hS[g]], w=[R_hSb[1 - b][g]])
                    K.op("dve", lambda e, g=g, gs=gs: e.tensor_tensor(
                        out=t1[b][:, gs].rearrange("p (h d) -> p h d", d=64),
                        in0=P[5 + g][:].rearrange("p (h d) -> p h d", d=64),
                        in1=ex[b][:, 16 + 8 * g:16 + 8 * g + 8, None].to_broadcast([128, 8, 64]), op=ALU.mult),
                        r=[R_P[5 + g], R_ex[b]], w=[R_t1[b]])
                    K.op("dve", lambda e, g=g, gs=gs: e.tensor_tensor(out=yd[b][:, gs], in0=t2[b][:, gs], in1=P[3 + g][:],
                                                                      op=ALU.add),
                         r=[R_t2[b], R_P[3 + g]], w=[R_yd[b]])

            def S2b(c):
                b = c % 2
                for g in range(2):
                    gs = slice(g * 512, (g + 1) * 512)
                    K.op("pool", lambda e, gs=gs: e.tensor_tensor(out=t1[b][:, gs], in0=t1[b][:, gs], in1=yd[b][:, gs],
                                                                  op=ALU.add),
                         r=[R_t1[b], R_yd[b]], w=[R_t1[b]])
                    K.op("pool", lambda e, gs=gs: e.tensor_tensor(out=yz[b][:, gs], in0=t1[b][:, gs], in1=szt[b][:, gs],
                                                                  op=ALU.mult),
                         r=[R_t1[b], R_sz[b]], w=[R_yz[b]])
                    K.op("act", lambda e, g=g, gs=gs: e.activation(out=junk[:], in_=yz[b][:, gs], func=AF.Square,
                                                                   accum_out=ssq[b][:, g:g + 1]),
                         r=[R_yz[b]], w=[R_junk, R_ssq[b]])
                K.op("act", lambda e: e.activation(out=ssq[b][:, 2:4], in_=ssq[b][:, 0:2], func=AF.Ln, scale=1.0 / 512,
                                                   bias=epsb[:, 0:1]), r=[R_ssq[b]], w=[R_ssq[b]])
                K.op("act", lambda e: e.activation(out=ssq[b][:, 4:6], in_=ssq[b][:, 2:4], func=AF.Exp, scale=-0.5),
                     r=[R_ssq[b]], w=[R_ssq[b]])
                for g in range(2):
                    gs = slice(g * 512, (g + 1) * 512)
                    K.op("dve", lambda e, g=g, gs=gs: e.scalar_tensor_tensor(
                        out=osm[b][:, gs], in0=yz[b][:, gs], scalar=ssq[b][:, 4 + g:5 + g],
                        in1=pp[:, PP_SOG + g * 512:PP_SOG + (g + 1) * 512], op0=ALU.mult, op1=ALU.mult),
                        r=[R_yz[b], R_ssq[b], R_pp], w=[R_osm[b]])
                tv = Pb[7].rearrange("p (i t) -> p i t", t=128)

                def tr(e):
                    for i in range(8):
                        ins = e.transpose(tv[:, i, :], osm[b][:, i * 128:(i + 1) * 128], ident_bf)
                    return ins
                K.op("pe", tr, r=[R_osm[b], R_cbf], w=[R_P[7]])
                K.op("act", lambda e: e.activation(out=osT[:, :, c * 128:(c + 1) * 128], in_=tv, func=AF.Copy),
                     r=[R_P[7]], w=[R_osT])

            S1(0)
            S1(1)
            S2a(0)
            for c in range(NT):
                if c + 2 < NT:
                    S1(c + 2)
                if c + 1 < NT:
                    S2a(c + 1)
                S2b(c)
        K.flush()

    def phase_outproj(L, q, xsrc, onT, R_onT, osT, R_osT):
        with ExitStack() as st:
            wbf = [sb(st, f"wb{i}", [128, 8192], BF16) for i in range(3)]
            R_wb = [K.res(f"wb{i}") for i in range(3)]
            xr = [sb(st, f"xr{i}", [128, 512], F32) for i in range(4)]
            R_xr = [K.res(f"xr{i}") for i in range(4)]
            xo = [sb(st, f"xo{i}", [128, 512], F32) for i in range(3)]
            R_xo = [K.res(f"xo{i}") for i in range(3)]
            w_l = w_out[L].rearrange("(c p) n -> p c n", p=128)
            units = [(db, tt) for db in range(4) for tt in range(NT)]

            def load(u):
                db, tt = units[u]
                K.dma("sp", xr[u % 4][:], xsrc[q, tt * 128:(tt + 1) * 128, db * 512:(db + 1) * 512],
                      w=[R_xr[u % 4]], key=f"xr{u % 4}")
            load(0)
            load(1)
            for u, (db, tt) in enumerate(units):
                slot = db % 3
                wv = wview(wbf[slot], 512)
                if tt == 0:
                    K.dma("pool", wv, w_l[:, :, db * 512:(db + 1) * 512], w=[R_wb[slot]], key=f"wb{slot}")
                if u + 2 < len(units):
                    load(u + 2)
                pa = u % 4

                def mm(e, tt=tt, pa=pa, wv=wv):
                    for c in range(16):
                        src = onT if c < 8 else osT
                        ins = e.matmul(P[pa][:], lhsT=src[:, c % 8, tt * 128:(tt + 1) * 128], rhs=wv[:, c, :],
                                       start=(c == 0), stop=(c == 15))
                    return ins
                K.op("pe", mm, r=[R_onT, R_osT, R_wb[slot]], w=[R_P[pa]])
                K.op("dve", lambda e, u=u, pa=pa: e.tensor_tensor(out=xo[u % 3][:], in0=xr[u % 4][:], in1=P[pa][:], op=ALU.add),
                     r=[R_xr[u % 4], R_P[pa]], w=[R_xo[u % 3]])
                K.dma("sp", y[q, tt * 128:(tt + 1) * 128, db * 512:(db + 1) * 512], xo[u % 3][:],
                      r=[R_xo[u % 3]], key=f"xo{u % 3}")
        K.flush()

    def phase_ffn(L, q):
        TBN = 1024
        for tbk in range(S // TBN):
            t0 = tbk * TBN
            with ExitStack() as st:
                hT = sb(st, "f_hT", [128, 16, TBN], BF16)
                R_hT = [K.res(f"f_hT{j}") for j in range(TBN // 128)]
                wbf = [sb(st, f"f_wb{i}", [128, 8192], BF16) for i in range(4)]
                R_wb = [K.res(f"f_wb{i}") for i in range(4)]
                wg_l = w_gate[L].rearrange("(c p) n -> p c n", p=128)
                wu_l = w_up[L].rearrange("(c p) n -> p c n", p=128)
                wd_l = w_down[L].rearrange("(c p) n -> p c n", p=128)

                def guload(fg):
                    sg_, su_ = fg % 2, 2 + fg % 2
                    K.dma("pool", wview(wbf[sg_], 512), wg_l[:, :, fg * 512:(fg + 1) * 512], w=[R_wb[sg_]], key=f"f_wb{sg_}")
                    K.dma("pool", wview(wbf[su_], 512), wu_l[:, :, fg * 512:(fg + 1) * 512], w=[R_wb[su_]], key=f"f_wb{su_}")
                guload(0)
                guload(1)
                with ExitStack() as st2:
                    emit_norm(st2, lambda j: y[q, t0 + j * 128: t0 + (j + 1) * 128, :], TBN // 128, hT, R_hT, PP_NFFN, [6, 7])
                    K.flush()
                actT = sb(st, "f_act", [128, NFC, TBN], BF16)
                R_act = [K.res(f"f_act{i}") for i in range(11)]
                sgt = [sb(st, f"f_sg{i}", [128, 512], F32) for i in range(2)]
                R_sg = [K.res(f"f_sg{i}") for i in range(2)]
                xr = [sb(st, f"f_xr{i}", [128, 256], F32) for i in range(4)]
                R_xr = [K.res(f"f_xr{i}") for i in range(4)]
                xo = [sb(st, f"f_xo{i}", [128, 256], F32) for i in range(3)]
                R_xo = [K.res(f"f_xo{i}") for i in range(3)]
                u = 0
                for fg in range(11):
                    sg_, su_ = fg % 2, 2 + fg % 2
                    wg = wview(wbf[sg_], 512)
                    wu = wview(wbf[su_], 512)
                    if fg >= 2:
                        guload(fg)
                    for j in range(4):
                        for ts in range(TBN // 512):
                            pg, pu = u % 3, 3 + u % 3
                            k2 = u % 2
                            u += 1

                            def mm(e, j=j, ts=ts, pg=pg, pu=pu, wg=wg, wu=wu):
                                for c in range(16):
                                    e.matmul(P[pg][:], lhsT=wg[:, c, j * 128:(j + 1) * 128],
                                             rhs=hT[:, c, ts * 512:(ts + 1) * 512], start=(c == 0), stop=(c == 15))
                                for c in range(16):
                                    ins = e.matmul(P[pu][:], lhsT=wu[:, c, j * 128:(j + 1) * 128],
                                                   rhs=hT[:, c, ts * 512:(ts + 1) * 512], start=(c == 0), stop=(c == 15))
                                return ins
                            K.op("pe", mm, r=[R_hT[4 * ts + i] for i in range(4)] + [R_wb[sg_], R_wb[su_]],
                                 w=[R_P[pg], R_P[pu]])
                            K.op("act", lambda e, pg=pg, k2=k2: e.activation(out=sgt[k2][:], in_=P[pg][:], func=AF.Silu),
                                 r=[R_P[pg]], w=[R_sg[k2]])
                            K.op("dve", lambda e, fg=fg, j=j, ts=ts, pu=pu, k2=k2: e.tensor_tensor(
                                out=actT[:, fg * 4 + j, ts * 512:(ts + 1) * 512], in0=sgt[k2][:], in1=P[pu][:], op=ALU.mult),
                                r=[R_sg[k2], R_P[pu]], w=[R_act[fg]])
                units = [(db, tt) for db in range(8) for tt in range(TBN // 128)]

                def load(i):
                    db, tt = units[i]
                    K.dma("sp", xr[i % 4][:], y[q, t0 + tt * 128: t0 + (tt + 1) * 128, db * 256:(db + 1) * 256],
                          w=[R_xr[i % 4]], key=f"f_xr{i % 4}")
                load(0)
                load(1)
                for i, (db, tt) in enumerate(units):
                    s0, s1 = (2 * db) % 4, (2 * db + 1) % 4
                    wd0 = wview(wbf[s0], 256)
                    wd1 = wview(wbf[s1], 256)
                    if tt == 0:
                        K.dma("pool", wd0, wd_l[:, 0:22, db * 256:(db + 1) * 256], w=[R_wb[s0]], key=f"f_wb{s0}")
                        K.dma("pool", wd1, wd_l[:, 22:44, db * 256:(db + 1) * 256], w=[R_wb[s1]], key=f"f_wb{s1}")
                    if i + 2 < len(units):
                        load(i + 2)
                    pa = i % 4

                    def mm(e, tt=tt, pa=pa, wd0=wd0, wd1=wd1):
                        for f in range(NFC):
                            wd = wd0 if f < 22 else wd1
                            ins = e.matmul(P[pa][:, 0:256], lhsT=actT[:, f, tt * 128:(tt + 1) * 128], rhs=wd[:, f % 22, :],
                                           start=(f == 0), stop=(f == NFC - 1))
                        return ins
                    K.op("pe", mm, r=R_act + [R_wb[s0], R_wb[s1]], w=[R_P[pa]])
                    K.op("dve", lambda e, i=i, pa=pa: e.tensor_tensor(out=xo[i % 3][:], in0=xr[i % 4][:], in1=P[pa][:, 0:256],
                                                                      op=ALU.add),
                         r=[R_xr[i % 4], R_P[pa]], w=[R_xo[i % 3]])
                    K.dma("sp", y[q, t0 + tt * 128: t0 + (tt + 1) * 128, db * 256:(db + 1) * 256], xo[i % 3][:],
                          r=[R_xo[i % 3]], key=f"f_xo{i % 3}")
            K.flush()

    for L in range(n_layers):
        K.dma("sp", pp[:], pp_d[L], w=[R_pp], key="pp")
        K.op("act", lambda e: e.activation(out=abc[:], in_=pp[:, PP_ALOG:PP_ALOG + 16], func=AF.Exp),
             r=[R_pp], w=[R_abc])
        K.op("dve", lambda e: e.tensor_scalar(out=abc[:], in0=abc[:], scalar1=-1.0, scalar2=None, op0=ALU.mult),
             r=[R_abc], w=[R_abc])
        K.flush()
        for q in range(n_seq):
            xsrc = x_in if L == 0 else y
            if "inproj" in phases:
                phase_inproj(L, q, xsrc)
            with ExitStack() as mst:
                onT = sb(mst, "onT", [128, NH, S], BF16)
                osT = sb(mst, "osT", [128, NH, S], BF16)
                R_onT, R_osT = K.res("onT"), K.res("osT")
                if "attn" in phases:
                    phase_attn(L, q, onT, R_onT)
                if "ssd" in phases:
                    phase_ssd(L, q, osT, R_osT)
                if debug:
                    K.dma("sp", dbg_on, onT[:], r=[R_onT], key="dbg_on")
                    K.dma("sp", dbg_os, osT[:], r=[R_osT], key="dbg_os")
                    K.flush()
                if "outproj" in phases:
                    phase_outproj(L, q, xsrc, onT, R_onT, osT, R_osT)
            if "ffn" in phases:
                phase_ffn(L, q)
    K.flush(final=True)
    es.close()
    return nc, K


def make_consts():
    p = np.arange(128)[:, None]
    f = np.arange(128)[None, :]
    c = np.zeros((128, NCST), np.float32)
    c[:, C_ID:C_ID + 128] = (p == f)
    c[:, C_ONE:C_ONE + 128] = 1.0
    c[:, C_US:C_US + 128] = (p > f)
    c[:, C_TRI:C_TRI + 128] = (p <= f)
    c[:, C_LGE:C_LGE + 128] = (p >= f)
    fa = np.arange(896)[None, :]
    c[:, C_AM:C_AM + 896] = (fa - p > 384)
    return c


def pack_params(inp, n_layers):
    pp = np.zeros((n_layers, 128, NP_), np.float32)
    for l in range(n_layers):
        pp[l, :, PP_NMIX:PP_NMIX + 16] = inp["norm_mix"][l].reshape(16, 128).T
        pp[l, :, PP_NFFN:PP_NFFN + 16] = inp["norm_ffn"][l].reshape(16, 128).T
        pp[l, :, PP_QG] = inp["q_gain"][l]
        pp[l, :, PP_KG] = inp["k_gain"][l]
        cw = inp["conv_w"][l].reshape(4, 12, 128)
        pp[l, :, PP_CW:PP_CW + 48] = cw.transpose(2, 1, 0).reshape(128, 48)
        pp[l, :, PP_CB:PP_CB + 12] = inp["conv_b"][l].reshape(12, 128).T
        pp[l, :, PP_DTB:PP_DTB + 16] = np.broadcast_to(inp["dt_bias"][l][None, :], (128, 16))
        pp[l, :, PP_ALOG:PP_ALOG + 16] = np.broadcast_to(inp["a_log"][l][None, :], (128, 16))
        pp[l, :, PP_DSK:PP_DSK + 16] = np.broadcast_to(inp["d_skip"][l][None, :], (128, 16))
        pp[l, :, PP_AOG:PP_AOG + 8] = inp["attn_out_gain"][l].reshape(8, 128).T
        pp[l, :, PP_SOG:PP_SOG + 1024] = np.broadcast_to(inp["ssm_out_gain"][l][None, :], (128, 1024))
    return pp


_CACHE = {}
N_LAUNCH_LAYERS = 4


def kernel(**inputs):
    inp = {k: np.asarray(v) for k, v in inputs.items()}
    x = np.ascontiguousarray(inp["x"], dtype=np.float32)
    n_cores = 8
    n_seq = x.shape[0] // n_cores
    depth = inp["w_in"].shape[0]
    lpl = N_LAUNCH_LAYERS
    key = (lpl, n_seq)
    if key not in _CACHE:
        _CACHE[key] = build(lpl, n_seq)[0]
    nc = _CACHE[key]
    cst = make_consts()
    pp = pack_params(inp, depth)
    cur = x
    for l0 in range(0, depth, lpl):
        sl = slice(l0, l0 + lpl)
        shared = {"pp": np.ascontiguousarray(pp[sl]), "cst": cst}
        for l in range(lpl):
            for nm in ("w_in", "w_out", "w_gate", "w_up", "w_down"):
                shared[f"{nm}{l}"] = np.ascontiguousarray(inp[nm][l0 + l], dtype=np.float32)
        in_maps = [dict(shared, x=np.ascontiguousarray(cur[c * n_seq:(c + 1) * n_seq])) for c in range(n_cores)]
        res = run_bass_kernel_spmd(nc, in_maps, core_ids=list(range(n_cores)))
        cur = np.concatenate([np.asarray(r["y"]) for r in res.results], axis=0).astype(np.float32)
    return cur
```

```python
from contextlib import ExitStack
import math
import numpy as np
import concourse.bass as bass
import concourse.mybir as mybir
from concourse.bass_utils import run_bass_kernel_spmd

F32, BF16 = mybir.dt.float32, mybir.dt.bfloat16
AF = mybir.ActivationFunctionType
ALU = mybir.AluOpType

S = 2048
D = 2048
NH = 8
DH = 128
DFF = 5632
IN_DIM = 5648
EPS = 1e-6
NT = S // 128
NB = S // 512
NFC = DFF // 128
PP_NMIX, PP_NFFN, PP_QG, PP_KG, PP_CW, PP_CB, PP_DTB, PP_ALOG, PP_DSK, PP_AOG, PP_SOG, NP_ = \
    0, 16, 32, 33, 34, 82, 94, 110, 126, 142, 150, 1174
C_ID, C_ONE, C_US, C_TRI, C_LGE, C_AM, NCST = 0, 128, 256, 384, 512, 640, 1536

ENG = ["pe", "act", "dve", "pool", "sp"]


class Res:
    __slots__ = ("name", "lw", "rd")

    def __init__(self, name):
        self.name = name
        self.lw = None
        self.rd = []


class Op:
    __slots__ = ("eng", "fn", "kind", "key", "need", "val", "deps")


class Sched:
    def __init__(self, nc, es):
        self.nc = nc
        self.sem = {e: es.enter_context(nc.semaphore("s_" + e)) for e in ENG}
        self.cnt = {e: 0 for e in ENG}
        self.seen = {e: {} for e in ENG}
        self.dsem = {}
        self.dcnt = {}
        self.es = es
        self.ops = {e: [] for e in ENG}
        self.pending = {e: [] for e in ENG}
        self.allres = []
        self.dirty = {}
        self.n_inst = 0

    def res(self, name):
        r = Res(name)
        self.allres.append(r)
        return r

    def _mk(self, eng, fn, r, w, kind, key):
        o = Op()
        o.eng, o.fn, o.kind, o.key, o.need, o.val = eng, fn, kind, key, False, None
        deps = {}
        for d in self.pending[eng]:
            deps[d] = True
        self.pending[eng] = []
        for x in r:
            if x.lw is not None:
                deps[x.lw] = True
        for x in w:
            if x.lw is not None and x.lw not in deps:
                deps[x.lw] = False
            for q in x.rd:
                if q not in deps:
                    deps[q] = False
        keep = []
        for d, raw in deps.items():
            if d is o:
                continue
            if kind == "c" and d.kind == "c" and d.eng == eng and not raw:
                continue
            keep.append(d)
            if d.kind != "d":
                d.need = True
        o.deps = keep
        for x in r:
            if x not in w:
                x.rd = [q for q in x.rd if not (q.eng == eng and q.kind == kind and q.key == key)]
                x.rd.append(o)
        for x in w:
            x.lw = o
            x.rd = []
        self.ops[eng].append(o)
        return o

    def op(self, eng, fn, r=(), w=()):
        return self._mk(eng, fn, r, w, "c", None)

    def dma(self, eng, out, in_, r=(), w=(), key=None):
        assert key is not None
        if key not in self.dsem:
            self.dsem[key] = self.es.enter_context(self.nc.semaphore("d_" + key))
            self.dcnt[key] = 0
        o = self._mk(eng, lambda e: e.dma_start(out=out, in_=in_), r, w, "d", key)
        self.dcnt[key] += 16
        o.val = self.dcnt[key]
        self.dirty[key] = o
        return o

    def flush(self, final=False):
        bar = Op()
        bar.eng, bar.fn, bar.kind, bar.key, bar.need, bar.val = "sp", None, "b", None, True, None
        deps = list(self.pending["sp"])
        self.pending["sp"] = []
        for e in ENG:
            if self.ops[e]:
                lo = None
                for o in reversed(self.ops[e]):
                    if o.kind == "c":
                        lo = o
                        break
                if lo is not None:
                    lo.need = True
                    deps.append(lo)
        deps.extend(self.dirty.values())
        self.dirty = {}
        bar.deps = deps
        self.ops["sp"].append(bar)
        for e in ENG:
            if e != "sp":
                self.pending[e].append(bar)
        for r in self.allres:
            r.lw = None
            r.rd = []
        for e in ENG:
            for o in self.ops[e]:
                if o.kind != "d" and o.need:
                    self.cnt[e] += 1
                    o.val = self.cnt[e]
        nc = self.nc
        me = self

        def replay(e, h):
            seen = me.seen[e]
            for o in me.ops[e]:
                for d in o.deps:
                    if d.kind == "d":
                        sem = me.dsem[d.key]
                    else:
                        sem = me.sem[d.eng]
                    if seen.get(sem.name, 0) >= d.val:
                        continue
                    seen[sem.name] = d.val
                    h.wait_ge(sem, d.val)
                    me.n_inst += 1
                if o.kind == "b":
                    h.sem_inc(me.sem["sp"], 1)
                    continue
                ins = o.fn(h)
                me.n_inst += 1
                if o.kind == "d":
                    ins.then_inc(me.dsem[o.key], 16)
                elif o.need:
                    ins.then_inc(me.sem[e], 1)
            if final and e != "sp":
                for d in me.pending[e]:
                    h.wait_ge(me.sem["sp"], d.val)

        with nc.Block() as block:
            @block.tensor
            def _(h):
                replay("pe", h)

            @block.scalar
            def _(h):
                replay("act", h)

            @block.vector
            def _(h):
                replay("dve", h)

            @block.gpsimd
            def _(h):
                replay("pool", h)

            @block.sync
            def _(h):
                replay("sp", h)
        self.ops = {e: [] for e in ENG}


ALL_PHASES = ('inproj', 'attn', 'ssd', 'outproj', 'ffn')


def build(n_layers, n_seq, debug=False, phases=ALL_PHASES):
    nc = bass.Bass("TRN2", target_bir_lowering=False)

    def dram(name, shape, dt, kind="Internal"):
        return nc.dram_tensor(name, shape, dt, kind=kind).ap()

    sk = "ExternalOutput" if debug else "Internal"
    x_in = dram("x", [n_seq, S, D], F32, "ExternalInput")
    y = dram("y", [n_seq, S, D], F32, "ExternalOutput")
    w_in = [dram(f"w_in{l}", [D, IN_DIM], F32, "ExternalInput") for l in range(n_layers)]
    w_out = [dram(f"w_out{l}", [D, D], F32, "ExternalInput") for l in range(n_layers)]
    w_gate = [dram(f"w_gate{l}", [D, DFF], F32, "ExternalInput") for l in range(n_layers)]
    w_up = [dram(f"w_up{l}", [D, DFF], F32, "ExternalInput") for l in range(n_layers)]
    w_down = [dram(f"w_down{l}", [DFF, D], F32, "ExternalInput") for l in range(n_layers)]
    pp_d = dram("pp", [n_layers, 128, NP_], F32, "ExternalInput")
    gb_d = dram("gb", [n_layers, 2, 128, D], F32, "ExternalInput")
    cst_d = dram("cst", [128, NCST], F32, "ExternalInput")
    qT_s = dram("qT_s", [NH, 128, S], BF16, sk)
    kT_s = dram("kT_s", [NH, 128, S], BF16, sk)
    v_s = dram("v_s", [S, 1024], BF16, sk)
    sz_s = dram("sz_s", [S, 1024], F32, sk)
    dt_s = dram("dt_s", [S, 16], F32, sk)
    xs_s = dram("xs_s", [S, 1024], F32, sk)
    BT_s = dram("BT_s", [2, 128, S], BF16, sk)
    CT_s = dram("CT_s", [2, 128, S], BF16, sk)
    Bt_s = dram("Bt_s", [S, 256], BF16, sk)

    if debug:
        dbg_on = dram("dbg_on", [128, NH, S], BF16, sk)
        dbg_os = dram("dbg_os", [128, NH, S], BF16, sk)
    es = ExitStack()
    K = Sched(nc, es)

    uniq = [0]

    def sb(stack, name, shape, dt):
        uniq[0] += 1
        return stack.enter_context(nc.sbuf_tensor(f"sb{uniq[0]}_{name}", shape, dt))

    def ps(stack, name, shape, dt):
        uniq[0] += 1
        return stack.enter_context(nc.psum_tensor(f"ps{uniq[0]}_{name}", shape, dt))

    cst = sb(es, "cst", [128, NCST], F32)
    cbf = sb(es, "cbf", [128, 384], BF16)
    pp = sb(es, "pp", [128, NP_], F32)
    abc = sb(es, "abc", [128, 16], F32)
    epsb = sb(es, "epsb", [128, 2], F32)
    R_cst, R_cbf, R_pp, R_abc = K.res("cst"), K.res("cbf"), K.res("pp"), K.res("abc")
    ident = cst[:, C_ID:C_ID + 128]
    ones = cst[:, C_ONE:C_ONE + 128]
    ustr = cst[:, C_US:C_US + 128]
    tri = cst[:, C_TRI:C_TRI + 128]
    ident_bf = cbf[:, 0:128]
    nlge_bf = cbf[:, 128:256]
    nones_bf = cbf[:, 256:384]

    K.dma("sp", cst[:], cst_d[:, :], w=[R_cst], key="cst")
    K.op("act", lambda e: e.activation(out=cbf[:, 0:128], in_=cst[:, C_ID:C_ID + 128], func=AF.Copy),
         r=[R_cst], w=[R_cbf])
    K.op("act", lambda e: e.activation(out=cbf[:, 128:256], in_=cst[:, C_LGE:C_LGE + 128], func=AF.Copy, scale=-1.0),
         r=[R_cst], w=[R_cbf])
    K.op("act", lambda e: e.activation(out=cbf[:, 256:384], in_=cst[:, C_ONE:C_ONE + 128], func=AF.Copy, scale=-1.0),
         r=[R_cst], w=[R_cbf])
    K.op("dve", lambda e: e.memset(epsb[:, 0:1], EPS), w=[R_abc])
    K.op("dve", lambda e: e.memset(epsb[:, 1:2], math.log(DH ** -0.5)), w=[R_abc])
    K.flush()

    P = [ps(es, f"P{i}", [128, 512], F32) for i in range(8)]
    R_P = [K.res(f"P{i}") for i in range(8)]
    Pb = [p[:].bitcast(BF16) for p in P]

    def wview(flat, n):
        c = 8192 // n if n == 512 else 22
        return flat[:, 0:c * n].rearrange("p (c n) -> p c n", n=n)

    def emit_norm(stack, src, ntile, hT, R_hT, gsrc, banks):
        gbc = sb(stack, "n_gbc", [128, D], F32)
        R_gbc = K.res("n_gbc")
        K.dma("sp", gbc[:], gsrc, w=[R_gbc], key="n_gbc")
        NBUF = 4
        xt = [sb(stack, f"n_xt{i}", [128, D], F32) for i in range(NBUF)]
        R_xt = [K.res(f"n_xt{i}") for i in range(NBUF)]
        xn = [sb(stack, f"n_xn{i}", [128, D], BF16) for i in range(NBUF)]
        R_xn = [K.res(f"n_xn{i}") for i in range(NBUF)]
        junk = sb(stack, "n_junk", [128, D], BF16)
        R_junk = K.res("n_junk")
        st = [sb(stack, f"n_st{i}", [128, 4], F32) for i in range(NBUF)]
        R_st = [K.res(f"n_st{i}") for i in range(NBUF)]
        for j in range(ntile):
            b = j % NBUF
            K.dma("sp", xt[b][:], src(j), w=[R_xt[b]], key=f"n_xt{b}")
            K.op("act", lambda e, b=b: e.activation(out=junk[:], in_=xt[b][:], func=AF.Square,
                                                     accum_out=st[b][:, 0:1]),
                 r=[R_xt[b]], w=[R_junk, R_st[b]])
            K.op("act", lambda e, b=b: e.activation(out=st[b][:, 1:2], in_=st[b][:, 0:1], func=AF.Ln,
                                                     scale=1.0 / D, bias=epsb[:, 0:1]),
                 r=[R_st[b]], w=[R_st[b]])
            K.op("act", lambda e, b=b: e.activation(out=st[b][:, 2:3], in_=st[b][:, 1:2], func=AF.Exp, scale=-0.5),
                 r=[R_st[b]], w=[R_st[b]])
            K.op("dve", lambda e, b=b: e.scalar_tensor_tensor(out=xn[b][:], in0=xt[b][:], scalar=st[b][:, 2:3],
                                                              in1=gbc[:], op0=ALU.mult, op1=ALU.mult),
                 r=[R_xt[b], R_st[b], R_gbc], w=[R_xn[b]])
            for c2 in range(2):
                pb = banks[(j * 2 + c2) % len(banks)]
                tpv = Pb[pb].rearrange("p (i t) -> p i t", t=128)

                def tr(e, b=b, c2=c2, tpv=tpv):
                    for i in range(8):
                        c = c2 * 8 + i
                        ins = e.transpose(tpv[:, i, :], xn[b][:, c * 128:(c + 1) * 128], ident_bf)
                    return ins
                K.op("pe", tr, r=[R_xn[b], R_cbf], w=[R_P[pb]])

                def ev(e, c2=c2, tpv=tpv, j=j):
                    return e.tensor_copy(out=hT[:, c2 * 8:(c2 + 1) * 8, j * 128:(j + 1) * 128], in_=tpv)
                K.op("dve", ev, r=[R_P[pb]], w=[R_hT[j]])

    def phase_inproj(L, q, xsrc):
        with ExitStack() as st:
            hT = sb(st, "hT", [128, 16, S], BF16)
            R_hT = [K.res(f"hT{j}") for j in range(NT)]
            wbf = [sb(st, f"wb{i}", [128, 8192], BF16) for i in range(3)]
            R_wb = [K.res(f"wb{i}") for i in range(3)]
            wdt = sb(st, "wdt", [128, 16, 16], BF16)
            R_wdt = K.res("wdt")
            w_l = w_in[L].rearrange("(c p) n -> p c n", p=128)
            tiles = [("v", 0), ("v", 1), ("z", 0), ("z", 1), ("q", 0), ("q", 1), ("k", 0), ("k", 1),
                     ("x", 0), ("x", 1), ("x", 2)]
            col0 = {"q": 0, "k": 1024, "v": 2048, "z": 3072, "x": 4096}

            def wload(ti):
                kind, w = tiles[ti]
                slot = ti % 3
                K.dma("pool", wview(wbf[slot], 512), w_l[:, :, col0[kind] + w * 512: col0[kind] + (w + 1) * 512],
                      w=[R_wb[slot]], key=f"wb{slot}")
            K.dma("pool", wdt[:], w_l[:, :, 5632:5648], w=[R_wdt], key="wdt")
            for ti in range(3):
                wload(ti)
            with ExitStack() as st2:
                emit_norm(st2, lambda j: xsrc[q, j * 128:(j + 1) * 128, :], NT, hT, R_hT, gb_d[L, 0], [4, 5, 6, 7])
                K.flush()
            vst = [sb(st, f"vst{i}", [128, 512], BF16) for i in range(2)]
            R_vst = [K.res(f"vst{i}") for i in range(2)]
            zst = [sb(st, f"zst{i}", [128, 512], F32) for i in range(2)]
            R_zst = [K.res(f"zst{i}") for i in range(2)]
            sqt = [sb(st, f"sqt{i}", [128, 512], BF16) for i in range(2)]
            R_sqt = [K.res(f"sqt{i}") for i in range(2)]
            rt = [sb(st, f"rt{i}", [128, 512], F32) for i in range(2)]
            R_rt = [K.res(f"rt{i}") for i in range(2)]
            qst = [sb(st, f"qst{i}", [128, 512], BF16) for i in range(2)]
            R_qst = [K.res(f"qst{i}") for i in range(2)]
            craw = [sb(st, f"craw{i}", [128, S + 3], F32) for i in range(2)]
            R_craw = [K.res(f"craw{i}") for i in range(2)]
            ct = [sb(st, f"ct{i}", [128, S], F32) for i in range(2)]
            R_ct = [K.res(f"ct{i}") for i in range(2)]
            xsT = [sb(st, f"xsT{i}", [128, S], F32) for i in range(1)]
            R_xsT = [K.res(f"xsT{i}") for i in range(1)]
            bcT = [sb(st, f"bcT{i}", [128, S], BF16) for i in range(2)]
            R_bcT = [K.res(f"bcT{i}") for i in range(2)]
            xst = sb(st, "xst", [128, 16, 128], F32)
            R_xst = K.res("xst")
            bst = sb(st, "bst", [128, 16, 128], BF16)
            R_bst = K.res("bst")
            dtw = sb(st, "dtw", [128, 3, 256], F32)
            R_dtw = K.res("dtw")
            for i in range(2):
                K.op("dve", lambda e, i=i: e.memset(craw[i][:, 0:3], 0.0), w=[R_craw[i]])

            pdt = P[4][:, 0:256].rearrange("p (t h) -> p t h", h=16)
            for tt in range(NT):
                def mm(e, tt=tt):
                    for c in range(16):
                        ins = e.matmul(pdt[:, tt, :], lhsT=hT[:, c, tt * 128:(tt + 1) * 128], rhs=wdt[:, c, :],
                                       start=(c == 0), stop=(c == 15))
                    return ins
                K.op("pe", mm, r=[R_hT[tt], R_wdt], w=[R_P[4]])
            dtv = [dtw[:, i, :].rearrange("p (t h) -> p t h", h=16) for i in range(3)]
            K.op("dve", lambda e: e.tensor_tensor(out=dtv[0], in0=pdt,
                                                  in1=pp[:, None, PP_DTB:PP_DTB + 16].to_broadcast([128, 16, 16]),
                                                  op=ALU.add), r=[R_P[4], R_pp], w=[R_dtw])
            K.op("act", lambda e: e.activation(out=dtw[:, 1, :], in_=dtw[:, 0, :], func=AF.Exp), r=[R_dtw], w=[R_dtw])
            K.op("act", lambda e: e.activation(out=dtw[:, 2, :], in_=dtw[:, 1, :], func=AF.Ln, bias=1.0),
                 r=[R_dtw], w=[R_dtw])
            K.dma("sp", dt_s.rearrange("(t p) h -> p t h", p=128), dtv[2], r=[R_dtw], key="dtw")

            state = {"u": 0, "ev": 0}
            dq = []

            def defer(fn, delay):
                dq.append([delay, fn])

            def run_deferred(all_=False):
                for it in list(dq):
                    it[0] -= 1
                    if it[0] <= 0 or all_:
                        dq.remove(it)
                        it[1]()

            for ti, (kind, w) in enumerate(tiles):
                slot = ti % 3
                wv = wview(wbf[slot], 512)
                if ti >= 3:
                    wload(ti)
                if kind in ("v", "z"):
                    for tt in range(NT):
                        u = state["u"]; state["u"] += 1
                        pa = u % 4

                        def mm(e, tt=tt, pa=pa, wv=wv):
                            for c in range(16):
                                ins = e.matmul(P[pa][:], lhsT=hT[:, c, tt * 128:(tt + 1) * 128], rhs=wv[:, c, :],
                                               start=(c == 0), stop=(c == 15))
                            return ins
                        K.op("pe", mm, r=[R_hT[tt], R_wb[slot]], w=[R_P[pa]])
                        run_deferred()
                        k2 = u % 2
                        if kind == "v":
                            K.op("act", lambda e, pa=pa, k2=k2: e.activation(out=vst[k2][:], in_=P[pa][:], func=AF.Copy),
                                 r=[R_P[pa]], w=[R_vst[k2]])
                            K.dma("sp", v_s[tt * 128:(tt + 1) * 128, w * 512:(w + 1) * 512], vst[k2][:],
                                  r=[R_vst[k2]], key=f"vst{k2}")
                        else:
                            K.op("act", lambda e, pa=pa, k2=k2: e.activation(out=zst[k2][:], in_=P[pa][:], func=AF.Silu),
                                 r=[R_P[pa]], w=[R_zst[k2]])
                            K.dma("sp", sz_s[tt * 128:(tt + 1) * 128, w * 512:(w + 1) * 512], zst[k2][:],
                                  r=[R_zst[k2]], key=f"zst{k2}")
                else:
                    for j in range(4):
                        cj = w * 4 + j
                        for tb in range(NB):
                            u = state["u"]; state["u"] += 1
                            pa = u % 4

                            def mm(e, j=j, tb=tb, pa=pa, wv=wv):
                                for c in range(16):
                                    ins = e.matmul(P[pa][:], lhsT=wv[:, c, j * 128:(j + 1) * 128],
                                                   rhs=hT[:, c, tb * 512:(tb + 1) * 512],
                                                   start=(c == 0), stop=(c == 15))
                                return ins
                            K.op("pe", mm, r=[R_hT[4 * tb + i] for i in range(4)] + [R_wb[slot]], w=[R_P[pa]])
                            run_deferred()
                            if kind in ("q", "k"):
                                k2 = state["ev"] % 2; state["ev"] += 1
                                K.op("act", lambda e, pa=pa, k2=k2: e.activation(out=sqt[k2][:], in_=P[pa][:],
                                                                                  func=AF.Square),
                                     r=[R_P[pa]], w=[R_sqt[k2]])

                                def rest(kind=kind, cj=cj, tb=tb, pa=pa, k2=k2):
                                    pn = 4 + k2
                                    K.op("pe", lambda e: e.matmul(P[pn][:], lhsT=nones_bf, rhs=sqt[k2][:],
                                                                  start=True, stop=True),
                                         r=[R_sqt[k2], R_cbf], w=[R_P[pn]])
                                    K.op("act", lambda e: e.activation(out=rt[k2][:], in_=P[pn][:], func=AF.Ln,
                                                                       scale=-1.0 / DH, bias=epsb[:, 0:1]),
                                         r=[R_P[pn]], w=[R_rt[k2]])
                                    if kind == "q":
                                        K.op("act", lambda e: e.activation(out=rt[k2][:], in_=rt[k2][:], func=AF.Exp,
                                                                           scale=-0.5, bias=epsb[:, 1:2]),
                                             r=[R_rt[k2]], w=[R_rt[k2]])
                                    else:
                                        K.op("act", lambda e: e.activation(out=rt[k2][:], in_=rt[k2][:], func=AF.Exp,
                                                                           scale=-0.5),
                                             r=[R_rt[k2]], w=[R_rt[k2]])
                                    gc = PP_QG if kind == "q" else PP_KG
                                    K.op("dve", lambda e: e.scalar_tensor_tensor(
                                        out=qst[k2][:], in0=P[pa][:], scalar=pp[:, gc:gc + 1], in1=rt[k2][:],
                                        op0=ALU.mult, op1=ALU.mult), r=[R_P[pa], R_rt[k2], R_pp], w=[R_qst[k2]])
                                    dst = (qT_s if kind == "q" else kT_s)[cj, :, tb * 512:(tb + 1) * 512]
                                    K.dma("sp", dst, qst[k2][:], r=[R_qst[k2]], key=f"qst{k2}")
                                defer(rest, 1)
                            else:
                                cb = cj % 2
                                K.op("act", lambda e, pa=pa, cb=cb, tb=tb: e.activation(
                                    out=craw[cb][:, 3 + tb * 512: 3 + (tb + 1) * 512], in_=P[pa][:], func=AF.Copy),
                                    r=[R_P[pa]], w=[R_craw[cb]])
                                if tb == NB - 1:
                                    def conv(cj=cj, cb=cb):
                                        wc = PP_CW + cj * 4
                                        pe_part = []
                                        K.op("dve", lambda e: e.tensor_scalar(
                                            out=ct[cb][:], in0=craw[cb][:, 0:S], scalar1=pp[:, wc:wc + 1],
                                            scalar2=pp[:, PP_CB + cj:PP_CB + cj + 1], op0=ALU.mult, op1=ALU.add),
                                            r=[R_craw[cb], R_pp], w=[R_ct[cb]])
                                        for i in range(1, 4):
                                            K.op("dve", lambda e, i=i: e.scalar_tensor_tensor(
                                                out=ct[cb][:], in0=craw[cb][:, i:i + S], scalar=pp[:, wc + i:wc + i + 1],
                                                in1=ct[cb][:], op0=ALU.mult, op1=ALU.add),
                                                r=[R_craw[cb], R_pp, R_ct[cb]], w=[R_ct[cb]])
                                        if cj < 8:
                                            K.op("act", lambda e: e.activation(out=xsT[0][:], in_=ct[cb][:], func=AF.Silu),
                                                 r=[R_ct[cb]], w=[R_xsT[0]])

                                            def xs_pe():
                                              for t4 in range(4):
                                                pb = 6 + (t4 % 2)
                                                tv = P[pb][:].rearrange("p (i t) -> p i t", t=128)

                                                def tr(e, t4=t4, tv=tv):
                                                    for i in range(4):
                                                        tt = t4 * 4 + i
                                                        ins = e.transpose(tv[:, i, :], xsT[0][:, tt * 128:(tt + 1) * 128],
                                                                          ident)
                                                    return ins
                                                K.op("pe", tr, r=[R_xsT[0], R_cst], w=[R_P[pb]])
                                                K.op("act", lambda e, t4=t4, tv=tv: e.activation(
                                                    out=xst[:, t4 * 4:(t4 + 1) * 4, :], in_=tv, func=AF.Copy),
                                                    r=[R_P[pb]], w=[R_xst])
                                              K.dma("sp", xs_s.rearrange("(t p) c -> p t c", p=128)[:, :, cj * 128:(cj + 1) * 128],
                                                    xst[:], r=[R_xst], key="xst")
                                            defer(xs_pe, 3)
                                        else:
                                            g = (cj - 8) % 2
                                            isB = cj < 10
                                            bb = cj % 2
                                            K.op("act", lambda e: e.activation(out=bcT[bb][:], in_=ct[cb][:], func=AF.Silu),
                                                 r=[R_ct[cb]], w=[R_bcT[bb]])
                                            K.dma("sp", (BT_s if isB else CT_s)[g], bcT[bb][:], r=[R_bcT[bb]], key=f"bcT{bb}")
                                            def b_pe():
                                                for t8 in range(2):
                                                    pb = 6 + t8
                                                    tv = Pb[pb].rearrange("p (i t) -> p i t", t=128)

                                                    def tr(e, t8=t8, tv=tv):
                                                        for i in range(8):
                                                            tt = t8 * 8 + i
                                                            ins = e.transpose(tv[:, i, :], bcT[bb][:, tt * 128:(tt + 1) * 128],
                                                                              ident_bf)
                                                        return ins
                                                    K.op("pe", tr, r=[R_bcT[bb], R_cbf], w=[R_P[pb]])
                                                    K.op("act", lambda e, t8=t8, tv=tv: e.activation(
                                                        out=bst[:, t8 * 8:(t8 + 1) * 8, :], in_=tv, func=AF.Copy),
                                                        r=[R_P[pb]], w=[R_bst])
                                                K.dma("sp", Bt_s.rearrange("(t p) c -> p t c", p=128)[:, :, g * 128:(g + 1) * 128],
                                                      bst[:], r=[R_bst], key="bst")
                                            if isB:
                                                defer(b_pe, 3)
                                    conv()
            run_deferred(all_=True)
        K.flush()

    def phase_attn(L, q, onT, R_onT):
        with ExitStack() as st:
            oT = sb(st, "oT", [128, NH, S], F32)
            R_oT = [[K.res(f"oT{h}_{tb}") for tb in range(NB)] for h in range(NH)]
            qb = [sb(st, f"qb{i}", [128, S], BF16) for i in range(2)]
            kb_ = [sb(st, f"kb{i}", [128, S], BF16) for i in range(2)]
            vb = [sb(st, f"vb{i}", [128, NT, 128], BF16) for i in range(2)]
            R_qb = [K.res(f"qb{i}") for i in range(2)]
            R_kb = [K.res(f"kb{i}") for i in range(2)]
            R_vb = [K.res(f"vb{i}") for i in range(2)]
            et = [sb(st, f"et{i}", [128, 512], F32) for i in range(2)]
            R_et = [K.res(f"et{i}") for i in range(2)]
            spt = [sb(st, f"spt{i}", [128, 512], BF16) for i in range(4)]
            R_spt = [K.res(f"spt{i}") for i in range(4)]
            spf = [sb(st, f"spf{i}", [128, 512], F32) for i in range(2)]
            R_spf = [K.res(f"spf{i}") for i in range(2)]
            Wt = [sb(st, f"Wt{i}", [128, 512], BF16) for i in range(3)]
            R_Wt = [K.res(f"Wt{i}") for i in range(3)]
            Wf = [sb(st, f"Wf{i}", [128, 512], F32) for i in range(2)]
            R_Wf = [K.res(f"Wf{i}") for i in range(2)]
            acc = [sb(st, f"acc{i}", [128, 512], BF16) for i in range(2)]
            R_acc = [K.res(f"acc{i}") for i in range(2)]
            sqb = [sb(st, f"sqb{i}", [128, 512], BF16) for i in range(2)]
            R_sqb = [K.res(f"sqb{i}") for i in range(2)]
            rtn = sb(st, "rtn", [128, 512], F32)
            R_rtn = K.res("rtn")
            vview = v_s.rearrange("(t p) c -> p t c", p=128)
            for i in range(2):
                K.op("dve", lambda e, i=i: e.memset(spf[i][:], 0.0), w=[R_spf[i]])
                K.op("dve", lambda e, i=i: e.memset(Wf[i][:], 0.0), w=[R_Wf[i]])

            def mask_ap(o):
                return cst[:, C_AM + (3 - o) * 128: C_AM + (3 - o) * 128 + 512]

            def load_head(h):
                b = h % 2
                K.dma("sp", qb[b][:], qT_s[h], w=[R_qb[b]], key=f"qb{b}")
                K.dma("sp", kb_[b][:], kT_s[h], w=[R_kb[b]], key=f"kb{b}")
                K.dma("sp", vb[b][:], vview[:, :, h * 128:(h + 1) * 128], w=[R_vb[b]], key=f"vb{b}")

            tiles = []
            for h in range(NH):
                for tb in range(NB):
                    unit = h * NB + tb
                    kbs = list(range(4 * tb + 3, -1, -1))
                    U = {"h": h, "tb": tb, "b": h % 2, "ob": 3 + unit % 2, "n": len(kbs), "accsrc": {}}
                    for i, kb in enumerate(kbs):
                        tiles.append((U, i, kb, len(tiles)))

            def stageA(T):
                U, i, kb, g = T
                tb, b, accsrc = U["tb"], U["b"], U["accsrc"]
                zi = g % 3
                diag = kb >= 4 * tb
                o = kb - 4 * tb
                K.op("pe", lambda e: e.matmul(P[zi][:], lhsT=kb_[b][:, kb * 128:(kb + 1) * 128],
                                              rhs=qb[b][:, tb * 512:(tb + 1) * 512], start=True, stop=True),
                     r=[R_kb[b], R_qb[b]], w=[R_P[zi]])
                c0 = 128 * o if diag else 0
                K.op("act", lambda e: e.activation(out=et[g % 2][:, c0:512], in_=P[zi][:, c0:512], func=AF.Exp),
                     r=[R_P[zi]], w=[R_et[g % 2]])
                if diag:
                    K.op("act", lambda e: e.activation(out=spf[g % 2][:, c0:512], in_=et[g % 2][:, c0:512],
                                                       func=AF.Ln, bias=1.0),
                         r=[R_et[g % 2]], w=[R_spf[g % 2]])
                    K.op("dve", lambda e: e.tensor_tensor(out=spt[g % 4][:], in0=spf[g % 2][:], in1=mask_ap(o),
                                                          op=ALU.mult),
                         r=[R_spf[g % 2], R_cst], w=[R_spt[g % 4]])
                else:
                    K.op("act", lambda e: e.activation(out=spt[g % 4][:], in_=et[g % 2][:], func=AF.Ln, bias=1.0),
                         r=[R_et[g % 2]], w=[R_spt[g % 4]])
                if i == 1:
                    accsrc[1] = (spt[(g - 1) % 4], R_spt[(g - 1) % 4])
                elif i >= 2:
                    prev, R_prev = accsrc[i - 1]
                    a = i % 2
                    K.op("pool", lambda e: e.tensor_tensor(out=acc[a][:], in0=prev[:], in1=spt[(g - 1) % 4][:],
                                                           op=ALU.add),
                         r=[R_prev, R_spt[(g - 1) % 4]], w=[R_acc[a]])
                    accsrc[i] = (acc[a], R_acc[a])

            def stageB(T):
                U, i, kb, g = T
                tb, accsrc = U["tb"], U["accsrc"]
                zi = g % 3
                diag = kb >= 4 * tb
                o = kb - 4 * tb

                def mm(e):
                    ins = e.matmul(P[zi][:], lhsT=nlge_bf, rhs=spt[g % 4][:], start=False, stop=(i == 0),
                                   skip_group_check=True)
                    if i >= 1:
                        ins = e.matmul(P[zi][:], lhsT=nones_bf, rhs=accsrc[i][0][:], start=False, stop=True,
                                       skip_group_check=True)
                    return ins
                rr = [R_spt[g % 4], R_cbf] + ([accsrc[i][1]] if i >= 1 else [])
                K.op("pe", mm, r=rr, w=[R_P[zi]])
                if diag:
                    c0 = 128 * o
                    K.op("act", lambda e: e.activation(out=Wf[g % 2][:, c0:512], in_=P[zi][:, c0:512], func=AF.Exp),
                         r=[R_P[zi]], w=[R_Wf[g % 2]])
                    K.op("dve", lambda e: e.tensor_tensor(out=Wt[g % 3][:], in0=Wf[g % 2][:], in1=mask_ap(o),
                                                          op=ALU.mult),
                         r=[R_Wf[g % 2], R_cst], w=[R_Wt[g % 3]])
                else:
                    K.op("act", lambda e: e.activation(out=Wt[g % 3][:], in_=P[zi][:], func=AF.Exp),
                         r=[R_P[zi]], w=[R_Wt[g % 3]])

            def stageC(T):
                U, i, kb, g = T
                h, tb, b, ob, n = U["h"], U["tb"], U["b"], U["ob"], U["n"]
                K.op("pe", lambda e: e.matmul(P[ob][:], lhsT=vb[b][:, kb, :], rhs=Wt[g % 3][:],
                                              start=(i == 0), stop=(i == n - 1), skip_group_check=True),
                     r=[R_vb[b], R_Wt[g % 3]], w=[R_P[ob]])
                if i == n - 1:
                    K.op("dve", lambda e: e.tensor_copy(out=oT[:, h, tb * 512:(tb + 1) * 512], in_=P[ob][:]),
                         r=[R_P[ob]], w=[R_oT[h][tb]])

            NTL = len(tiles)
            load_head(0)
            load_head(1)
            for step in range(NTL + 2):
                if step < NTL:
                    U, i, kb, g = tiles[step]
                    if U["tb"] == 0 and i == 3 and 1 <= U["h"] and U["h"] + 1 < NH:
                        load_head(U["h"] + 1)
                    stageA(tiles[step])
                if 1 <= step <= NTL:
                    stageB(tiles[step - 1])
                if step >= 2:
                    stageC(tiles[step - 2])
            def do_norm(tb):
                pn = 5 + tb % 2
                for h in range(NH):
                    k2 = (tb * NH + h) % 2
                    K.op("act", lambda e, h=h, k2=k2: e.activation(out=sqb[k2][:], in_=oT[:, h, tb * 512:(tb + 1) * 512],
                                                                    func=AF.Square),
                         r=[R_oT[h][tb]], w=[R_sqb[k2]])
                    K.op("pe", lambda e, h=h, k2=k2: e.matmul(P[pn][:], lhsT=nones_bf, rhs=sqb[k2][:], start=(h == 0),
                                                               stop=(h == NH - 1), skip_group_check=True),
                         r=[R_sqb[k2], R_cbf], w=[R_P[pn]])
                K.op("act", lambda e: e.activation(out=rtn[:], in_=P[pn][:], func=AF.Ln, scale=-1.0 / 1024,
                                                   bias=epsb[:, 0:1]), r=[R_P[pn]], w=[R_rtn])
                K.op("act", lambda e: e.activation(out=rtn[:], in_=rtn[:], func=AF.Exp, scale=-0.5),
                     r=[R_rtn], w=[R_rtn])
                for h in range(NH):
                    K.op("dve", lambda e, h=h: e.scalar_tensor_tensor(
                        out=onT[:, h, tb * 512:(tb + 1) * 512], in0=oT[:, h, tb * 512:(tb + 1) * 512],
                        scalar=pp[:, PP_AOG + h:PP_AOG + h + 1], in1=rtn[:], op0=ALU.mult, op1=ALU.mult),
                        r=[R_oT[h][tb], R_rtn, R_pp], w=[R_onT])
            for tb in range(NB):
                do_norm(tb)
        K.flush()

    def phase_ssd(L, q, osT, R_osT):
        with ExitStack() as st:
            def dbl(name, shape, dt):
                return ([sb(st, f"{name}{i}", shape, dt) for i in range(2)], [K.res(f"{name}{i}") for i in range(2)])
            xs, R_xs = dbl("s_xs", [128, 1024], F32)
            szt, R_sz = dbl("s_sz", [128, 1024], F32)
            dtt, R_dt = dbl("s_dt", [128, 16], F32)
            BT, R_BT = dbl("s_BT", [128, 2, 128], BF16)
            CT, R_CT = dbl("s_CT", [128, 2, 128], BF16)
            Bt, R_Bt = dbl("s_Bt", [128, 256], BF16)
            da, R_da = dbl("s_da", [128, 16], F32)
            ex, R_ex = dbl("s_ex", [128, 64], F32)
            Dm, R_Dm = dbl("s_Dm", [128, 16, 128], F32)
            dec, R_dec = dbl("s_dec", [128, 16, 128], F32)
            cbm, R_cbm = dbl("s_cbm", [128, 2, 128], F32)
            MT, R_MT = dbl("s_MT", [128, 16, 128], BF16)
            xdt, R_xdt = dbl("s_xdt", [128, 16, 64], BF16)
            xde, R_xde = dbl("s_xde", [128, 16, 64], BF16)
            hSb, _ = dbl("s_hSb", [128, 1024], BF16)
            R_hSb = [[K.res(f"s_hSb{i}_{g}") for g in range(2)] for i in range(2)]
            t1, R_t1 = dbl("s_t1", [128, 1024], F32)
            t2, R_t2 = dbl("s_t2", [128, 1024], F32)
            yz, R_yz = dbl("s_yz", [128, 1024], F32)
            yd, R_yd = dbl("s_yd", [128, 1024], F32)
            osm, R_osm = dbl("s_osm", [128, 1024], BF16)
            ssq, R_ssq = dbl("s_ssq", [128, 8], F32)
            hS = sb(st, "s_hS", [128, 1024], F32)
            R_hS = [K.res(f"s_hS{g}") for g in range(2)]
            junk = sb(st, "s_junk", [128, 512], BF16)
            R_junk = K.res("s_junk")
            K.op("dve", lambda e: e.memset(hS[:], 0.0), w=R_hS)
            K.op("dve", lambda e: e.memset(hSb[0][:], 0.0), w=R_hSb[0])

            def S1(c):
                b = c % 2
                tok = slice(c * 128, (c + 1) * 128)
                K.dma("sp", xs[b][:], xs_s[tok, :], w=[R_xs[b]], key=f"s_xs{b}")
                K.dma("sp", dtt[b][:], dt_s[tok, :], w=[R_dt[b]], key=f"s_dt{b}")
                K.dma("sp", BT[b][:], BT_s[:, :, tok].rearrange("g n t -> n g t"), w=[R_BT[b]], key=f"s_BT{b}")
                K.dma("sp", CT[b][:], CT_s[:, :, tok].rearrange("g n t -> n g t"), w=[R_CT[b]], key=f"s_CT{b}")
                K.dma("sp", Bt[b][:], Bt_s[tok, :], w=[R_Bt[b]], key=f"s_Bt{b}")
                K.op("dve", lambda e: e.tensor_tensor(out=da[b][:], in0=dtt[b][:], in1=abc[:], op=ALU.mult),
                     r=[R_dt[b], R_abc], w=[R_da[b]])

                def mm(e):
                    e.matmul(P[0][:, 0:16], lhsT=ustr, rhs=da[b][:], start=True, stop=True)
                    e.matmul(P[0][:, 16:32], lhsT=tri, rhs=da[b][:], start=True, stop=True)
                    return e.matmul(P[0][:, 32:48], lhsT=ones, rhs=da[b][:], start=True, stop=True)
                K.op("pe", mm, r=[R_da[b], R_cst], w=[R_P[0]])
                K.op("act", lambda e: e.activation(out=ex[b][:, 0:48], in_=P[0][:, 0:48], func=AF.Exp),
                     r=[R_P[0]], w=[R_ex[b]])
                K.op("dve", lambda e: e.tensor_tensor(out=ex[b][:, 48:64], in0=ex[b][:, 0:16], in1=dtt[b][:],
                                                      op=ALU.mult), r=[R_ex[b], R_dt[b]], w=[R_ex[b]])
                K.op("dve", lambda e: e.tensor_tensor(out=Dm[b][:], in0=ustr[:, None, :].to_broadcast([128, 16, 128]),
                                                      in1=da[b][:, :, None].to_broadcast([128, 16, 128]), op=ALU.mult),
                     r=[R_cst, R_da[b]], w=[R_Dm[b]])
                cv = P[0][:, 128:384].rearrange("p (g l) -> p g l", l=128)

                def mmc(e):
                    e.matmul(cv[:, 0, :], lhsT=BT[b][:, 0, :], rhs=CT[b][:, 0, :], start=True, stop=True)
                    return e.matmul(cv[:, 1, :], lhsT=BT[b][:, 1, :], rhs=CT[b][:, 1, :], start=True, stop=True)
                K.op("pe", mmc, r=[R_BT[b], R_CT[b]], w=[R_P[0]])
                K.op("dve", lambda e: e.tensor_tensor(out=cbm[b][:], in0=cv,
                                                      in1=tri[:, None, :].to_broadcast([128, 2, 128]), op=ALU.mult),
                     r=[R_P[0], R_cst], w=[R_cbm[b]])
                for k in range(4):
                    pk = 1 + k % 2
                    sv = P[pk][:].rearrange("p (h l) -> p h l", l=128)

                    def mms(e, k=k, sv=sv):
                        for i in range(4):
                            ins = e.matmul(sv[:, i, :], lhsT=Dm[b][:, 4 * k + i, :], rhs=tri, start=True, stop=True)
                        return ins
                    K.op("pe", mms, r=[R_Dm[b], R_cst], w=[R_P[pk]])
                    K.op("act", lambda e, k=k, sv=sv: e.activation(out=dec[b][:, 4 * k:4 * k + 4, :], in_=sv, func=AF.Exp),
                         r=[R_P[pk]], w=[R_dec[b]])
                for g in range(2):
                    K.op("dve", lambda e, g=g: e.tensor_tensor(
                        out=MT[b][:, 8 * g:8 * g + 8, :], in0=dec[b][:, 8 * g:8 * g + 8, :],
                        in1=cbm[b][:, g:g + 1, :].to_broadcast([128, 8, 128]), op=ALU.mult),
                        r=[R_dec[b], R_cbm[b]], w=[R_MT[b]])
                xs3 = xs[b][:].rearrange("p (h d) -> p h d", d=64)
                K.op("dve", lambda e: e.tensor_tensor(
                    out=xdt[b][:], in0=xs3, in1=dtt[b][:, :, None].to_broadcast([128, 16, 64]), op=ALU.mult),
                    r=[R_xs[b], R_dt[b]], w=[R_xdt[b]])
                K.op("pool", lambda e: e.tensor_tensor(
                    out=xde[b][:], in0=xs3, in1=ex[b][:, 48:64, None].to_broadcast([128, 16, 64]), op=ALU.mult),
                    r=[R_xs[b], R_ex[b]], w=[R_xde[b]])

                K.op("pool", lambda e: e.tensor_tensor(
                    out=t2[b][:].rearrange("p (h d) -> p h d", d=64), in0=xs3,
                    in1=pp[:, PP_DSK:PP_DSK + 16, None].to_broadcast([128, 16, 64]), op=ALU.mult),
                    r=[R_xs[b], R_pp], w=[R_t2[b]])

            def S2a(c):
                b = c % 2
                tok = slice(c * 128, (c + 1) * 128)
                K.dma("sp", szt[b][:], sz_s[tok, :], w=[R_sz[b]], key=f"s_sz{b}")
                for g in range(2):
                    gs = slice(g * 512, (g + 1) * 512)
                    yv = P[3 + g][:].rearrange("p (h d) -> p h d", d=64)

                    def mmy(e, g=g, yv=yv):
                        for i in range(8):
                            ins = e.matmul(yv[:, i, :], lhsT=MT[b][:, 8 * g + i, :], rhs=xdt[b][:, 8 * g + i, :],
                                           start=True, stop=True)
                        return ins
                    K.op("pe", mmy, r=[R_MT[b], R_xdt[b]], w=[R_P[3 + g]])
                    K.op("pe", lambda e, g=g, gs=gs: e.matmul(P[5 + g][:], lhsT=CT[b][:, g, :],
                                                              rhs=hSb[b][:, gs], start=True, stop=True),
                         r=[R_CT[b], R_hSb[b][g]], w=[R_P[5 + g]])
                    K.op("pe", lambda e, g=g: e.matmul(
                        P[7][:], lhsT=Bt[b][:, g * 128:(g + 1) * 128],
                        rhs=xde[b][:, 8 * g:8 * g + 8, :].rearrange("p h d -> p (h d)"), start=True, stop=True),
                        r=[R_Bt[b], R_xde[b]], w=[R_P[7]])
                    K.op("pool", lambda e, g=g, gs=gs: e.tensor_tensor(
                        out=hS[:, gs].rearrange("p (h d) -> p h d", d=64), in0=hS[:, gs].rearrange("p (h d) -> p h d", d=64),
                        in1=ex[b][:, 32 + 8 * g:40 + 8 * g, None].to_broadcast([128, 8, 64]), op=ALU.mult),
                        r=[R_hS[g], R_ex[b]], w=[R_hS[g]])
                    K.op("dve", lambda e, gs=gs: e.tensor_tensor(out=hS[:, gs], in0=hS[:, gs], in1=P[7][:], op=ALU.add),
                         r=[R_hS[g], R_P[7]], w=[R_hS[g]])
                    K.op("act", lambda e, gs=gs: e.activation(out=hSb[1 - b][:, gs], in_=hS[:, gs], func=AF.Copy),
                         r=[R_hS[g]], w=[R_hSb[1 - b][g]])
                    K.op("dve", lambda e, g=g, gs=gs: e.tensor_tensor(
                        out=t1[b][:, gs].rearrange("p (h d) -> p h d", d=64),
                        in0=P[5 + g][:].rearrange("p (h d) -> p h d", d=64),
                        in1=ex[b][:, 16 + 8 * g:16 + 8 * g + 8, None].to_broadcast([128, 8, 64]), op=ALU.mult),
                        r=[R_P[5 + g], R_ex[b]], w=[R_t1[b]])
                    K.op("dve", lambda e, g=g, gs=gs: e.tensor_tensor(out=yd[b][:, gs], in0=t2[b][:, gs], in1=P[3 + g][:],
                                                                      op=ALU.add),
                         r=[R_t2[b], R_P[3 + g]], w=[R_yd[b]])

            def S2b(c):
                b = c % 2
                for g in range(2):
                    gs = slice(g * 512, (g + 1) * 512)
                    K.op("pool", lambda e, gs=gs: e.tensor_tensor(out=t1[b][:, gs], in0=t1[b][:, gs], in1=yd[b][:, gs],
                                                                  op=ALU.add),
                         r=[R_t1[b], R_yd[b]], w=[R_t1[b]])
                    K.op("pool", lambda e, gs=gs: e.tensor_tensor(out=yz[b][:, gs], in0=t1[b][:, gs], in1=szt[b][:, gs],
                                                                  op=ALU.mult),
                         r=[R_t1[b], R_sz[b]], w=[R_yz[b]])
                    K.op("act", lambda e, g=g, gs=gs: e.activation(out=junk[:], in_=yz[b][:, gs], func=AF.Square,
                                                                   accum_out=ssq[b][:, g:g + 1]),
                         r=[R_yz[b]], w=[R_junk, R_ssq[b]])
                K.op("act", lambda e: e.activation(out=ssq[b][:, 2:4], in_=ssq[b][:, 0:2], func=AF.Ln, scale=1.0 / 512,
                                                   bias=epsb[:, 0:1]), r=[R_ssq[b]], w=[R_ssq[b]])
                K.op("act", lambda e: e.activation(out=ssq[b][:, 4:6], in_=ssq[b][:, 2:4], func=AF.Exp, scale=-0.5),
                     r=[R_ssq[b]], w=[R_ssq[b]])
                for g in range(2):
                    gs = slice(g * 512, (g + 1) * 512)
                    K.op("dve", lambda e, g=g, gs=gs: e.scalar_tensor_tensor(
                        out=osm[b][:, gs], in0=yz[b][:, gs], scalar=ssq[b][:, 4 + g:5 + g],
                        in1=pp[:, PP_SOG + g * 512:PP_SOG + (g + 1) * 512], op0=ALU.mult, op1=ALU.mult),
                        r=[R_yz[b], R_ssq[b], R_pp], w=[R_osm[b]])
                tv = Pb[7].rearrange("p (i t) -> p i t", t=128)

                def tr(e):
                    for i in range(8):
                        ins = e.transpose(tv[:, i, :], osm[b][:, i * 128:(i + 1) * 128], ident_bf)
                    return ins
                K.op("pe", tr, r=[R_osm[b], R_cbf], w=[R_P[7]])
                K.op("act", lambda e: e.activation(out=osT[:, :, c * 128:(c + 1) * 128], in_=tv, func=AF.Copy),
                     r=[R_P[7]], w=[R_osT])

            S1(0)
            S1(1)
            S2a(0)
            for c in range(NT):
                if c + 2 < NT:
                    S1(c + 2)
                if c + 1 < NT:
                    S2a(c + 1)
                S2b(c)
        K.flush()

    def phase_outproj(L, q, xsrc, onT, R_onT, osT, R_osT):
        with ExitStack() as st:
            wbf = [sb(st, f"wb{i}", [128, 8192], BF16) for i in range(3)]
            R_wb = [K.res(f"wb{i}") for i in range(3)]
            xr = [sb(st, f"xr{i}", [128, 512], F32) for i in range(4)]
            R_xr = [K.res(f"xr{i}") for i in range(4)]
            xo = [sb(st, f"xo{i}", [128, 512], F32) for i in range(3)]
            R_xo = [K.res(f"xo{i}") for i in range(3)]
            w_l = w_out[L].rearrange("(c p) n -> p c n", p=128)
            units = [(db, tt) for db in range(4) for tt in range(NT)]

            def load(u):
                db, tt = units[u]
                K.dma("sp", xr[u % 4][:], xsrc[q, tt * 128:(tt + 1) * 128, db * 512:(db + 1) * 512],
                      w=[R_xr[u % 4]], key=f"xr{u % 4}")
            load(0)
            load(1)
            for u, (db, tt) in enumerate(units):
                slot = db % 3
                wv = wview(wbf[slot], 512)
                if tt == 0:
                    K.dma("pool", wv, w_l[:, :, db * 512:(db + 1) * 512], w=[R_wb[slot]], key=f"wb{slot}")
                if u + 2 < len(units):
                    load(u + 2)
                pa = u % 4

                def mm(e, tt=tt, pa=pa, wv=wv):
                    for c in range(16):
                        src = onT if c < 8 else osT
                        ins = e.matmul(P[pa][:], lhsT=src[:, c % 8, tt * 128:(tt + 1) * 128], rhs=wv[:, c, :],
                                       start=(c == 0), stop=(c == 15))
                    return ins
                K.op("pe", mm, r=[R_onT, R_osT, R_wb[slot]], w=[R_P[pa]])
                K.op("dve", lambda e, u=u, pa=pa: e.tensor_tensor(out=xo[u % 3][:], in0=xr[u % 4][:], in1=P[pa][:], op=ALU.add),
                     r=[R_xr[u % 4], R_P[pa]], w=[R_xo[u % 3]])
                K.dma("sp", y[q, tt * 128:(tt + 1) * 128, db * 512:(db + 1) * 512], xo[u % 3][:],
                      r=[R_xo[u % 3]], key=f"xo{u % 3}")
        K.flush()

    def phase_ffn(L, q):
        TBN = 1024
        for tbk in range(S // TBN):
            t0 = tbk * TBN
            with ExitStack() as st:
                hT = sb(st, "f_hT", [128, 16, TBN], BF16)
                R_hT = [K.res(f"f_hT{j}") for j in range(TBN // 128)]
                wbf = [sb(st, f"f_wb{i}", [128, 8192], BF16) for i in range(4)]
                R_wb = [K.res(f"f_wb{i}") for i in range(4)]
                wg_l = w_gate[L].rearrange("(c p) n -> p c n", p=128)
                wu_l = w_up[L].rearrange("(c p) n -> p c n", p=128)
                wd_l = w_down[L].rearrange("(c p) n -> p c n", p=128)

                def guload(fg):
                    sg_, su_ = fg % 2, 2 + fg % 2
                    K.dma("pool", wview(wbf[sg_], 512), wg_l[:, :, fg * 512:(fg + 1) * 512], w=[R_wb[sg_]], key=f"f_wb{sg_}")
                    K.dma("pool", wview(wbf[su_], 512), wu_l[:, :, fg * 512:(fg + 1) * 512], w=[R_wb[su_]], key=f"f_wb{su_}")
                guload(0)
                guload(1)
                with ExitStack() as st2:
                    emit_norm(st2, lambda j: y[q, t0 + j * 128: t0 + (j + 1) * 128, :], TBN // 128, hT, R_hT, gb_d[L, 1], [4, 5, 6, 7])
                    K.flush()
                actT = sb(st, "f_act", [128, NFC, TBN], BF16)
                R_act = [K.res(f"f_act{i}") for i in range(11)]
                sgt = [sb(st, f"f_sg{i}", [128, 512], F32) for i in range(2)]
                R_sg = [K.res(f"f_sg{i}") for i in range(2)]
                xr = [sb(st, f"f_xr{i}", [128, 256], F32) for i in range(4)]
                R_xr = [K.res(f"f_xr{i}") for i in range(4)]
                xo = [sb(st, f"f_xo{i}", [128, 256], F32) for i in range(3)]
                R_xo = [K.res(f"f_xo{i}") for i in range(3)]
                u = 0
                for fg in range(11):
                    sg_, su_ = fg % 2, 2 + fg % 2
                    wg = wview(wbf[sg_], 512)
                    wu = wview(wbf[su_], 512)
                    if fg >= 2:
                        guload(fg)
                    for j in range(4):
                        for ts in range(TBN // 512):
                            pg, pu = u % 3, 3 + u % 3
                            k2 = u % 2
                            u += 1

                            def mm(e, j=j, ts=ts, pg=pg, pu=pu, wg=wg, wu=wu):
                                for c in range(16):
                                    e.matmul(P[pg][:], lhsT=wg[:, c, j * 128:(j + 1) * 128],
                                             rhs=hT[:, c, ts * 512:(ts + 1) * 512], start=(c == 0), stop=(c == 15))
                                for c in range(16):
                                    ins = e.matmul(P[pu][:], lhsT=wu[:, c, j * 128:(j + 1) * 128],
                                                   rhs=hT[:, c, ts * 512:(ts + 1) * 512], start=(c == 0), stop=(c == 15))
                                return ins
                            K.op("pe", mm, r=[R_hT[4 * ts + i] for i in range(4)] + [R_wb[sg_], R_wb[su_]],
                                 w=[R_P[pg], R_P[pu]])
                            K.op("act", lambda e, pg=pg, k2=k2: e.activation(out=sgt[k2][:], in_=P[pg][:], func=AF.Silu),
                                 r=[R_P[pg]], w=[R_sg[k2]])
                            K.op("dve", lambda e, fg=fg, j=j, ts=ts, pu=pu, k2=k2: e.tensor_tensor(
                                out=actT[:, fg * 4 + j, ts * 512:(ts + 1) * 512], in0=sgt[k2][:], in1=P[pu][:], op=ALU.mult),
                                r=[R_sg[k2], R_P[pu]], w=[R_act[fg]])
                units = [(db, tt) for db in range(8) for tt in range(TBN // 128)]

                def load(i):
                    db, tt = units[i]
                    K.dma("sp", xr[i % 4][:], y[q, t0 + tt * 128: t0 + (tt + 1) * 128, db * 256:(db + 1) * 256],
                          w=[R_xr[i % 4]], key=f"f_xr{i % 4}")
                load(0)
                load(1)
                for i, (db, tt) in enumerate(units):
                    s0, s1 = (2 * db) % 4, (2 * db + 1) % 4
                    wd0 = wview(wbf[s0], 256)
                    wd1 = wview(wbf[s1], 256)
                    if tt == 0:
                        K.dma("pool", wd0, wd_l[:, 0:22, db * 256:(db + 1) * 256], w=[R_wb[s0]], key=f"f_wb{s0}")
                        K.dma("pool", wd1, wd_l[:, 22:44, db * 256:(db + 1) * 256], w=[R_wb[s1]], key=f"f_wb{s1}")
                    if i + 2 < len(units):
                        load(i + 2)
                    pa = i % 4

                    def mm(e, tt=tt, pa=pa, wd0=wd0, wd1=wd1):
                        for f in range(NFC):
                            wd = wd0 if f < 22 else wd1
                            ins = e.matmul(P[pa][:, 0:256], lhsT=actT[:, f, tt * 128:(tt + 1) * 128], rhs=wd[:, f % 22, :],
                                           start=(f == 0), stop=(f == NFC - 1))
                        return ins
                    K.op("pe", mm, r=R_act + [R_wb[s0], R_wb[s1]], w=[R_P[pa]])
                    K.op("dve", lambda e, i=i, pa=pa: e.tensor_tensor(out=xo[i % 3][:], in0=xr[i % 4][:], in1=P[pa][:, 0:256],
                                                                      op=ALU.add),
                         r=[R_xr[i % 4], R_P[pa]], w=[R_xo[i % 3]])
                    K.dma("sp", y[q, t0 + tt * 128: t0 + (tt + 1) * 128, db * 256:(db + 1) * 256], xo[i % 3][:],
                          r=[R_xo[i % 3]], key=f"f_xo{i % 3}")
            K.flush()

    for L in range(n_layers):
        K.dma("sp", pp[:], pp_d[L], w=[R_pp], key="pp")
        K.op("act", lambda e: e.activation(out=abc[:], in_=pp[:, PP_ALOG:PP_ALOG + 16], func=AF.Exp),
             r=[R_pp], w=[R_abc])
        K.op("dve", lambda e: e.tensor_scalar(out=abc[:], in0=abc[:], scalar1=-1.0, scalar2=None, op0=ALU.mult),
             r=[R_abc], w=[R_abc])
        K.flush()
        for q in range(n_seq):
            xsrc = x_in if L == 0 else y
            if "inproj" in phases:
                phase_inproj(L, q, xsrc)
            with ExitStack() as mst:
                onT = sb(mst, "onT", [128, NH, S], BF16)
                osT = sb(mst, "osT", [128, NH, S], BF16)
                R_onT, R_osT = K.res("onT"), K.res("osT")
                if "attn" in phases:
                    phase_attn(L, q, onT, R_onT)
                if "ssd" in phases:
                    phase_ssd(L, q, osT, R_osT)
                if debug:
                    K.dma("sp", dbg_on, onT[:], r=[R_onT], key="dbg_on")
                    K.dma("sp", dbg_os, osT[:], r=[R_osT], key="dbg_os")
                    K.flush()
                if "outproj" in phases:
                    phase_outproj(L, q, xsrc, onT, R_onT, osT, R_osT)
            if "ffn" in phases:
                phase_ffn(L, q)
    K.flush(final=True)
    es.close()
    return nc, K


def make_consts():
    p = np.arange(128)[:, None]
    f = np.arange(128)[None, :]
    c = np.zeros((128, NCST), np.float32)
    c[:, C_ID:C_ID + 128] = (p == f)
    c[:, C_ONE:C_ONE + 128] = 1.0
    c[:, C_US:C_US + 128] = (p > f)
    c[:, C_TRI:C_TRI + 128] = (p <= f)
    c[:, C_LGE:C_LGE + 128] = (p >= f)
    fa = np.arange(896)[None, :]
    c[:, C_AM:C_AM + 896] = (fa - p > 384)
    return c


def pack_params(inp, n_layers):
    pp = np.zeros((n_layers, 128, NP_), np.float32)
    for l in range(n_layers):
        pp[l, :, PP_NMIX:PP_NMIX + 16] = inp["norm_mix"][l].reshape(16, 128).T
        pp[l, :, PP_NFFN:PP_NFFN + 16] = inp["norm_ffn"][l].reshape(16, 128).T
        pp[l, :, PP_QG] = inp["q_gain"][l]
        pp[l, :, PP_KG] = inp["k_gain"][l]
        cw = inp["conv_w"][l].reshape(4, 12, 128)
        pp[l, :, PP_CW:PP_CW + 48] = cw.transpose(2, 1, 0).reshape(128, 48)
        pp[l, :, PP_CB:PP_CB + 12] = inp["conv_b"][l].reshape(12, 128).T
        pp[l, :, PP_DTB:PP_DTB + 16] = np.broadcast_to(inp["dt_bias"][l][None, :], (128, 16))
        pp[l, :, PP_ALOG:PP_ALOG + 16] = np.broadcast_to(inp["a_log"][l][None, :], (128, 16))
        pp[l, :, PP_DSK:PP_DSK + 16] = np.broadcast_to(inp["d_skip"][l][None, :], (128, 16))
        pp[l, :, PP_AOG:PP_AOG + 8] = inp["attn_out_gain"][l].reshape(8, 128).T
        pp[l, :, PP_SOG:PP_SOG + 1024] = np.broadcast_to(inp["ssm_out_gain"][l][None, :], (128, 1024))
    return pp


def pack_gb(inp, n_layers):
    gb = np.zeros((n_layers, 2, 128, D), np.float32)
    for l in range(n_layers):
        gb[l, 0] = np.broadcast_to(inp["norm_mix"][l][None, :], (128, D))
        gb[l, 1] = np.broadcast_to(inp["norm_ffn"][l][None, :], (128, D))
    return gb


_CACHE = {}
N_LAUNCH_LAYERS = 4


def kernel(**inputs):
    inp = {k: np.asarray(v) for k, v in inputs.items()}
    x = np.ascontiguousarray(inp["x"], dtype=np.float32)
    n_cores = 8
    n_seq = x.shape[0] // n_cores
    depth = inp["w_in"].shape[0]
    lpl = N_LAUNCH_LAYERS
    key = (lpl, n_seq)
    if key not in _CACHE:
        _CACHE[key] = build(lpl, n_seq)[0]
    nc = _CACHE[key]
    cst = make_consts()
    pp = pack_params(inp, depth)
    gb = pack_gb(inp, depth)
    cur = x
    for l0 in range(0, depth, lpl):
        sl = slice(l0, l0 + lpl)
        shared = {"pp": np.ascontiguousarray(pp[sl]), "gb": np.ascontiguousarray(gb[sl]), "cst": cst}
        for l in range(lpl):
            for nm in ("w_in", "w_out", "w_gate", "w_up", "w_down"):
                shared[f"{nm}{l}"] = np.ascontiguousarray(inp[nm][l0 + l], dtype=np.float32)
        in_maps = [dict(shared, x=np.ascontiguousarray(cur[c * n_seq:(c + 1) * n_seq])) for c in range(n_cores)]
        res = run_bass_kernel_spmd(nc, in_maps, core_ids=list(range(n_cores)))
        cur = np.concatenate([np.asarray(r["y"]) for r in res.results], axis=0).astype(np.float32)
    return cur
```
